# Optimizing a Trainium2 kernel written in Bass

```python
import jax, jax.numpy as jnp
from jax import lax
import numpy as np

D_MODEL = 1024
BATCH = 8
SEQ = 4096
DEPTH = 2

N_A_LAYERS = DEPTH // 2
N_B_LAYERS = DEPTH - N_A_LAYERS
M_HEADS = 4
M_DV = D_MODEL // M_HEADS
M_DK = M_DV // 2
M_CHUNK = 64
GATE_CAP = 15.0
M_QK_W = M_HEADS * M_DK
M_V_W = M_HEADS * M_DV
M_IN_COLS = 2 * M_QK_W + 2 * M_V_W + 2 * M_HEADS
M_SPLITS = [M_QK_W, 2 * M_QK_W, 2 * M_QK_W + M_V_W, 2 * M_QK_W + 2 * M_V_W, 2 * M_QK_W + 2 * M_V_W + M_HEADS]
A_HEAD_DIM = 64
A_Q_HEADS = D_MODEL // A_HEAD_DIM
A_KV_HEADS = 4
A_GROUP = A_Q_HEADS // A_KV_HEADS
WINDOW = 128
ROPE_DIM = A_HEAD_DIM // 4
ROPE_THETA = 500000.0
D_FF = 4 * D_MODEL
PLE_DIM = 256
LN_EPS = 1e-5
DEEPNORM_ALPHA = (2 * DEPTH) ** 0.25
DEEPNORM_BETA = (8 * DEPTH) ** -0.25

kernel_name = "yoco_mlstm_swa_sink_hybrid"


def layer_norm(x, g, b):
    xf = x.astype(jnp.float32)
    mu = jnp.mean(xf, axis=-1, keepdims=True)
    var = jnp.mean(jnp.square(xf - mu), axis=-1, keepdims=True)
    return ((xf - mu) * lax.rsqrt(var + LN_EPS) * g + b).astype(x.dtype)


def partial_rope(t, pos):
    half = ROPE_DIM // 2
    inv_freq = jnp.power(ROPE_THETA, -jnp.arange(half, dtype=jnp.float32) * (2.0 / ROPE_DIM))
    ang = pos.astype(jnp.float32)[..., None] * inv_freq
    cos = jnp.cos(ang)[:, :, None, :]
    sin = jnp.sin(ang)[:, :, None, :]
    tr = t[..., :ROPE_DIM].astype(jnp.float32)
    t1, t2 = tr[..., :half], tr[..., half:]
    rot = jnp.concatenate([t1 * cos - t2 * sin, t2 * cos + t1 * sin], axis=-1).astype(t.dtype)
    return jnp.concatenate([rot, t[..., ROPE_DIM:]], axis=-1)


def soft_cap(g):
    return GATE_CAP * jnp.tanh(g / GATE_CAP)


def to_chunks(t, nc):
    t = t.reshape(t.shape[0], nc, M_CHUNK, *t.shape[2:])
    return jnp.moveaxis(t, (1, 3), (0, 2))


def mlstm_chunkwise(q, k, v, ig, lf):
    B, S, H, DK = q.shape
    DV = v.shape[-1]
    nc = S // M_CHUNK
    xs = tuple(to_chunks(t, nc) for t in (q, k, v, ig, lf))
    causal = jnp.tril(jnp.ones((M_CHUNK, M_CHUNK), dtype=bool))
    init = (jnp.zeros((B, H, DK, DV), jnp.float32),
            jnp.zeros((B, H, DK), jnp.float32),
            jnp.zeros((B, H), jnp.float32))

    def step(carry, chunk):
        C, n, m = carry
        qc, kc, vc, igc, lfc = chunk
        b = jnp.cumsum(lfc, axis=-1)
        log_d = b[..., :, None] - b[..., None, :] + igc[..., None, :]
        log_d = jnp.where(causal, log_d, -jnp.inf)
        log_inter = b + m[..., None]
        m_t = jnp.maximum(jnp.max(log_d, axis=-1), log_inter)
        d = jnp.exp(log_d - m_t[..., None])
        inter = jnp.exp(log_inter - m_t)
        s = jnp.einsum('bhtk,bhsk->bhts', qc, kc) * d
        num = jnp.einsum('bhts,bhsv->bhtv', s, vc) + inter[..., None] * jnp.einsum('bhtk,bhkv->bhtv', qc, C)
        den = jnp.sum(s, axis=-1) + inter * jnp.einsum('bhtk,bhk->bht', qc, n)
        h = num / jnp.maximum(jnp.abs(den), jnp.exp(-m_t))[..., None]
        b_last = b[..., -1]
        a = b_last[..., None] - b + igc
        m_new = jnp.maximum(b_last + m, jnp.max(a, axis=-1))
        w = jnp.exp(a - m_new[..., None])
        decay = jnp.exp(b_last + m - m_new)
        C_new = decay[..., None, None] * C + jnp.einsum('bhs,bhsk,bhsv->bhkv', w, kc, vc)
        n_new = decay[..., None] * n + jnp.einsum('bhs,bhsk->bhk', w, kc)
        return (C_new, n_new, m_new), h

    _, h = lax.scan(step, init, xs)
    return jnp.moveaxis(h, (0, 2), (1, 3)).reshape(B, S, H, DV)


def mlstm_mixer(x, w_in, b_ig, b_fg, head_g, w_out):
    B, S, _ = x.shape
    proj = x @ w_in
    q, k, v, o, ig, fg = jnp.split(proj, M_SPLITS, axis=-1)
    q = q.reshape(B, S, M_HEADS, M_DK).astype(jnp.float32) * (M_DK ** -0.5)
    k = k.reshape(B, S, M_HEADS, M_DK).astype(jnp.float32)
    v = v.reshape(B, S, M_HEADS, M_DV).astype(jnp.float32)
    log_i = soft_cap(ig.astype(jnp.float32) + b_ig)
    log_f = jax.nn.log_sigmoid(soft_cap(fg.astype(jnp.float32) + b_fg))
    h = mlstm_chunkwise(q, k, v, log_i, log_f)
    mu = jnp.mean(h, axis=-1, keepdims=True)
    var = jnp.mean(jnp.square(h - mu), axis=-1, keepdims=True)
    h = (h - mu) * lax.rsqrt(var + LN_EPS) * head_g.reshape(M_HEADS, M_DV)
    h = h.reshape(B, S, D_MODEL).astype(x.dtype)
    return (jax.nn.sigmoid(o) * h) @ w_out


def shared_kv(x, pos, kv_w, kv_b):
    B, S, _ = x.shape
    kv = x @ kv_w + kv_b
    k, v = jnp.split(kv, 2, axis=-1)
    k = partial_rope(k.reshape(B, S, A_KV_HEADS, A_HEAD_DIM), pos)
    v = v.reshape(B, S, A_KV_HEADS, A_HEAD_DIM)
    return k, v


def banded(t, nb):
    t = t.reshape(t.shape[0], nb, WINDOW, A_KV_HEADS, A_HEAD_DIM)
    prev = jnp.pad(t, ((0, 0), (1, 0), (0, 0), (0, 0), (0, 0)))[:, :-1]
    return jnp.concatenate([prev, t], axis=2)


def swa_sink_attention(x, pos, k_sh, v_sh, w_q, b_q, sinks, w_o, b_o):
    B, S, _ = x.shape
    nb = S // WINDOW
    q = partial_rope((x @ w_q + b_q).reshape(B, S, A_Q_HEADS, A_HEAD_DIM), pos)
    q = q.reshape(B, nb, WINDOW, A_KV_HEADS, A_GROUP, A_HEAD_DIM)
    kb, vb = banded(k_sh, nb), banded(v_sh, nb)
    scores = jnp.einsum('bnqkgd,bnskd->bnkgqs', q, kb).astype(jnp.float32) * (A_HEAD_DIM ** -0.5)
    qi = jnp.arange(WINDOW)[:, None]
    si = jnp.arange(2 * WINDOW)[None, :]
    diff = qi + WINDOW - si
    band = (diff >= 0) & (diff < WINDOW)
    blk = jnp.arange(nb)[:, None, None]
    valid = band[None] & (blk * WINDOW - WINDOW + si[None] >= 0)
    scores = jnp.where(valid[None, :, None, None], scores, -jnp.inf)
    sink = sinks.astype(jnp.float32).reshape(A_KV_HEADS, A_GROUP)[None, None, :, :, None]
    m = jnp.maximum(jnp.max(scores, axis=-1), sink)
    e = jnp.exp(scores - m[..., None])
    probs = e / (jnp.sum(e, axis=-1) + jnp.exp(sink - m))[..., None]
    out = jnp.einsum('bnkgqs,bnskd->bnqkgd', probs.astype(x.dtype), vb).reshape(B, S, D_MODEL)
    return out @ w_o + b_o


def squared_relu_mlp(x, w_up, w_down):
    return jnp.square(jax.nn.relu(x @ w_up)) @ w_down


def setup_inputs(seed: int = 0) -> dict:
    key = jax.random.key(seed)
    ks = jax.random.split(key, 24)
    f32 = jnp.float32
    nrm = lambda k, shape, scale: jax.random.normal(k, shape, f32) * scale
    x = jax.random.normal(ks[0], (BATCH, SEQ, D_MODEL), f32)
    p = jax.random.normal(ks[1], (DEPTH, BATCH, SEQ, PLE_DIM), f32)
    start = jax.random.randint(ks[2], (BATCH, 1), 0, 1024, dtype=jnp.int32)
    positions = (start + jnp.arange(SEQ, dtype=jnp.int32)[None, :]).astype(jnp.int32)
    a_w_in = nrm(ks[3], (N_A_LAYERS, D_MODEL, M_IN_COLS), D_MODEL ** -0.5)
    a_b_igate = nrm(ks[4], (N_A_LAYERS, M_HEADS), 0.1)
    a_b_fgate = jnp.linspace(3.0, 6.0, M_HEADS, dtype=f32)[None, :] + nrm(ks[5], (N_A_LAYERS, M_HEADS), 0.1)
    a_head_norm_g = 1.0 + nrm(ks[6], (N_A_LAYERS, D_MODEL), 0.02)
    a_w_out = nrm(ks[7], (N_A_LAYERS, D_MODEL, D_MODEL), D_MODEL ** -0.5 * DEEPNORM_BETA)
    kv_w = nrm(ks[8], (D_MODEL, 2 * A_KV_HEADS * A_HEAD_DIM), D_MODEL ** -0.5)
    kv_b = nrm(ks[9], (2 * A_KV_HEADS * A_HEAD_DIM,), 0.02)
    b_w_q = nrm(ks[10], (N_B_LAYERS, D_MODEL, A_Q_HEADS * A_HEAD_DIM), D_MODEL ** -0.5)
    b_b_q = nrm(ks[11], (N_B_LAYERS, A_Q_HEADS * A_HEAD_DIM), 0.02)
    b_sinks = nrm(ks[12], (N_B_LAYERS, A_Q_HEADS), 0.5)
    b_w_o = nrm(ks[13], (N_B_LAYERS, A_Q_HEADS * A_HEAD_DIM, D_MODEL), D_MODEL ** -0.5 * DEEPNORM_BETA)
    b_b_o = nrm(ks[14], (N_B_LAYERS, D_MODEL), 0.02)
    mix_ln_g = 1.0 + nrm(ks[15], (DEPTH, D_MODEL), 0.02)
    mix_ln_b = nrm(ks[16], (DEPTH, D_MODEL), 0.02)
    mlp_w_up = nrm(ks[17], (DEPTH, D_MODEL, D_FF), D_MODEL ** -0.5)
    mlp_w_down = nrm(ks[18], (DEPTH, D_FF, D_MODEL), D_FF ** -0.5 * DEEPNORM_BETA)
    mlp_ln_g = 1.0 + nrm(ks[19], (DEPTH, D_MODEL), 0.02)
    mlp_ln_b = nrm(ks[20], (DEPTH, D_MODEL), 0.02)
    ple_w_gate = nrm(ks[21], (DEPTH, D_MODEL, D_MODEL), D_MODEL ** -0.5)
    ple_b_gate = nrm(ks[22], (DEPTH, D_MODEL), 0.02)
    ple_w_proj = nrm(ks[23], (DEPTH, PLE_DIM, D_MODEL), PLE_DIM ** -0.5 * DEEPNORM_BETA)
    return {"x": x, "p": p, "positions": positions,
            "a_w_in": a_w_in, "a_b_igate": a_b_igate, "a_b_fgate": a_b_fgate,
            "a_head_norm_g": a_head_norm_g, "a_w_out": a_w_out,
            "kv_w": kv_w, "kv_b": kv_b,
            "b_w_q": b_w_q, "b_b_q": b_b_q, "b_sinks": b_sinks, "b_w_o": b_w_o, "b_b_o": b_b_o,
            "mix_ln_g": mix_ln_g, "mix_ln_b": mix_ln_b,
            "mlp_w_up": mlp_w_up, "mlp_w_down": mlp_w_down, "mlp_ln_g": mlp_ln_g, "mlp_ln_b": mlp_ln_b,
            "ple_w_gate": ple_w_gate, "ple_b_gate": ple_b_gate, "ple_w_proj": ple_w_proj}


def reference(x, p, positions, a_w_in, a_b_igate, a_b_fgate, a_head_norm_g, a_w_out,
              kv_w, kv_b, b_w_q, b_b_q, b_sinks, b_w_o, b_b_o,
              mix_ln_g, mix_ln_b, mlp_w_up, mlp_w_down, mlp_ln_g, mlp_ln_b,
              ple_w_gate, ple_b_gate, ple_w_proj):
    k_sh, v_sh = None, None
    for i in range(DEPTH):
        if i < N_A_LAYERS:
            mix = mlstm_mixer(x, a_w_in[i], a_b_igate[i], a_b_fgate[i], a_head_norm_g[i], a_w_out[i])
        else:
            if i == N_A_LAYERS:
                k_sh, v_sh = shared_kv(x, positions, kv_w, kv_b)
            j = i - N_A_LAYERS
            mix = swa_sink_attention(x, positions, k_sh, v_sh, b_w_q[j], b_b_q[j], b_sinks[j], b_w_o[j], b_b_o[j])
        x = layer_norm(DEEPNORM_ALPHA * x + mix, mix_ln_g[i], mix_ln_b[i])
        x = layer_norm(DEEPNORM_ALPHA * x + squared_relu_mlp(x, mlp_w_up[i], mlp_w_down[i]), mlp_ln_g[i], mlp_ln_b[i])
        x = x + jax.nn.sigmoid(x @ ple_w_gate[i] + ple_b_gate[i]) * (p[i] @ ple_w_proj[i])
    return x
```

```python
import math
from contextlib import ExitStack

import numpy as np
import concourse.bass as bass
import concourse.mybir as mybir
from concourse.bass_utils import run_bass_kernel_spmd

F32 = mybir.dt.float32
BF16 = mybir.dt.bfloat16
I32 = mybir.dt.int32
ALU = mybir.AluOpType
AF = mybir.ActivationFunctionType
AX = mybir.AxisListType

D = 1024
TT = 1024
NBLK = TT // 128
NHALF = TT // 512
NW = 4
SLAB = 4096
ALPHA = float((2 * 2) ** 0.25)
LN_EPS = 1e-5
NS = 51
PI = math.pi

_c = {}
_off = 0


def _add(name, n):
    global _off
    _c[name] = (_off, n)
    _off += n


for _l in range(2):
    for _n in ("mixg", "mixb", "mlpg", "mlpb", "pleb"):
        _add(f"{_n}{_l}", 8)
_add("bo", 8)
_add("hg", 8)
_add("bq", 8)
_add("bk", 2)
_add("invf", 1)
_add("sgn", 1)
_add("gb", 64)
_add("vb", 256)
_add("sink", 16)
_add("ustrict", 128)
_add("ones", 128)
_add("ident", 128)
_add("maskm", 128)
_add("maskb", 256)
_add("maskbf", 256)
_add("pm", 128)
_add("wg", 64)
NCST = _off


class Buf:
    __slots__ = ("name", "lw", "rd", "const", "excl")

    def __init__(self, name, const=False, excl=False):
        self.name = name
        self.lw = None
        self.rd = []
        self.const = const
        self.excl = excl


class Op:
    __slots__ = ("id", "eng", "fns", "cost", "deps", "kind", "dsem", "nbytes", "tag", "ms", "xlat")

    def __init__(self, id, eng, fns, cost, deps, kind, dsem, nbytes, tag):
        self.id = id
        self.eng = eng
        self.fns = fns
        self.cost = cost
        self.deps = deps
        self.kind = kind
        self.dsem = dsem
        self.nbytes = nbytes
        self.tag = tag
        self.ms = None
        self.xlat = 0.0


class Sched:
    WINDOW = 128
    LAT_X = 140.0
    LAT_S = 50.0
    DMA_BW = 170.0
    DMA_LAT = 2000.0
    TOL = 150.0

    def __init__(self, nc, es):
        self.nc = nc
        self.es = es
        self.sems = {}
        self.engs = ("pe", "act", "dve", "pool", "sp")
        for n in self.engs:
            self.sems[n] = es.enter_context(nc.semaphore("s_" + n))
        self.ops = []
        self.pending = None
        self.phase = "init"
        self.final_waits = []

    def dsem(self, name):
        self.sems[name] = self.es.enter_context(self.nc.semaphore("d_" + name))
        return name

    def _collect(self, reads, writes):
        deps = set()
        for b in reads:
            if b.lw is not None:
                deps.add(b.lw)
            if b.excl:
                deps.update(b.rd)
        for b in writes:
            if b.lw is not None:
                deps.add(b.lw)
            deps.update(b.rd)
        return deps

    def _commit(self, oid, reads, writes):
        for b in writes:
            b.lw = oid
            b.rd = []
        for b in reads:
            if b.excl:
                b.lw = oid
                b.rd = []
            elif not b.const:
                b.rd.append(oid)

    def emit(self, eng, fn, reads=(), writes=(), inc=True, cost=100.0):
        if eng == "pe":
            if self.pending is None:
                self.pending = Op(len(self.ops), "pe", [], 0.0, set(), "c", None, 0, self.phase)
                self.ops.append(self.pending)
            op = self.pending
            op.fns.append(fn)
            op.cost += cost
            d = self._collect(reads, writes)
            d.discard(op.id)
            op.deps |= d
            self._commit(op.id, reads, writes)
            if inc:
                self.pending = None
            return
        assert self.pending is None or True
        oid = len(self.ops)
        op = Op(oid, eng, [fn], cost, self._collect(reads, writes), "c", None, 0, self.phase)
        self.ops.append(op)
        self._commit(oid, reads, writes)

    def dma(self, q, ds, out, in_, reads=(), writes=(), nbytes=0, xlat=0.0):
        oid = len(self.ops)
        op = Op(oid, q, [lambda e: e.dma_start(out=out, in_=in_)], 60.0 if q == "sp" else 900.0,
                self._collect(reads, writes), "d", ds, nbytes, self.phase)
        op.xlat = xlat
        self.ops.append(op)
        self._commit(oid, reads, writes)

    def schedule(self):
        assert self.pending is None, "open PE accumulation group"
        ops = self.ops
        n = len(ops)
        elist = {e: [] for e in self.engs}
        for op in ops:
            elist[op.eng].append(op.id)
        pos = {e: 0 for e in self.engs}
        done = [False] * n
        fin = [0.0] * n
        free = {e: 0.0 for e in self.engs}
        dma_free = 0.0
        order = {e: [] for e in self.engs}
        left = n
        W = self.WINDOW
        while left:
            best = None
            for e in self.engs:
                lst = elist[e]
                p = pos[e]
                while p < len(lst) and done[lst[p]]:
                    p += 1
                pos[e] = p
                cnt = 0
                i = p
                fe = free[e]
                cand = None
                while i < len(lst) and cnt < W:
                    oid = lst[i]
                    i += 1
                    if done[oid]:
                        continue
                    cnt += 1
                    op = ops[oid]
                    st = fe
                    ok = True
                    for d in op.deps:
                        if not done[d]:
                            ok = False
                            break
                        de = ops[d].eng
                        if ops[d].kind == "d":
                            t = fin[d] + self.LAT_X
                        elif de == e:
                            t = fin[d] + (0.0 if e == "pe" else self.LAT_S)
                        else:
                            t = fin[d] + self.LAT_X
                        if t > st:
                            st = t
                    if not ok:
                        continue
                    if cand is None:
                        cand = (st, oid)
                        older_min = st
                    elif st < cand[0] and st + op.cost <= older_min + self.TOL:
                        cand = (st, oid)
                    if st < older_min:
                        older_min = st
                    if cand[0] <= fe:
                        break
                if cand is not None and (best is None or cand < best):
                    best = cand
            assert best is not None, "scheduler deadlock"
            st, oid = best
            op = ops[oid]
            e = op.eng
            done[oid] = True
            left -= 1
            order[e].append(oid)
            if op.kind == "d":
                free[e] = st + op.cost
                ds_ = max(free[e], dma_free)
                dma_free = ds_ + op.nbytes / self.DMA_BW
                fin[oid] = dma_free + self.DMA_LAT + op.xlat
            else:
                free[e] = st + op.cost
                fin[oid] = free[e]
        self.order = order
        self.fin = fin
        self.est_ns = max(fin) if fin else 0.0
        cnt = {k: 0 for k in self.sems}
        for e in self.engs:
            for oid in order[e]:
                op = ops[oid]
                if op.kind == "d":
                    cnt[op.dsem] += 16
                    op.ms = (op.dsem, cnt[op.dsem])
                else:
                    cnt[e] += 1
                    op.ms = (e, cnt[e])
        self.final_counts = cnt

    def run(self, eng, e):
        ops = self.ops
        seen = {}
        for oid in self.order[eng]:
            op = ops[oid]
            need = {}
            for d in op.deps:
                sn, idx = ops[d].ms
                if sn == "pe" and eng == "pe":
                    continue
                if need.get(sn, 0) < idx:
                    need[sn] = idx
            for sn, idx in need.items():
                if seen.get(sn, 0) >= idx:
                    continue
                seen[sn] = idx
                e.wait_ge(self.sems[sn], idx)
            last = None
            for fn in op.fns:
                last = fn(e)
            last.then_inc(self.sems[op.ms[0]], 16 if op.kind == "d" else 1)
        if eng == "sp":
            for sn in self.final_waits:
                if self.final_counts[sn] > 0:
                    e.wait_ge(self.sems[sn], self.final_counts[sn])


def build(ntiles=4, nlayers=2, dbg=()):
    nc = bass.Bass("TRN2", target_bir_lowering=False)
    SEQL = ntiles * TT
    xT_d = nc.dram_tensor("xT", [D, SEQL], F32, kind="ExternalInput").ap()
    pT_d = nc.dram_tensor("pT", [2, 256, SEQL], F32, kind="ExternalInput").ap()
    pos_d = nc.dram_tensor("pos", [1, SEQL], I32, kind="ExternalInput").ap()
    ws_d = nc.dram_tensor("wslab", [NS, 128, SLAB], F32, kind="ExternalInput").ap()
    cst_d = nc.dram_tensor("cst", [128, NCST], F32, kind="ExternalInput").ap()
    out_d = nc.dram_tensor("outT", [D, SEQL], F32, kind="ExternalOutput").ap()

    es = ExitStack()
    with es:
        S = Sched(nc, es)

        def sb(name, shape, dt):
            return es.enter_context(nc.sbuf_tensor(name, shape, dt))

        xT32 = sb("xT32", [128, 8, TT], F32)
        xb = sb("xb", [128, 8, TT], BF16)
        yT = xb
        arena = sb("arena", [128, 32768], BF16)
        ring = [sb(f"ring{i}", [128, SLAB], BF16) for i in range(NW)]
        cst = sb("cst_sb", [128, NCST], F32)
        zst = [sb(f"zst{i}", [128, 2, 512], BF16) for i in range(4)]
        lnmean = sb("lnmean", [128, TT], F32)
        lnrstd = sb("lnrstd", [128, TT], F32)
        lntmp = [sb(f"lntmp{i}", [128, 512], F32) for i in range(2)]
        ptb = sb("ptb_sb", [128, 2, TT], BF16)
        identb = sb("identb", [128, 128], BF16)
        onesb = sb("onesb", [128, 128], BF16)
        maskmb = sb("maskmb", [128, 128], BF16)
        maskbb = [sb(f"maskbb{i}", [128, 2, 256], BF16) for i in range(2)]
        sinkmax = sb("sinkmax", [128, 4], F32)
        pmb = sb("pmb", [128, 128], BF16)
        wgb = sb("wgb", [128, 8, 8], BF16)
        bq8 = sb("bq8", [128, 8], F32)
        epsc = sb("epsc", [128, 1], F32)
        Cs = sb("Cs", [128, 4, 257], F32)
        Cb = [sb(f"Cb{i}", [128, 257], BF16) for i in range(2)]
        PTm = [sb(f"PTm{i}", [128, 128], BF16) for i in range(2)]
        gz = sb("gz", [128, 8, 8], F32)
        gth = sb("gth", [128, 8, 8], F32)
        gef = sb("gef", [128, 8, 4], F32)
        gsp = sb("gsp", [128, 8, 4], F32)
        gei = sb("gei", [128, 8, 4], F32)
        ges = sb("ges", [128, 8, 4], F32)
        gfr = sb("gfr", [128, 2, 8, 4], F32)
        nsm = [sb(f"nsm{i}", [128, 8, 4], F32) for i in range(2)]
        nst = [sb(f"nst{i}", [128, 4, 6], F32) for i in range(2)]
        nmv = [sb(f"nmv{i}", [128, 4, 2], F32) for i in range(2)]
        hn = [sb(f"hn{i}", [128, 4, 256], BF16) for i in range(2)]
        ytok = [sb(f"ytok{i}", [128, 1024], BF16) for i in range(2)]
        kTa = sb("kTa", [128, 2, 128 + TT], BF16)
        va = sb("va", [128, NBLK + 1, 4, 65], BF16)
        posi = sb("posi", [128, 1, TT], I32)
        asm = [sb(f"asm{i}", [128, 6, 4], F32) for i in range(3)]
        ps_all = es.enter_context(nc.psum_tensor("ps", [128, 8, 512], F32))
        nc.sbuf_left = nc.sbuf_bytes_remaining

        def av(off, n):
            return arena[:, off:off + n]

        hT = arena[:, :].rearrange("p (c t) -> p c t", c=32)
        xo = arena[:, 0:16384].bitcast(F32).rearrange("p (c t) -> p c t", c=8)
        m_qT = av(0, 4096).rearrange("p (h t) -> p h t", h=4)
        m_kT = av(4096, 4096).rearrange("p (h t) -> p h t", h=4)
        m_kw = av(8192, 4096).rearrange("p (b n) -> p b n", b=8)
        m_va = av(12288, 8 * 4 * 258).rearrange("p (b h n) -> p b h n", b=8, h=4)
        m_sgo = av(20608, 8192).rearrange("p (b n) -> p b n", b=8)
        a_qT = av(0, 8192).rearrange("p (c t) -> p c t", c=8)
        a_cos = av(8192, 2048).bitcast(F32)
        a_sin = av(10240, 2048).bitcast(F32)
        a_posf = av(12288, 2048).bitcast(F32)
        a_ang = av(14336, 2048).bitcast(F32)
        a_u = av(27648, 2048).bitcast(F32)
        a_ki = av(29696, 2048).bitcast(I32)
        a_q32 = [av(16384 + i * 1024, 1024).bitcast(F32) for i in range(2)]
        a_qb = [av(18432 + i * 512, 512) for i in range(2)]
        a_ra = [av(19456 + i * 1024, 1024).bitcast(F32) for i in range(2)]
        a_rb = [av(21504 + i * 1024, 1024).bitcast(F32) for i in range(2)]
        Pexp = [av(23552 + i * 1024, 1024).rearrange("p (g s) -> p g s", g=4) for i in range(2)]
        PTa = [av(25600 + i * 1024, 1024).rearrange("p (c t) -> p c t", c=8) for i in range(2)]
        Pexp.append(av(12288, 1024).rearrange("p (g s) -> p g s", g=4))
        PTa.append(av(27648, 1024).rearrange("p (c t) -> p c t", c=8))
        PEXP_B = ["Pexp0", "Pexp1", "aposf"]
        PTA_B = ["PTa0", "PTa1", "au"]

        def cc(name, lo=0, n=None):
            o, nn = _c[name]
            if n is None:
                n = nn - lo
            return cst[:, o + lo:o + lo + n]

        B = {}

        CONST_BUFS = {"cst", "identb", "onesb", "maskmb", "maskbb", "sinkmax", "pmb", "wgb", "bq8", "epsc"}

        def bf(name):
            if name not in B:
                B[name] = Buf(name, const=name in CONST_BUFS)
            return B[name]

        AT = Buf("arena_tok")
        dummy = sb("dummy_sb", [128, 2], F32)

        def arena_gen():
            S.emit("dve", lambda e: e.memset(dummy[:, 0:1], 0.0), [], [AT], cost=60.0)

        def fsz(ap):
            n = 1
            for d in ap.shape[1:]:
                n *= d
            return n

        def ar(aps, reads):
            for a in aps:
                if getattr(a, "name", None) == "arena":
                    return list(reads) + [AT]
            return reads

        psb = [Buf(f"ps{i}", excl=True) for i in range(8)]
        ringb = [Buf(f"ring{i}") for i in range(NW)]
        ringsem = [S.dsem(f"ring{i}") for i in range(NW)]
        cst_sem = S.dsem("cst")
        x_sems = [S.dsem(f"xin{i}") for i in range(16)]
        p_sem = S.dsem("pin")
        pos_sem = S.dsem("pos")
        out_sems = [S.dsem(f"out{i}") for i in range(8)]

        def ps(i):
            return ps_all[:, i, :]

        dbg_map = {}
        if dbg:
            dbg_d = nc.dram_tensor("dbg", [128, 16384], F32, kind="ExternalOutput").ap()
            dbg_sem = S.dsem("dbg")

        def dump(tag, ap2d, bufs):
            if tag not in dbg or tag in dbg_map:
                return
            shp = list(ap2d.shape[1:])
            n = int(np.prod(shp))
            o = sum(v[1] for v in dbg_map.values())
            dbg_map[tag] = (o, n)
            dst = dbg_d[:, o:o + n]
            if len(shp) == 2:
                dst = dst.rearrange("p (a b) -> p a b", a=shp[0])
            elif len(shp) == 3:
                dst = dst.rearrange("p (a b c) -> p a b c", a=shp[0], b=shp[1])
            S.dma("pool", dbg_sem, dst, ap2d, reads=ar((ap2d,), bufs), nbytes=128 * n * 4)

        def mm(out, lhsT, rhs, start, stop, reads, writes, inc):
            n = fsz(rhs)
            c = (max(n, 48) * 0.5 + 14.0) * (4.0 if rhs.dtype == F32 else 1.0)
            S.emit("pe", lambda e: e.matmul(out, lhsT, rhs, start=start, stop=stop), ar((lhsT, rhs), reads), writes, inc, cost=c)

        def tr(out, in_, reads, writes, inc):
            S.emit("pe", lambda e: e.transpose(out, in_, identb[:, :]), ar((in_,), reads + [bf("identb")]), writes, inc, cost=80.0)

        def act(out, in_, func, reads, writes, bias=None, scale=None):
            kw = {}
            if bias is not None:
                kw["bias"] = bias
            if scale is not None:
                kw["scale"] = scale
            S.emit("act", lambda e: e.activation(out, in_, func, **kw), ar((out, in_), reads), writes, cost=210.0 + 0.9 * fsz(out))

        def tt(eng, out, in0, in1, op, reads, writes):
            S.emit(eng, lambda e: e.tensor_tensor(out, in0, in1, op), ar((out, in0, in1), reads), writes,
                   cost=(70.0 + 1.05 * fsz(out)) * (2.0 if eng == "pool" else 1.0))

        def ts(eng, out, in0, s1, s2, op0, op1, reads, writes):
            c = 70.0 + 1.05 * fsz(out)
            if op1 is None:
                S.emit(eng, lambda e: e.tensor_scalar(out, in0, s1, None, op0), ar((out, in0), reads), writes, cost=c)
            else:
                S.emit(eng, lambda e: e.tensor_scalar(out, in0, s1, s2, op0, op1), ar((out, in0), reads), writes, cost=c)

        def stt(eng, out, in0, sc, in1, op0, op1, reads, writes):
            S.emit(eng, lambda e: e.scalar_tensor_tensor(out, in0, sc, in1, op0, op1), ar((out, in0, in1), reads), writes,
                   cost=70.0 + 1.05 * fsz(out))

        def cp(eng, out, in_, reads, writes):
            S.emit(eng, lambda e: e.tensor_copy(out, in_), ar((out, in_), reads), writes,
                   cost=(70.0 + 1.05 * fsz(out)) * (1.6 if eng == "pool" else 1.0))

        order0 = [1, 0, 2, 3, 4, 5, 6, 7] + list(range(8, 24)) + [26, 24, 25]
        order1 = [27, 28, 29, 30, 31] + list(range(32, 48)) + [50, 48, 49]
        per_tile = order0 + (order1 if nlayers > 1 else [])
        wseq = per_tile * ntiles
        wstate = {"issued": 0, "used": 0}
        wlive = set()
        wheld = set()

        def w_pump():
            while wstate["issued"] < len(wseq):
                i = wstate["issued"]
                prev = i - NW
                if prev >= 0 and (prev >= wstate["used"] or prev in wlive):
                    break
                slot = i % NW
                S.dma("pool", ringsem[slot], ring[slot][:, :], ws_d[wseq[i]], reads=(), writes=[ringb[slot]], nbytes=128 * SLAB * 4)
                wstate["issued"] += 1

        def w_next(expect, hold=False):
            i = wstate["used"]
            assert wseq[i] == expect, (wseq[i], expect)
            for j in list(wlive):
                if j not in wheld:
                    wlive.discard(j)
            wstate["used"] += 1
            wlive.add(i)
            if hold:
                wheld.add(i)
            w_pump()
            assert wstate["issued"] > i
            slot = i % NW
            return ring[slot], ringb[slot]

        def w_release_held():
            for j in list(wheld):
                wheld.discard(j)
                wlive.discard(j)

        S.dma("sp", cst_sem, cst[:, :], cst_d[:, :], writes=[bf("cst")], nbytes=128 * NCST * 4)
        w_pump()
        cb = bf("cst")
        cp("dve", identb[:, :], cc("ident"), [cb], [bf("identb")])
        cp("dve", maskmb[:, :], cc("maskm"), [cb], [bf("maskmb")])
        for i_, nm_ in enumerate(("maskb", "maskbf")):
            cp("dve", maskbb[i_][:, :, :], cc(nm_).unsqueeze(1).broadcast_to([128, 2, 256]), [cb], [bf("maskbb")])
        S.emit("dve", lambda e: e.tensor_reduce(sinkmax[:, :], cc("sink").rearrange("p (k g) -> p k g", k=4), AX.X, ALU.max),
               [cb], [bf("sinkmax")])
        cp("dve", pmb[:, :], cc("pm"), [cb], [bf("pmb")])
        cp("dve", wgb[:, :, :], cc("wg").rearrange("p (k g) -> p k g", k=8), [cb], [bf("wgb")])
        S.emit("dve", lambda e: e.memset(onesb[:, :], 1.0 / 1024.0), [], [bf("onesb")])
        ts("dve", bq8[:, :], cc("bq"), 0.125, None, ALU.mult, None, [cb], [bf("bq8")])
        S.emit("dve", lambda e: e.memset(Cs[:, :, :], 0.0), [], [bf("Cs")])
        S.emit("dve", lambda e: e.memset(epsc[:, :], LN_EPS), [], [bf("epsc")])
        S.emit("dve", lambda e: e.memset(kTa[:, :, 0:128], 0.0), [], [bf("kTa0")])
        S.emit("dve", lambda e: e.memset(va[:, :, :, :], 0.0), [], [bf("va_all")])
        S.emit("dve", lambda e: e.memset(va[:, :, :, 64:65], 1.0), [bf("va_all")], [bf("va_all")])

        psrot = {"i": 0}

        def next_ps(lst):
            i = lst[psrot["i"] % len(lst)]
            psrot["i"] += 1
            return i

        ALLB = list(range(8))

        def ln_phase(l, which, groups, order):
            gcol = f"{which}g{l}"
            bcol = f"{which}b{l}"
            emit_group, bias_name = groups
            rot = [0, 1, 2, 3]
            pm_i = [4, 5]
            pq_i = [6, 7]
            pend = []
            cnt_h = [0, 0]

            def normalize(half):
                hs = slice(half * 512, (half + 1) * 512)
                rb = bf(f"lnrstd{half}")
                pmb_ = psb[pm_i[half]]
                act(lnrstd[:, hs], ps(pm_i[half]), AF.Square, [pmb_], [rb])
                tt("dve", lnrstd[:, hs], ps(pq_i[half]), lnrstd[:, hs], ALU.subtract, [psb[pq_i[half]], rb], [rb])
                act(lnrstd[:, hs], lnrstd[:, hs], AF.Sqrt, [rb, bf("epsc")], [rb], bias=epsc[:, 0:1])
                S.emit("dve", lambda e, o=lnrstd[:, hs]: e.reciprocal(o, o), [rb], [rb], cost=610.0)
                for oc in range(8):
                    xs = xT32[:, oc, hs]
                    xbuf = bf(f"x{oc}_{half}")
                    xbb = bf(f"xb{oc}_{half}")
                    tt("dve", xs, xs, ps(pm_i[half]), ALU.subtract, [xbuf, pmb_], [xbuf])
                    tt("dve", xs, xs, lnrstd[:, hs], ALU.mult, [xbuf, rb], [xbuf])
                    act(xb[:, oc, hs], xs, AF.Identity, [xbuf, cb], [xbb], bias=cc(bcol, oc, 1), scale=cc(gcol, oc, 1))
                for oc in range(8):
                    xs = xT32[:, oc, hs]
                    xbuf = bf(f"x{oc}_{half}")
                    act(xs, xs, AF.Identity, [xbuf, cb], [xbuf], bias=cc(bcol, oc, 1), scale=cc(gcol, oc, 1))

            def stats(half, slot):
                first = cnt_h[half] == 0
                last = cnt_h[half] == 7
                cnt_h[half] += 1
                mm(ps(pm_i[half]), onesb[:, :], zst[slot][:, 0, :], first, last,
                   [bf("onesb"), bf(f"zsta{slot}")], [psb[pm_i[half]]], False)
                mm(ps(pq_i[half]), onesb[:, :], zst[slot][:, 1, :], first, last,
                   [bf("onesb"), bf(f"zst{slot}")], [psb[pq_i[half]]], True)
                if last:
                    normalize(half)

            k = 0
            for oc, half in order:
                pi = next_ps(rot)
                emit_group(oc, half, pi)
                xs = xT32[:, oc, half * 512:(half + 1) * 512]
                xbuf = bf(f"x{oc}_{half}")
                if bias_name == "pre":
                    tt("dve", xs, xs, ps(pi), ALU.add, [xbuf, psb[pi]], [xbuf])
                elif bias_name is not None:
                    tmp = lntmp[k % 2]
                    tb = bf(f"lntmp{k % 2}")
                    act(tmp[:, :], ps(pi), AF.Identity, [psb[pi], cb], [tb], bias=cc(bias_name, oc, 1))
                    stt("dve", xs, xs, ALPHA, tmp[:, :], ALU.mult, ALU.add, [xbuf, tb], [xbuf])
                else:
                    stt("dve", xs, xs, ALPHA, ps(pi), ALU.mult, ALU.add, [xbuf, psb[pi]], [xbuf])
                slot = k % 4
                zb_ = bf(f"zst{slot}")
                cp("pool", zst[slot][:, 0, :], xs, [xbuf], [bf(f"zsta{slot}")])
                act(zst[slot][:, 1, :], xs, AF.Square, [xbuf], [zb_])
                pend.append((half, slot))
                if len(pend) > 1:
                    stats(*pend.pop(0))
                k += 1
            while pend:
                stats(*pend.pop(0))

        ORDER_HALF_MAJOR = [(oc, h) for h in range(NHALF) for oc in range(8)]
        ORDER_PAIRS = [(2 * p_ + i_, h) for p_ in range(4) for h in range(NHALF) for i_ in range(2)]

        def xb_reads(half=None):
            if half is None:
                return [bf(f"xb{oc}_{h}") for oc in range(8) for h in range(NHALF)]
            return [bf(f"xb{oc}_{half}") for oc in range(8)]

        def mlp_ple(l, it, base, last=False):
            t0 = it * TT
            S.dma("pool", p_sem, ptb[:, :, :], pT_d[l, :, t0:t0 + TT].rearrange("(k p) t -> p k t", p=128),
                  writes=[bf("ptb")], nbytes=256 * TT * 4)
            S.phase = f"l{l}.up"
            arena_gen()
            for sp_ in range(4):
                pair = []
                for i_ in range(2):
                    slot, sbuf_ = w_next(base + 8 + 2 * sp_ + i_, hold=True)
                    pair.append((2 * sp_ + i_, slot[:, :].rearrange("p (k n) -> p k n", k=8), sbuf_))
                for half in range(NHALF):
                    for s, sv, sbuf_ in pair:
                        for j in range(4):
                            hc = s * 4 + j
                            pi = next_ps(ALLB)
                            for kc in range(8):
                                mm(ps(pi), sv[:, kc, j * 128:(j + 1) * 128], xb[:, kc, half * 512:(half + 1) * 512],
                                   kc == 0, kc == 7, [sbuf_] + xb_reads(half), [psb[pi]], kc == 7)
                            hb = bf(f"hT{hc}_{half}")
                            ho = hT[:, hc, half * 512:(half + 1) * 512]
                            lt = lntmp[(hc * 2 + half) % 2]
                            ltb = bf(f"lntmp{(hc * 2 + half) % 2}")
                            act(lt[:, :], ps(pi), AF.Relu, [psb[pi]], [ltb])
                            tt("dve", ho, lt[:, :], lt[:, :], ALU.mult, [ltb], [hb])
                w_release_held()

            S.phase = f"l{l}.down_ln"
            dsl = {}

            def down_group(oc, half, pi):
                if oc not in dsl:
                    slot, sbuf_ = w_next(base + 16 + oc, hold=True)
                    dsl[oc] = (slot[:, :].rearrange("p (k n) -> p k n", k=32), sbuf_)
                sv, sbuf_ = dsl[oc]
                for hc in range(32):
                    mm(ps(pi), sv[:, hc, :], hT[:, hc, half * 512:(half + 1) * 512], hc == 0, hc == 31,
                       [sbuf_, bf(f"hT{hc}_{half}")], [psb[pi]], hc % 8 == 7)
                if oc % 2 == 1 and half == NHALF - 1:
                    w_release_held()

            ln_phase(l, "mlp", (down_group, None), ORDER_PAIRS)
            S.phase = f"l{l}.ple"
            if last:
                arena_gen()
            slot_p, sbuf_p = w_next(base + 26, hold=True)
            spv = slot_p[:, 0:2048].rearrange("p (k n) -> p k n", k=2)
            rotg = [0, 1, 2, 3]
            rotp = [4, 5, 6, 7]
            gsl = [w_next(base + 24 + i_, hold=True) for i_ in range(2)]
            k = 0
            for half in range(NHALF):
                hs = slice(half * 512, (half + 1) * 512)
                for oc in range(8):
                    gslot, gbuf = gsl[oc // 4]
                    gv = gslot[:, :].rearrange("p (k n) -> p k n", k=8)
                    j = oc % 4
                    pg = rotg[k % 4]
                    pp = rotp[k % 4]
                    for kc in range(8):
                        mm(ps(pg), gv[:, kc, j * 128:(j + 1) * 128], xb[:, kc, hs], kc == 0, kc == 7,
                           [gbuf] + xb_reads(half), [psb[pg]], kc == 7)
                    for kc in range(2):
                        mm(ps(pp), spv[:, kc, oc * 128:(oc + 1) * 128], ptb[:, kc, hs], kc == 0, kc == 1,
                           [sbuf_p, bf("ptb")], [psb[pp]], kc == 1)
                    lt = lntmp[k % 2]
                    ltb = bf(f"lntmp{k % 2}")
                    act(lt[:, :], ps(pg), AF.Sigmoid, [psb[pg], cb], [ltb], bias=cc(f"pleb{l}", oc, 1))
                    tt("dve", lt[:, :], lt[:, :], ps(pp), ALU.mult, [ltb, psb[pp]], [ltb])
                    xs = xT32[:, oc, hs]
                    xbuf = bf(f"x{oc}_{half}")
                    if last:
                        tt("pool", xo[:, oc, hs], xs, lt[:, :], ALU.add, [xbuf, ltb], [bf(f"xo{oc}")])
                    else:
                        tt("pool", xs, xs, lt[:, :], ALU.add, [xbuf, ltb], [xbuf])
                    k += 1
            w_release_held()
            if last:
                return
            for oc in range(8):
                for half in range(NHALF):
                    hs = slice(half * 512, (half + 1) * 512)
                    act(xb[:, oc, hs], xT32[:, oc, hs], AF.Copy, [bf(f"x{oc}_{half}")], [bf(f"xb{oc}_{half}")])

        def layer0(it):
            t0 = it * TT
            S.phase = "l0.load"
            arena_gen()
            for half in range(NHALF):
                for oc in range(8):
                    S.dma("sp", x_sems[oc * 2 + half], xT32[:, oc, half * 512:(half + 1) * 512],
                          xT_d[oc * 128:(oc + 1) * 128, t0 + half * 512:t0 + (half + 1) * 512], nbytes=128 * 512 * 4, xlat=15000.0,
                          writes=[bf(f"x{oc}_{half}")])
            for half in range(NHALF):
                for oc in range(8):
                    hs = slice(half * 512, (half + 1) * 512)
                    act(xb[:, oc, hs], xT32[:, oc, hs], AF.Copy, [bf(f"x{oc}_{half}")], [bf(f"xb{oc}_{half}")])
            S.emit("dve", lambda e: e.memset(m_va[:, :, :, 256:257], 1.0), [AT], [bf("mva_ones")], cost=100.0)
            S.phase = "l0.gates"
            for hf in range(NHALF):
                b4 = slice(hf * 4, hf * 4 + 4)
                pg = next_ps(ALLB)
                for bb in range(4):
                    blk = hf * 4 + bb
                    for kc in range(8):
                        mm(ps(pg)[:, bb * 8:(bb + 1) * 8], xb[:, kc, blk * 128:(blk + 1) * 128], wgb[:, kc, :],
                           kc == 0, kc == 7, [bf("wgb")] + xb_reads(hf), [psb[pg]], kc == 7 and bb == 3)
                gzb, gthb = bf(f"gz{hf}"), bf(f"gth{hf}")
                tt("dve", gz[:, b4, :], ps(pg)[:, 0:32].rearrange("p (b g) -> p b g", g=8),
                   cc("gb", 0, 32).rearrange("p (b g) -> p b g", g=8), ALU.add, [psb[pg], cb], [gzb])
                act(gth[:, b4, :], gz[:, b4, :], AF.Tanh, [gzb], [gthb], scale=1.0 / 15.0)
                act(gef[:, b4, :], gth[:, b4, 4:8], AF.Exp, [gthb], [bf(f"gef{hf}")], scale=-15.0)
                act(gsp[:, b4, :], gef[:, b4, :], AF.Ln, [bf(f"gef{hf}")], [bf(f"gsp{hf}")], bias=1.0)
                pa = next_ps(ALLB)
                gspf = gsp[:, b4, :].rearrange("p b g -> p (b g)")
                mm(ps(pa)[:, 0:16], cc("ustrict"), gspf, True, True, [cb, bf(f"gsp{hf}")], [psb[pa]], False)
                mm(ps(pa)[:, 16:32], cc("ones"), gspf, True, True, [cb, bf(f"gsp{hf}")], [psb[pa]], True)
                stt("dve", gei[:, b4, :], ps(pa)[:, 0:16].rearrange("p (b g) -> p b g", g=4), -1.0 / 15.0, gth[:, b4, 0:4],
                    ALU.mult, ALU.add, [gthb, psb[pa]], [bf(f"gei{hf}")])
                act(ges[:, b4, :], gei[:, b4, :], AF.Exp, [bf(f"gei{hf}")], [bf(f"ges{hf}")], scale=15.0)
                act(gfr[:, :, b4, :], ps(pa)[:, 0:32].rearrange("p (a b g) -> p a b g", a=2, g=4), AF.Exp,
                    [psb[pa]], [bf(f"gfr{hf}")], scale=-1.0)
            S.phase = "l0.proj"
            slot, sbuf_ = w_next(1)
            sv = slot[:, :].rearrange("p (k n) -> p k n", k=8)
            for h in range(4):
                for half in range(NHALF):
                    hs = slice(half * 512, (half + 1) * 512)
                    pi = next_ps(ALLB)
                    for kc in range(8):
                        mm(ps(pi), sv[:, kc, h * 128:(h + 1) * 128], xb[:, kc, hs], kc == 0, kc == 7,
                           [sbuf_] + xb_reads(half), [psb[pi]], kc == 7)
                    act(m_kT[:, h, hs], ps(pi), AF.Copy, [psb[pi]], [bf(f"mkT{h}_{half}")])
            for blk in range(NBLK):
                bs = slice(blk * 128, (blk + 1) * 128)
                pi = next_ps(ALLB)
                for kc in range(8):
                    mm(ps(pi), xb[:, kc, bs], sv[:, kc, :], kc == 0, kc == 7,
                       [sbuf_] + xb_reads(blk // 4), [psb[pi]], kc == 7)
                tt("dve", m_kw[:, blk, :].rearrange("p (h n) -> p h n", h=4),
                   ps(pi).rearrange("p (h n) -> p h n", h=4),
                   ges[:, blk, :].unsqueeze(2).broadcast_to([128, 4, 128]), ALU.mult,
                   [psb[pi], bf(f"ges{blk // 4}")], [bf(f"mkw{blk}")])
            slot, sbuf_ = w_next(0)
            sv = slot[:, :].rearrange("p (k n) -> p k n", k=8)
            for h in range(4):
                for half in range(NHALF):
                    hs = slice(half * 512, (half + 1) * 512)
                    pi = next_ps(ALLB)
                    for kc in range(8):
                        mm(ps(pi), sv[:, kc, h * 128:(h + 1) * 128], xb[:, kc, hs], kc == 0, kc == 7,
                           [sbuf_] + xb_reads(half), [psb[pi]], kc == 7)
                    act(m_qT[:, h, hs], ps(pi), AF.Copy, [psb[pi]], [bf(f"mqT{h}_{half}")], scale=128.0 ** -0.5)
            for s in range(2):
                slot, sbuf_ = w_next(2 + s)
                sv = slot[:, :].rearrange("p (k n) -> p k n", k=8)
                for blk in range(NBLK):
                    bs = slice(blk * 128, (blk + 1) * 128)
                    pi = next_ps(ALLB)
                    for kc in range(8):
                        mm(ps(pi), xb[:, kc, bs], sv[:, kc, :], kc == 0, kc == 7,
                           [sbuf_] + xb_reads(blk // 4), [psb[pi]], kc == 7)
                    S.emit("dve" if blk % 2 else "act",
                           (lambda e, o=m_va[:, blk, 2 * s:2 * s + 2, 0:256], i=ps(pi).rearrange("p (h n) -> p h n", h=2):
                            e.tensor_copy(o, i)) if blk % 2 else
                           (lambda e, o=m_va[:, blk, 2 * s:2 * s + 2, 0:256], i=ps(pi).rearrange("p (h n) -> p h n", h=2):
                            e.activation(o, i, AF.Copy)),
                           [psb[pi], AT], [bf(f"mva{blk}_{s}")], cost=680.0)
            for s in range(2):
                slot, sbuf_ = w_next(4 + s)
                sv = slot[:, :].rearrange("p (k n) -> p k n", k=8)
                for blk in range(NBLK):
                    bs = slice(blk * 128, (blk + 1) * 128)
                    pi = next_ps(ALLB)
                    for kc in range(8):
                        mm(ps(pi), xb[:, kc, bs], sv[:, kc, :], kc == 0, kc == 7,
                           [sbuf_] + xb_reads(blk // 4), [psb[pi]], kc == 7)
                    act(m_sgo[:, blk, s * 512:(s + 1) * 512], ps(pi), AF.Sigmoid, [psb[pi]], [bf(f"msgo{blk}_{s}")])
            S.phase = "l0.recur"
            rot1 = [5, 6]
            for blk in range(NBLK):
                bs = slice(blk * 128, (blk + 1) * 128)
                half = blk // 4
                q = blk % 2
                pn = [ps_all[:, 2 * q + h // 2, (h % 2) * 256:(h % 2) * 256 + 256] for h in range(4)]
                pnb = [psb[2 * q + h // 2] for h in range(4)]
                pd = ps_all[:, 4, q * 4:q * 4 + 4]
                for h in range(4):
                    k = blk * 4 + h
                    pS = next_ps(rot1)
                    mm(ps(pS)[:, 0:128], m_kT[:, h, bs], m_qT[:, h, bs], True, True,
                       [bf(f"mkT{h}_{half}"), bf(f"mqT{h}_{half}")], [psb[pS]], False)
                    mm(ps(pS)[:, 128:385], m_kw[:, blk, h * 128:(h + 1) * 128], m_va[:, blk, h, 0:257], True, True,
                       [bf(f"mkw{blk}"), bf(f"mva{blk}_{h // 2}"), bf("mva_ones")], [psb[pS]], True)
                    ptm = PTm[k % 2]
                    ptb_ = bf(f"PTm{k % 2}")
                    stt("dve", ptm[:, :], ps(pS)[:, 0:128], ges[:, blk, h:h + 1], maskmb[:, :], ALU.mult, ALU.mult,
                        [psb[pS], bf(f"ges{half}"), bf("maskmb")], [ptb_])
                    cbt = Cb[k % 2]
                    cbb = bf(f"Cb{k % 2}")
                    act(cbt[:, :], Cs[:, h, :], AF.Copy, [bf(f"Cs{h}"), bf("Cs"), bf(f"gfr{half}")], [cbb],
                        scale=gfr[:, 1, blk, h:h + 1])
                    vb_ = [bf(f"mva{blk}_{h // 2}"), bf("mva_ones")]
                    mm(pn[h], ptm[:, :], m_va[:, blk, h, 0:256], True, False, [ptb_] + vb_, [pnb[h]], False)
                    mm(pn[h], m_qT[:, h, bs], cbt[:, 0:256], False, True, [bf(f"mqT{h}_{half}"), cbb], [pnb[h]], False)
                    mm(pd[:, h:h + 1], ptm[:, :], m_va[:, blk, h, 256:257], True, False, [ptb_] + vb_, [psb[4]], False)
                    mm(pd[:, h:h + 1], m_qT[:, h, bs], cbt[:, 256:257], False, True, [bf(f"mqT{h}_{half}"), cbb], [psb[4]], True)
                    stt("dve", Cs[:, h, :], Cs[:, h, :], gfr[:, 1, blk, h:h + 1], ps(pS)[:, 128:385], ALU.mult, ALU.add,
                        [bf(f"Cs{h}"), bf("Cs"), bf(f"gfr{half}"), psb[pS]], [bf(f"Cs{h}")])
                sm = nsm[q]
                smb = bf(f"nsm{q}")
                act(sm[:, 0, :], pd, AF.Abs, [psb[4]], [smb])
                tt("dve", sm[:, 0, :], sm[:, 0, :], gfr[:, 0, blk, :], ALU.max, [smb, bf(f"gfr{half}")], [smb])
                S.emit("dve", lambda e, o=sm[:, 1, :], i=sm[:, 0, :]: e.reciprocal(o, i), [smb], [smb])
                for h in range(4):
                    S.emit("dve", lambda e, o=nst[q][:, h, :], i=pn[h]: e.bn_stats(o, i), [pnb[h]], [bf(f"nst{q}")], cost=340.0)
                for h in range(4):
                    S.emit("dve", lambda e, o=nmv[q][:, h, :], i=nst[q][:, h, :]: e.bn_aggr(o, i), [bf(f"nst{q}")], [bf(f"nmv{q}")])
                tt("dve", sm[:, 2, :], sm[:, 1, :], sm[:, 1, :], ALU.mult, [smb], [smb])
                tt("dve", sm[:, 2, :].unsqueeze(2), sm[:, 2, :].unsqueeze(2), nmv[q][:, :, 1:2], ALU.mult,
                   [smb, bf(f"nmv{q}")], [smb])
                act(sm[:, 2, :], sm[:, 2, :], AF.Sqrt, [smb, bf("epsc")], [smb], bias=epsc[:, 0:1])
                S.emit("dve", lambda e, o=sm[:, 2, :]: e.reciprocal(o, o), [smb], [smb])
                tt("dve", sm[:, 3, :], sm[:, 2, :], sm[:, 1, :], ALU.mult, [smb], [smb])
                stt("dve", sm[:, 4, :].unsqueeze(2), nmv[q][:, :, 0:1], -1.0, sm[:, 3, :].unsqueeze(2), ALU.mult, ALU.mult,
                    [smb, bf(f"nmv{q}")], [smb])
                hnb = bf(f"hn{q}")
                for h in range(4):
                    act(hn[q][:, h, :], pn[h], AF.Identity, [pnb[h], smb], [hnb],
                        bias=sm[:, 4, h:h + 1], scale=sm[:, 3, h:h + 1])
                yb = bf(f"ytok{q}")
                tt("pool", ytok[q][:, :], hn[q][:, :, :].rearrange("p h n -> p (h n)"), m_sgo[:, blk, :], ALU.mult,
                   [hnb, bf(f"msgo{blk}_0"), bf(f"msgo{blk}_1")], [yb])
                pT_ = 7
                ptv = ps(pT_).bitcast(BF16).rearrange("p (c t) -> p c t", c=8)
                for c in range(8):
                    tr(ptv[:, c, :], ytok[q][:, c * 128:(c + 1) * 128], [yb], [psb[pT_]], c == 7)
                act(yT[:, :, bs], ptv, AF.Copy, [psb[pT_]], xb_reads(blk // 4))
            S.phase = "l0.outproj_ln"
            wst = {}

            def out_group(oc, half, pi):
                si = oc // 4
                if si not in wst:
                    slot, sbuf_ = w_next(6 + si, hold=True)
                    sv = slot[:, :].rearrange("p (k n) -> p k n", k=8)
                    tt("pool", sv, sv, cc("hg").unsqueeze(2).broadcast_to([128, 8, 512]), ALU.mult, [sbuf_, cb], [sbuf_])
                    wst[si] = (sv, sbuf_)
                sv, sbuf_ = wst[si]
                j = oc % 4
                for kc in range(8):
                    mm(ps(pi), sv[:, kc, j * 128:(j + 1) * 128], yT[:, kc, half * 512:(half + 1) * 512], kc == 0, kc == 7,
                       [sbuf_] + xb_reads(half), [psb[pi]], kc == 7)

            ln_phase(0, "mix", (out_group, None), ORDER_HALF_MAJOR)
            w_release_held()
            mlp_ple(0, it, 0, last=(nlayers == 1))

        def rope_evac(pi, dst, dstbuf, biasap, sc, half, k):
            hs = slice(half * 512, (half + 1) * 512)
            q32 = a_q32[k % 2]
            qb_ = a_qb[k % 2]
            ra = a_ra[k % 2]
            rb = a_rb[k % 2]
            b32, bqb, bra, brb = bf(f"aq32{k % 2}"), bf(f"aqb{k % 2}"), bf(f"ara{k % 2}"), bf(f"arb{k % 2}")
            act(q32, ps(pi), AF.Identity, [psb[pi], cb, bf("bq8")], [b32], bias=biasap, scale=sc)
            act(qb_, ps(pi), AF.Identity, [psb[pi], cb, bf("bq8")], [bqb], bias=biasap, scale=sc)
            psw = next_ps([4, 5, 6, 7])
            mm(ps(psw), pmb[:, :], qb_, True, True, [bf("pmb"), bqb], [psb[psw]], True)
            tt("pool", ra, q32, a_cos[:, hs], ALU.mult, [b32, bf("acos")], [bra])
            tt("dve", rb, ps(psw), a_sin[:, hs], ALU.mult, [psb[psw], bf("asin")], [brb])
            tt("dve", dst, ra, rb, ALU.add, [bra, brb], [dstbuf])

        def layer1(it):
            t0 = it * TT
            gb0 = it * NBLK
            S.phase = "l1.rope"
            arena_gen()
            for half in range(NHALF):
                for oc in range(8):
                    hs = slice(half * 512, (half + 1) * 512)
                    act(xT32[:, oc, hs], xT32[:, oc, hs], AF.Identity, [bf(f"x{oc}_{half}"), cb], [bf(f"x{oc}_{half}")],
                        bias=cc("bo", oc, 1), scale=ALPHA)
            S.dma("sp", pos_sem, posi[:, :, :], pos_d[0:1, t0:t0 + TT].partition_broadcast(128), writes=[bf("posi")], nbytes=128 * TT * 4)
            cp("dve", a_posf, posi[:, 0, :], [bf("posi")], [bf("aposf")])
            C1 = 6.28125
            C2 = 2.0 * PI - C1
            ab, ub = bf("aang"), bf("au")
            ts("dve", a_ang, a_posf, cc("invf"), None, ALU.mult, None, [bf("aposf"), cb], [ab])
            ts("dve", a_u, a_ang, 1.0 / (2.0 * PI), None, ALU.mult, None, [ab], [ub])
            cp("dve", a_ki, a_u, [ub], [bf("aki")])
            cp("dve", a_u, a_ki, [bf("aki")], [ub])
            stt("dve", a_ang, a_u, -C1, a_ang, ALU.mult, ALU.add, [ub, ab], [ab])
            stt("dve", a_ang, a_u, -C2, a_ang, ALU.mult, ALU.add, [ub, ab], [ab])
            ts("dve", a_u, a_ang, PI, 2.0 * PI, ALU.is_gt, ALU.mult, [ab], [ub])
            tt("dve", a_ang, a_ang, a_u, ALU.subtract, [ab, ub], [ab])
            act(a_sin, a_ang, AF.Sin, [ab], [bf("asin")])
            ts("dve", a_ang, a_ang, 0.5 * PI, None, ALU.add, None, [ab], [ab])
            ts("dve", a_u, a_ang, PI, 2.0 * PI, ALU.is_gt, ALU.mult, [ab], [ub])
            tt("dve", a_ang, a_ang, a_u, ALU.subtract, [ab, ub], [ab])
            act(a_cos, a_ang, AF.Sin, [ab], [bf("acos")])
            ts("dve", a_sin, a_sin, cc("sgn"), None, ALU.mult, None, [bf("asin"), cb], [bf("asin")])
            S.phase = "l1.proj"
            slot, sbuf_ = w_next(27)
            sv = slot[:, :].rearrange("p (k n) -> p k n", k=8)
            kk = 0
            for c in range(2):
                for half in range(NHALF):
                    hs = slice(half * 512, (half + 1) * 512)
                    pi = next_ps([0, 1, 2, 3])
                    for kc in range(8):
                        mm(ps(pi), sv[:, kc, c * 128:(c + 1) * 128], xb[:, kc, hs], kc == 0, kc == 7,
                           [sbuf_] + xb_reads(half), [psb[pi]], kc == 7)
                    rope_evac(pi, kTa[:, c, 128 + half * 512:128 + (half + 1) * 512], bf(f"kTa{c}_{half}"),
                              cc("bk", c, 1), 1.0, half, kk)
                    kk += 1
            for blk in range(NBLK):
                bs = slice(blk * 128, (blk + 1) * 128)
                pi = next_ps([0, 1, 2, 3])
                for kc in range(8):
                    mm(ps(pi)[:, 0:256], xb[:, kc, bs], sv[:, kc, 256:512], kc == 0, kc == 7,
                       [sbuf_] + xb_reads(blk // 4), [psb[pi]], kc == 7)
                tt("dve", va[:, blk + 1, :, 0:64], ps(pi)[:, 0:256].rearrange("p (h d) -> p h d", h=4),
                   cc("vb").rearrange("p (h d) -> p h d", h=4), ALU.add, [psb[pi], cb, bf("va_all")], [bf(f"va{blk + 1}")])
            for s in range(2):
                slot, sbuf_ = w_next(28 + s)
                sv = slot[:, :].rearrange("p (k n) -> p k n", k=8)
                for jj in range(4):
                    j = s * 4 + jj
                    for half in range(NHALF):
                        hs = slice(half * 512, (half + 1) * 512)
                        pi = next_ps([0, 1, 2, 3])
                        for kc in range(8):
                            mm(ps(pi), sv[:, kc, jj * 128:(jj + 1) * 128], xb[:, kc, hs], kc == 0, kc == 7,
                               [sbuf_] + xb_reads(half), [psb[pi]], kc == 7)
                        rope_evac(pi, a_qT[:, j, hs], bf(f"aqT{j}_{half}"), bq8[:, j:j + 1], 0.125, half, kk)
                        kk += 1
            dump("cos", a_cos, [bf("acos")])
            dump("sin", a_sin, [bf("asin")])
            dump("qT", a_qT[:, :, 0:256], [bf(f"aqT{j}_0") for j in range(8)])
            dump("kT", kTa[:, :, 0:384], [bf(f"kTa{c}_0") for c in range(2)] + [bf("kTa0")])
            dump("va", va[:, 0:3, :, :], [bf("va1"), bf("va2"), bf("va_all")])
            dump("xin", xT32[:, :, 0:128], [bf(f"x{oc}_0") for oc in range(8)])
            S.phase = "l1.attn"
            rot1 = [6, 7]
            for blk in range(NBLK):
                bs = slice(blk * 128, (blk + 1) * 128)
                half = blk // 4
                first = (gb0 + blk == 0)
                q2 = blk % 2
                atok = ytok[q2]
                atb = bf(f"ytok{q2}")
                for kv in range(4):
                    k = blk * 4 + kv
                    off = (kv % 2) * 64
                    kch = kv // 2
                    sbank = (k % 3) * 2
                    pSv = ps_all[:, sbank:sbank + 2, :].rearrange("p a (g s) -> p (a g) s", g=2)
                    kreads = [bf(f"kTa{kch}_{h_}") for h_ in range(NHALF)] + [bf("kTa0"), bf("kTaprev")]
                    mbias = maskbb[1 if first else 0]
                    for pb in range(2):
                        mm(ps(sbank + pb), identb[:, :], mbias[:, :, :].rearrange("p a s -> p (a s)"), True, False,
                           [bf("identb"), bf("maskbb")], [psb[sbank + pb]], False)
                    for g in range(4):
                        j = kch * 4 + g
                        mm(pSv[:, g, :], a_qT[off:off + 64, j, bs], kTa[off:off + 64, kch, blk * 128:blk * 128 + 256],
                           False, g % 2 == 1, [bf(f"aqT{j}_{half}")] + kreads, [psb[sbank + g // 2]], g == 3)
                    sm = asm[k % 3]
                    smb = bf(f"asm{k % 3}")
                    sb2 = [psb[sbank], psb[sbank + 1]]
                    S.emit("dve", lambda e, o=sm[:, 0, 0:1], i=ps_all[:, sbank:sbank + 2, :]: e.tensor_reduce(o, i, AX.XY, ALU.max),
                           sb2, [smb], cost=1150.0)
                    ts("dve", sm[:, 1, 0:1], sm[:, 0, 0:1], sinkmax[:, kv:kv + 1], -1.0, ALU.max, ALU.mult, [smb, bf("sinkmax")], [smb])
                    pe_ = Pexp[k % 3]
                    peb = bf(PEXP_B[k % 3])
                    act(pe_, pSv, AF.Exp, sb2 + [smb], [peb], bias=sm[:, 1, 0:1])
                    pT_ = next_ps(rot1)
                    ptv = ps(pT_).bitcast(BF16).rearrange("p (c t) -> p c t", c=8)
                    for g in range(4):
                        for jj in range(2):
                            tr(ptv[:, g * 2 + jj, :], pe_[:, g, jj * 128:(jj + 1) * 128], [peb], [psb[pT_]],
                               g == 3 and jj == 1)
                    pta = PTa[k % 3]
                    ptab = bf(PTA_B[k % 3])
                    if k % 4 == 3:
                        cp("dve", pta, ptv, [psb[pT_]], [ptab])
                    else:
                        act(pta, ptv, AF.Copy, [psb[pT_]], [ptab])
                    pO = next_ps(rot1)
                    pOv = ps(pO).rearrange("p (g n) -> p g n", g=4)
                    for g in range(4):
                        if not first:
                            mm(pOv[:, g, 0:65], pta[:, g * 2, :], va[:, blk, kv, :], True, False,
                               [ptab, bf(f"va{blk}"), bf("va_all")], [psb[pO]], False)
                        mm(pOv[:, g, 0:65], pta[:, g * 2 + 1, :], va[:, blk + 1, kv, :], first, True,
                           [ptab, bf(f"va{blk + 1}"), bf("va_all")], [psb[pO]], g == 3)
                    act(sm[:, 3, :], cc("sink", kv * 4, 4), AF.Exp, [smb, cb], [smb], bias=sm[:, 1, 0:1])
                    tt("dve", sm[:, 4, :].unsqueeze(2), pOv[:, :, 64:65], sm[:, 3, :].unsqueeze(2), ALU.add,
                       [psb[pO], smb], [smb])
                    S.emit("dve", lambda e, o=sm[:, 5, :], i=sm[:, 4, :]: e.reciprocal(o, i), [smb], [smb])
                    tt("dve", atok[:, kv * 256:(kv + 1) * 256].rearrange("p (g d) -> p g d", g=4), pOv[:, :, 0:64],
                       sm[:, 5, :].unsqueeze(2).broadcast_to([128, 4, 64]), ALU.mult, [psb[pO], smb], [atb])
                dump(f"atok{blk}", atok[:, :], [atb])
                pT_ = next_ps(rot1)
                ptv = ps(pT_).bitcast(BF16).rearrange("p (c t) -> p c t", c=8)
                for c in range(8):
                    tr(ptv[:, c, :], atok[:, c * 128:(c + 1) * 128], [atb], [psb[pT_]], c == 7)
                act(yT[:, :, bs], ptv, AF.Copy, [psb[pT_]], xb_reads(blk // 4))
            cp("dve", kTa[:, :, 0:128], kTa[:, :, TT:TT + 128],
               [bf(f"kTa{c}_{NHALF - 1}") for c in range(2)] + [bf("kTa0")], [bf("kTaprev"), bf("kTa0")])
            cp("dve", va[:, 0, :, :], va[:, NBLK, :, :], [bf(f"va{NBLK}"), bf("va_all")], [bf("va0")])
            S.phase = "l1.oproj_ln"
            wst = {}

            def o_group(oc, half, pi):
                si = oc // 4
                if si not in wst:
                    slot, sbuf_ = w_next(30 + si, hold=True)
                    wst[si] = (slot[:, :].rearrange("p (k n) -> p k n", k=8), sbuf_)
                sv, sbuf_ = wst[si]
                j = oc % 4
                for kc in range(8):
                    mm(ps(pi), sv[:, kc, j * 128:(j + 1) * 128], yT[:, kc, half * 512:(half + 1) * 512], kc == 0, kc == 7,
                       [sbuf_] + xb_reads(half), [psb[pi]], kc == 7)

            ln_phase(1, "mix", (o_group, "pre"), ORDER_HALF_MAJOR)
            w_release_held()
            mlp_ple(1, it, 24, last=True)

        for it in range(ntiles):
            layer0(it)
            if nlayers > 1:
                layer1(it)
            t0 = it * TT
            for oc in range(8):
                S.dma("sp", out_sems[oc], out_d[oc * 128:(oc + 1) * 128, t0:t0 + TT], xo[:, oc, :], nbytes=128 * TT * 4,
                      reads=[bf(f"xo{oc}"), AT], writes=[bf(f"out{it}_{oc}")])
        S.final_waits = list(out_sems) + ([dbg_sem] if dbg else [])
        S.schedule()
        nc.sched = S
        nc.dbg_map = dbg_map
        with nc.Block() as block:
            @block.tensor
            def _(e):
                S.run("pe", e)

            @block.scalar
            def _(e):
                S.run("act", e)

            @block.vector
            def _(e):
                S.run("dve", e)

            @block.gpsimd
            def _(e):
                S.run("pool", e)

            @block.sync
            def _(e):
                S.run("sp", e)
    return nc


def _slab_std(W):
    K, N = W.shape
    a = W.reshape(K // 128, 128, N // 512, 512).transpose(2, 1, 0, 3)
    a = a.reshape(N // 512, 128, (K // 128) * 512)
    if a.shape[2] < SLAB:
        a = np.concatenate([a, np.zeros((a.shape[0], 128, SLAB - a.shape[2]), np.float32)], axis=2)
    return a


def _slab_down(W):
    return W.reshape(32, 128, 8, 128).transpose(2, 1, 0, 3).reshape(8, 128, SLAB)


def _slab_proj(W):
    a = W.reshape(2, 128, 1024).transpose(1, 0, 2).reshape(1, 128, 2048)
    return np.concatenate([a, np.zeros((1, 128, SLAB - 2048), np.float32)], axis=2)


def _pk(v):
    return np.ascontiguousarray(v.reshape(-1, 128).T)


_QPERM = None


def _qperm():
    cols = []
    for j in range(8):
        kp, g = j // 4, j % 4
        for kv in (kp * 2, kp * 2 + 1):
            h = kv * 4 + g
            cols.extend(range(h * 64, h * 64 + 64))
    return np.array(cols)


def prep_shared(inp):
    f = np.float32
    qp = _qperm()
    slabs = []
    w_in = inp["a_w_in"][0]
    slabs.append(_slab_std(w_in[:, 0:3072]))
    slabs.append(_slab_std(inp["a_w_out"][0]))
    slabs.append(_slab_std(inp["mlp_w_up"][0]))
    slabs.append(_slab_down(inp["mlp_w_down"][0]))
    slabs.append(_slab_std(inp["ple_w_gate"][0]))
    slabs.append(_slab_proj(inp["ple_w_proj"][0]))
    slabs.append(_slab_std(inp["kv_w"]))
    slabs.append(_slab_std(inp["b_w_q"][0][:, qp]))
    slabs.append(_slab_std(inp["b_w_o"][0]))
    slabs.append(_slab_std(inp["mlp_w_up"][1]))
    slabs.append(_slab_down(inp["mlp_w_down"][1]))
    slabs.append(_slab_std(inp["ple_w_gate"][1]))
    slabs.append(_slab_proj(inp["ple_w_proj"][1]))
    ws = np.ascontiguousarray(np.concatenate(slabs, axis=0).astype(f))
    assert ws.shape == (NS, 128, SLAB), ws.shape

    cst = np.zeros((128, NCST), f)

    def put(name, arr):
        o, n = _c[name]
        cst[:, o:o + n] = arr

    for l in range(2):
        put(f"mixg{l}", _pk(inp["mix_ln_g"][l]))
        put(f"mixb{l}", _pk(inp["mix_ln_b"][l]))
        put(f"mlpg{l}", _pk(inp["mlp_ln_g"][l]))
        put(f"mlpb{l}", _pk(inp["mlp_ln_b"][l]))
        put(f"pleb{l}", _pk(inp["ple_b_gate"][l]))
    put("bo", _pk(inp["b_b_o"][0]))
    put("hg", _pk(inp["a_head_norm_g"][0]))
    put("bq", _pk(inp["b_b_q"][0][qp]))
    put("bk", _pk(inp["kv_b"][0:256]))
    p = np.arange(128)
    d = p % 64
    half = 8
    invf_tab = np.power(np.float32(500000.0), -np.arange(half, dtype=f) * f(2.0 / 16)).astype(f)
    invf = np.where(d < 16, invf_tab[d % 8], 0.0).astype(f)
    sgn = np.where(d < 8, -1.0, np.where(d < 16, 1.0, 0.0)).astype(f)
    put("invf", invf[:, None])
    put("sgn", sgn[:, None])
    gbias = np.concatenate([inp["a_b_igate"][0], inp["a_b_fgate"][0]]).astype(f)
    put("gb", np.broadcast_to(np.tile(gbias, 8)[None, :], (128, 64)))
    put("vb", np.broadcast_to(inp["kv_b"][256:512][None, :], (128, 256)))
    put("sink", np.broadcast_to(inp["b_sinks"][0][None, :], (128, 16)))
    s = np.arange(128)[:, None]
    t = np.arange(128)[None, :]
    put("ustrict", (s > t).astype(f))
    put("ones", np.ones((128, 128), f))
    put("ident", np.eye(128, dtype=f))
    put("maskm", (s <= t).astype(f))
    NEG = f(-30000.0)
    tq = np.arange(128)[:, None]
    sk = np.arange(128)[None, :]
    prev_ok = sk > tq
    cur_ok = sk <= tq
    put("maskb", np.concatenate([np.where(prev_ok, f(0), NEG), np.where(cur_ok, f(0), NEG)], axis=1).astype(f))
    put("maskbf", np.concatenate([np.full((128, 128), NEG, f), np.where(cur_ok, f(0), NEG)], axis=1).astype(f))
    pm = np.zeros((128, 128), f)
    for m in range(128):
        dm = m % 64
        if dm < 8:
            pm[m + 8, m] = 1.0
        elif dm < 16:
            pm[m - 8, m] = 1.0
    put("pm", pm)
    wg = w_in[:, 3072:3080].reshape(8, 128, 8).transpose(1, 0, 2).reshape(128, 64)
    put("wg", wg)
    return ws, cst


def kernel(**inputs):
    inp = {k: np.asarray(v) for k, v in inputs.items()}
    x = inp["x"]
    Bn, SEQL, _ = x.shape
    ntiles = SEQL // TT
    ws, cst = prep_shared(inp)
    nc = build(ntiles=ntiles, nlayers=2)
    in_maps = []
    for b in range(Bn):
        in_maps.append({
            "xT": np.ascontiguousarray(x[b].T),
            "pT": np.ascontiguousarray(inp["p"][:, b].transpose(0, 2, 1)),
            "pos": np.ascontiguousarray(inp["positions"][b][None, :].astype(np.int32)),
            "wslab": ws,
            "cst": cst,
        })
    res = run_bass_kernel_spmd(nc, in_maps, core_ids=list(range(Bn)))
    out = np.stack([np.ascontiguousarray(r["outT"].T) for r in res.results], axis=0)
    return out.astype(np.float32)
```

```python
import math
from contextlib import ExitStack

import numpy as np
import concourse.bass as bass
import concourse.mybir as mybir
from concourse.bass_utils import run_bass_kernel_spmd

F32 = mybir.dt.float32
BF16 = mybir.dt.bfloat16
I32 = mybir.dt.int32
ALU = mybir.AluOpType
AF = mybir.ActivationFunctionType
AX = mybir.AxisListType

D = 1024
TT = 1024
NBLK = TT // 128
NHALF = TT // 512
NW = 4
SLAB = 4096
ALPHA = float((2 * 2) ** 0.25)
LN_EPS = 1e-5
NS = 51
PI = math.pi

_c = {}
_off = 0


def _add(name, n):
    global _off
    _c[name] = (_off, n)
    _off += n


for _l in range(2):
    for _n in ("mixg", "mixb", "mlpg", "mlpb", "pleb"):
        _add(f"{_n}{_l}", 8)
_add("bo", 8)
_add("hg", 8)
_add("bq", 8)
_add("bk", 2)
_add("invf", 1)
_add("sgn", 1)
_add("gb", 64)
_add("vb", 256)
_add("sink", 16)
_add("ustrict", 128)
_add("ones", 128)
_add("ident", 128)
_add("maskm", 128)
_add("maskb", 256)
_add("maskbf", 256)
_add("pm", 128)
_add("wg", 64)
NCST = _off


class Buf:
    __slots__ = ("name", "lw", "rd", "const", "excl")

    def __init__(self, name, const=False, excl=False):
        self.name = name
        self.lw = None
        self.rd = []
        self.const = const
        self.excl = excl


class Op:
    __slots__ = ("id", "eng", "fns", "cost", "deps", "kind", "dsem", "nbytes", "tag", "ms", "xlat")

    def __init__(self, id, eng, fns, cost, deps, kind, dsem, nbytes, tag):
        self.id = id
        self.eng = eng
        self.fns = fns
        self.cost = cost
        self.deps = deps
        self.kind = kind
        self.dsem = dsem
        self.nbytes = nbytes
        self.tag = tag
        self.ms = None
        self.xlat = 0.0


class Sched:
    WINDOW = 128
    LAT_X = 140.0
    LAT_S = 50.0
    DMA_BW = 170.0
    DMA_LAT = 2000.0

    def __init__(self, nc, es):
        self.nc = nc
        self.es = es
        self.sems = {}
        self.engs = ("pe", "act", "dve", "pool", "sp")
        for n in self.engs:
            self.sems[n] = es.enter_context(nc.semaphore("s_" + n))
        self.ops = []
        self.pending = None
        self.phase = "init"
        self.final_waits = []

    def dsem(self, name):
        self.sems[name] = self.es.enter_context(self.nc.semaphore("d_" + name))
        return name

    def _collect(self, reads, writes):
        deps = set()
        for b in reads:
            if b.lw is not None:
                deps.add(b.lw)
            if b.excl:
                deps.update(b.rd)
        for b in writes:
            if b.lw is not None:
                deps.add(b.lw)
            deps.update(b.rd)
        return deps

    def _commit(self, oid, reads, writes):
        for b in writes:
            b.lw = oid
            b.rd = []
        for b in reads:
            if b.excl:
                b.lw = oid
                b.rd = []
            elif not b.const:
                b.rd.append(oid)

    def emit(self, eng, fn, reads=(), writes=(), inc=True, cost=100.0):
        if eng == "pe":
            if self.pending is None:
                self.pending = Op(len(self.ops), "pe", [], 0.0, set(), "c", None, 0, self.phase)
                self.ops.append(self.pending)
            op = self.pending
            op.fns.append(fn)
            op.cost += cost
            d = self._collect(reads, writes)
            d.discard(op.id)
            op.deps |= d
            self._commit(op.id, reads, writes)
            if inc:
                self.pending = None
            return
        assert self.pending is None or True
        oid = len(self.ops)
        op = Op(oid, eng, [fn], cost, self._collect(reads, writes), "c", None, 0, self.phase)
        self.ops.append(op)
        self._commit(oid, reads, writes)

    def dma(self, q, ds, out, in_, reads=(), writes=(), nbytes=0, xlat=0.0):
        oid = len(self.ops)
        op = Op(oid, q, [lambda e: e.dma_start(out=out, in_=in_)], 60.0 if q == "sp" else 900.0,
                self._collect(reads, writes), "d", ds, nbytes, self.phase)
        op.xlat = xlat
        self.ops.append(op)
        self._commit(oid, reads, writes)

    def schedule(self):
        assert self.pending is None, "open PE accumulation group"
        ops = self.ops
        n = len(ops)
        elist = {e: [] for e in self.engs}
        for op in ops:
            elist[op.eng].append(op.id)
        pos = {e: 0 for e in self.engs}
        done = [False] * n
        fin = [0.0] * n
        free = {e: 0.0 for e in self.engs}
        dma_free = 0.0
        order = {e: [] for e in self.engs}
        left = n
        W = self.WINDOW
        while left:
            best = None
            for e in self.engs:
                lst = elist[e]
                p = pos[e]
                while p < len(lst) and done[lst[p]]:
                    p += 1
                pos[e] = p
                cnt = 0
                i = p
                fe = free[e]
                cand = None
                while i < len(lst) and cnt < W:
                    oid = lst[i]
                    i += 1
                    if done[oid]:
                        continue
                    cnt += 1
                    op = ops[oid]
                    st = fe
                    ok = True
                    for d in op.deps:
                        if not done[d]:
                            ok = False
                            break
                        de = ops[d].eng
                        if ops[d].kind == "d":
                            t = fin[d] + self.LAT_X
                        elif de == e:
                            t = fin[d] + (0.0 if e == "pe" else self.LAT_S)
                        else:
                            t = fin[d] + self.LAT_X
                        if t > st:
                            st = t
                    if not ok:
                        continue
                    if cand is None or st < cand[0]:
                        cand = (st, oid)
                        if st <= fe:
                            break
                if cand is not None and (best is None or cand < best):
                    best = cand
            assert best is not None, "scheduler deadlock"
            st, oid = best
            op = ops[oid]
            e = op.eng
            done[oid] = True
            left -= 1
            order[e].append(oid)
            if op.kind == "d":
                free[e] = st + op.cost
                ds_ = max(free[e], dma_free)
                dma_free = ds_ + op.nbytes / self.DMA_BW
                fin[oid] = dma_free + self.DMA_LAT + op.xlat
            else:
                free[e] = st + op.cost
                fin[oid] = free[e]
        self.order = order
        self.fin = fin
        self.est_ns = max(fin) if fin else 0.0
        cnt = {k: 0 for k in self.sems}
        for e in self.engs:
            for oid in order[e]:
                op = ops[oid]
                if op.kind == "d":
                    cnt[op.dsem] += 16
                    op.ms = (op.dsem, cnt[op.dsem])
                else:
                    cnt[e] += 1
                    op.ms = (e, cnt[e])
        self.final_counts = cnt

    def run(self, eng, e):
        ops = self.ops
        seen = {}
        for oid in self.order[eng]:
            op = ops[oid]
            need = {}
            for d in op.deps:
                sn, idx = ops[d].ms
                if sn == "pe" and eng == "pe":
                    continue
                if need.get(sn, 0) < idx:
                    need[sn] = idx
            for sn, idx in need.items():
                if seen.get(sn, 0) >= idx:
                    continue
                seen[sn] = idx
                e.wait_ge(self.sems[sn], idx)
            last = None
            for fn in op.fns:
                last = fn(e)
            last.then_inc(self.sems[op.ms[0]], 16 if op.kind == "d" else 1)
        if eng == "sp":
            for sn in self.final_waits:
                if self.final_counts[sn] > 0:
                    e.wait_ge(self.sems[sn], self.final_counts[sn])


def build(ntiles=4, nlayers=2, dbg=()):
    nc = bass.Bass("TRN2", target_bir_lowering=False)
    SEQL = ntiles * TT
    xT_d = nc.dram_tensor("xT", [D, SEQL], F32, kind="ExternalInput").ap()
    pT_d = nc.dram_tensor("pT", [2, 256, SEQL], F32, kind="ExternalInput").ap()
    pos_d = nc.dram_tensor("pos", [1, SEQL], I32, kind="ExternalInput").ap()
    ws_d = nc.dram_tensor("wslab", [NS, 128, SLAB], F32, kind="ExternalInput").ap()
    cst_d = nc.dram_tensor("cst", [128, NCST], F32, kind="ExternalInput").ap()
    out_d = nc.dram_tensor("outT", [D, SEQL], F32, kind="ExternalOutput").ap()

    es = ExitStack()
    with es:
        S = Sched(nc, es)

        def sb(name, shape, dt):
            return es.enter_context(nc.sbuf_tensor(name, shape, dt))

        xT32 = sb("xT32", [128, 8, TT], F32)
        xb = sb("xb", [128, 8, TT], BF16)
        yT = xb
        arena = sb("arena", [128, 32768], BF16)
        ring = [sb(f"ring{i}", [128, SLAB], BF16) for i in range(NW)]
        cst = sb("cst_sb", [128, NCST], F32)
        zst = [sb(f"zst{i}", [128, 2, 512], BF16) for i in range(4)]
        lnmean = sb("lnmean", [128, TT], F32)
        lnrstd = sb("lnrstd", [128, TT], F32)
        lntmp = [sb(f"lntmp{i}", [128, 512], F32) for i in range(2)]
        ptb = sb("ptb_sb", [128, 2, TT], BF16)
        identb = sb("identb", [128, 128], BF16)
        onesb = sb("onesb", [128, 128], BF16)
        maskmb = sb("maskmb", [128, 128], BF16)
        maskbb = [sb(f"maskbb{i}", [128, 2, 256], BF16) for i in range(2)]
        sinkmax = sb("sinkmax", [128, 4], F32)
        pmb = sb("pmb", [128, 128], BF16)
        wgb = sb("wgb", [128, 8, 8], BF16)
        bq8 = sb("bq8", [128, 8], F32)
        epsc = sb("epsc", [128, 1], F32)
        Cs = sb("Cs", [128, 4, 257], F32)
        Cb = [sb(f"Cb{i}", [128, 257], BF16) for i in range(3)]
        PTm = [sb(f"PTm{i}", [128, 128], BF16) for i in range(3)]
        gz = sb("gz", [128, 8, 8], F32)
        gth = sb("gth", [128, 8, 8], F32)
        gef = sb("gef", [128, 8, 4], F32)
        gsp = sb("gsp", [128, 8, 4], F32)
        gei = sb("gei", [128, 8, 4], F32)
        ges = sb("ges", [128, 8, 4], F32)
        gfr = sb("gfr", [128, 2, 8, 4], F32)
        nsm = [sb(f"nsm{i}", [128, 8, 4], F32) for i in range(2)]
        nst = [sb(f"nst{i}", [128, 4, 6], F32) for i in range(2)]
        nmv = [sb(f"nmv{i}", [128, 4, 2], F32) for i in range(2)]
        hn = [sb(f"hn{i}", [128, 4, 256], BF16) for i in range(2)]
        ytok = [sb(f"ytok{i}", [128, 1024], BF16) for i in range(2)]
        kTa = sb("kTa", [128, 2, 128 + TT], BF16)
        va = sb("va", [128, NBLK + 1, 4, 65], BF16)
        posi = sb("posi", [128, 1, TT], I32)
        asm = [sb(f"asm{i}", [128, 6, 4], F32) for i in range(3)]
        ps_all = es.enter_context(nc.psum_tensor("ps", [128, 8, 512], F32))
        nc.sbuf_left = nc.sbuf_bytes_remaining

        def av(off, n):
            return arena[:, off:off + n]

        hT = arena[:, :].rearrange("p (c t) -> p c t", c=32)
        xo = arena[:, 0:16384].bitcast(F32).rearrange("p (c t) -> p c t", c=8)
        m_qT = av(0, 4096).rearrange("p (h t) -> p h t", h=4)
        m_kT = av(4096, 4096).rearrange("p (h t) -> p h t", h=4)
        m_kw = av(8192, 4096).rearrange("p (b n) -> p b n", b=8)
        m_va = av(12288, 8 * 4 * 258).rearrange("p (b h n) -> p b h n", b=8, h=4)
        m_sgo = av(20608, 8192).rearrange("p (b n) -> p b n", b=8)
        a_qT = av(0, 8192).rearrange("p (c t) -> p c t", c=8)
        a_cos = av(8192, 2048).bitcast(F32)
        a_sin = av(10240, 2048).bitcast(F32)
        a_posf = av(12288, 2048).bitcast(F32)
        a_ang = av(14336, 2048).bitcast(F32)
        a_u = av(27648, 2048).bitcast(F32)
        a_ki = av(29696, 2048).bitcast(I32)
        a_q32 = [av(16384 + i * 1024, 1024).bitcast(F32) for i in range(2)]
        a_qb = [av(18432 + i * 512, 512) for i in range(2)]
        a_ra = [av(19456 + i * 1024, 1024).bitcast(F32) for i in range(2)]
        a_rb = [av(21504 + i * 1024, 1024).bitcast(F32) for i in range(2)]
        Pexp = [av(23552 + i * 1024, 1024).rearrange("p (g s) -> p g s", g=4) for i in range(2)]
        PTa = [av(25600 + i * 1024, 1024).rearrange("p (c t) -> p c t", c=8) for i in range(2)]
        Pexp.append(av(12288, 1024).rearrange("p (g s) -> p g s", g=4))
        PTa.append(av(27648, 1024).rearrange("p (c t) -> p c t", c=8))
        PEXP_B = ["Pexp0", "Pexp1", "aposf"]
        PTA_B = ["PTa0", "PTa1", "au"]

        def cc(name, lo=0, n=None):
            o, nn = _c[name]
            if n is None:
                n = nn - lo
            return cst[:, o + lo:o + lo + n]

        B = {}

        CONST_BUFS = {"cst", "identb", "onesb", "maskmb", "maskbb", "sinkmax", "pmb", "wgb", "bq8", "epsc"}

        def bf(name):
            if name not in B:
                B[name] = Buf(name, const=name in CONST_BUFS)
            return B[name]

        AT = Buf("arena_tok")
        dummy = sb("dummy_sb", [128, 2], F32)

        def arena_gen():
            S.emit("dve", lambda e: e.memset(dummy[:, 0:1], 0.0), [], [AT], cost=60.0)

        def fsz(ap):
            n = 1
            for d in ap.shape[1:]:
                n *= d
            return n

        def ar(aps, reads):
            for a in aps:
                if getattr(a, "name", None) == "arena":
                    return list(reads) + [AT]
            return reads

        psb = [Buf(f"ps{i}", excl=True) for i in range(8)]
        ringb = [Buf(f"ring{i}") for i in range(NW)]
        ringsem = [S.dsem(f"ring{i}") for i in range(NW)]
        cst_sem = S.dsem("cst")
        x_sems = [S.dsem(f"xin{i}") for i in range(16)]
        p_sem = S.dsem("pin")
        pos_sem = S.dsem("pos")
        out_sems = [S.dsem(f"out{i}") for i in range(8)]

        def ps(i):
            return ps_all[:, i, :]

        dbg_map = {}
        if dbg:
            dbg_d = nc.dram_tensor("dbg", [128, 16384], F32, kind="ExternalOutput").ap()
            dbg_sem = S.dsem("dbg")

        def dump(tag, ap2d, bufs):
            if tag not in dbg or tag in dbg_map:
                return
            shp = list(ap2d.shape[1:])
            n = int(np.prod(shp))
            o = sum(v[1] for v in dbg_map.values())
            dbg_map[tag] = (o, n)
            dst = dbg_d[:, o:o + n]
            if len(shp) == 2:
                dst = dst.rearrange("p (a b) -> p a b", a=shp[0])
            elif len(shp) == 3:
                dst = dst.rearrange("p (a b c) -> p a b c", a=shp[0], b=shp[1])
            S.dma("pool", dbg_sem, dst, ap2d, reads=ar((ap2d,), bufs), nbytes=128 * n * 4)

        def mm(out, lhsT, rhs, start, stop, reads, writes, inc):
            n = fsz(rhs)
            c = (max(n, 48) * 0.5 + 14.0) * (4.0 if rhs.dtype == F32 else 1.0)
            S.emit("pe", lambda e: e.matmul(out, lhsT, rhs, start=start, stop=stop), ar((lhsT, rhs), reads), writes, inc, cost=c)

        def tr(out, in_, reads, writes, inc):
            S.emit("pe", lambda e: e.transpose(out, in_, identb[:, :]), ar((in_,), reads + [bf("identb")]), writes, inc, cost=80.0)

        def act(out, in_, func, reads, writes, bias=None, scale=None):
            kw = {}
            if bias is not None:
                kw["bias"] = bias
            if scale is not None:
                kw["scale"] = scale
            S.emit("act", lambda e: e.activation(out, in_, func, **kw), ar((out, in_), reads), writes, cost=210.0 + 0.9 * fsz(out))

        def tt(eng, out, in0, in1, op, reads, writes):
            S.emit(eng, lambda e: e.tensor_tensor(out, in0, in1, op), ar((out, in0, in1), reads), writes,
                   cost=(70.0 + 1.05 * fsz(out)) * (2.0 if eng == "pool" else 1.0))

        def ts(eng, out, in0, s1, s2, op0, op1, reads, writes):
            c = 70.0 + 1.05 * fsz(out)
            if op1 is None:
                S.emit(eng, lambda e: e.tensor_scalar(out, in0, s1, None, op0), ar((out, in0), reads), writes, cost=c)
            else:
                S.emit(eng, lambda e: e.tensor_scalar(out, in0, s1, s2, op0, op1), ar((out, in0), reads), writes, cost=c)

        def stt(eng, out, in0, sc, in1, op0, op1, reads, writes):
            S.emit(eng, lambda e: e.scalar_tensor_tensor(out, in0, sc, in1, op0, op1), ar((out, in0, in1), reads), writes,
                   cost=70.0 + 1.05 * fsz(out))

        def cp(eng, out, in_, reads, writes):
            S.emit(eng, lambda e: e.tensor_copy(out, in_), ar((out, in_), reads), writes,
                   cost=(70.0 + 1.05 * fsz(out)) * (1.6 if eng == "pool" else 1.0))

        order0 = [1, 0, 2, 3, 4, 5, 6, 7] + list(range(8, 24)) + [26, 24, 25]
        order1 = [27, 28, 29, 30, 31] + list(range(32, 48)) + [50, 48, 49]
        per_tile = order0 + (order1 if nlayers > 1 else [])
        wseq = per_tile * ntiles
        wstate = {"issued": 0, "used": 0}
        wlive = set()
        wheld = set()

        def w_pump():
            while wstate["issued"] < len(wseq):
                i = wstate["issued"]
                prev = i - NW
                if prev >= 0 and (prev >= wstate["used"] or prev in wlive):
                    break
                slot = i % NW
                S.dma("pool", ringsem[slot], ring[slot][:, :], ws_d[wseq[i]], reads=(), writes=[ringb[slot]], nbytes=128 * SLAB * 4)
                wstate["issued"] += 1

        def w_next(expect, hold=False):
            i = wstate["used"]
            assert wseq[i] == expect, (wseq[i], expect)
            for j in list(wlive):
                if j not in wheld:
                    wlive.discard(j)
            wstate["used"] += 1
            wlive.add(i)
            if hold:
                wheld.add(i)
            w_pump()
            assert wstate["issued"] > i
            slot = i % NW
            return ring[slot], ringb[slot]

        def w_release_held():
            for j in list(wheld):
                wheld.discard(j)
                wlive.discard(j)

        S.dma("sp", cst_sem, cst[:, :], cst_d[:, :], writes=[bf("cst")], nbytes=128 * NCST * 4)
        w_pump()
        cb = bf("cst")
        cp("dve", identb[:, :], cc("ident"), [cb], [bf("identb")])
        cp("dve", maskmb[:, :], cc("maskm"), [cb], [bf("maskmb")])
        for i_, nm_ in enumerate(("maskb", "maskbf")):
            cp("dve", maskbb[i_][:, :, :], cc(nm_).unsqueeze(1).broadcast_to([128, 2, 256]), [cb], [bf("maskbb")])
        S.emit("dve", lambda e: e.tensor_reduce(sinkmax[:, :], cc("sink").rearrange("p (k g) -> p k g", k=4), AX.X, ALU.max),
               [cb], [bf("sinkmax")])
        cp("dve", pmb[:, :], cc("pm"), [cb], [bf("pmb")])
        cp("dve", wgb[:, :, :], cc("wg").rearrange("p (k g) -> p k g", k=8), [cb], [bf("wgb")])
        S.emit("dve", lambda e: e.memset(onesb[:, :], 1.0 / 1024.0), [], [bf("onesb")])
        ts("dve", bq8[:, :], cc("bq"), 0.125, None, ALU.mult, None, [cb], [bf("bq8")])
        S.emit("dve", lambda e: e.memset(Cs[:, :, :], 0.0), [], [bf("Cs")])
        S.emit("dve", lambda e: e.memset(epsc[:, :], LN_EPS), [], [bf("epsc")])
        S.emit("dve", lambda e: e.memset(kTa[:, :, 0:128], 0.0), [], [bf("kTa0")])
        S.emit("dve", lambda e: e.memset(va[:, :, :, :], 0.0), [], [bf("va_all")])
        S.emit("dve", lambda e: e.memset(va[:, :, :, 64:65], 1.0), [bf("va_all")], [bf("va_all")])

        psrot = {"i": 0}

        def next_ps(lst):
            i = lst[psrot["i"] % len(lst)]
            psrot["i"] += 1
            return i

        ALLB = list(range(8))

        def ln_phase(l, which, groups, order):
            gcol = f"{which}g{l}"
            bcol = f"{which}b{l}"
            emit_group, bias_name = groups
            rot = [0, 1, 2, 3]
            pm_i = [4, 5]
            pq_i = [6, 7]
            pend = []
            cnt_h = [0, 0]

            def normalize(half):
                hs = slice(half * 512, (half + 1) * 512)
                rb = bf(f"lnrstd{half}")
                pmb_ = psb[pm_i[half]]
                act(lnrstd[:, hs], ps(pm_i[half]), AF.Square, [pmb_], [rb])
                tt("dve", lnrstd[:, hs], ps(pq_i[half]), lnrstd[:, hs], ALU.subtract, [psb[pq_i[half]], rb], [rb])
                act(lnrstd[:, hs], lnrstd[:, hs], AF.Sqrt, [rb, bf("epsc")], [rb], bias=epsc[:, 0:1])
                S.emit("dve", lambda e, o=lnrstd[:, hs]: e.reciprocal(o, o), [rb], [rb], cost=610.0)
                for oc in range(8):
                    xs = xT32[:, oc, hs]
                    xbuf = bf(f"x{oc}_{half}")
                    xbb = bf(f"xb{oc}_{half}")
                    tt("dve", xs, xs, ps(pm_i[half]), ALU.subtract, [xbuf, pmb_], [xbuf])
                    tt("dve", xs, xs, lnrstd[:, hs], ALU.mult, [xbuf, rb], [xbuf])
                    act(xb[:, oc, hs], xs, AF.Identity, [xbuf, cb], [xbb], bias=cc(bcol, oc, 1), scale=cc(gcol, oc, 1))
                for oc in range(8):
                    xs = xT32[:, oc, hs]
                    xbuf = bf(f"x{oc}_{half}")
                    act(xs, xs, AF.Identity, [xbuf, cb], [xbuf], bias=cc(bcol, oc, 1), scale=cc(gcol, oc, 1))

            def stats(half, slot):
                first = cnt_h[half] == 0
                last = cnt_h[half] == 7
                cnt_h[half] += 1
                mm(ps(pm_i[half]), onesb[:, :], zst[slot][:, 0, :], first, last,
                   [bf("onesb"), bf(f"zsta{slot}")], [psb[pm_i[half]]], False)
                mm(ps(pq_i[half]), onesb[:, :], zst[slot][:, 1, :], first, last,
                   [bf("onesb"), bf(f"zst{slot}")], [psb[pq_i[half]]], True)
                if last:
                    normalize(half)

            k = 0
            for oc, half in order:
                pi = next_ps(rot)
                emit_group(oc, half, pi)
                xs = xT32[:, oc, half * 512:(half + 1) * 512]
                xbuf = bf(f"x{oc}_{half}")
                if bias_name == "pre":
                    tt("dve", xs, xs, ps(pi), ALU.add, [xbuf, psb[pi]], [xbuf])
                elif bias_name is not None:
                    tmp = lntmp[k % 2]
                    tb = bf(f"lntmp{k % 2}")
                    act(tmp[:, :], ps(pi), AF.Identity, [psb[pi], cb], [tb], bias=cc(bias_name, oc, 1))
                    stt("dve", xs, xs, ALPHA, tmp[:, :], ALU.mult, ALU.add, [xbuf, tb], [xbuf])
                else:
                    stt("dve", xs, xs, ALPHA, ps(pi), ALU.mult, ALU.add, [xbuf, psb[pi]], [xbuf])
                slot = k % 4
                zb_ = bf(f"zst{slot}")
                cp("pool", zst[slot][:, 0, :], xs, [xbuf], [bf(f"zsta{slot}")])
                act(zst[slot][:, 1, :], xs, AF.Square, [xbuf], [zb_])
                pend.append((half, slot))
                if len(pend) > 1:
                    stats(*pend.pop(0))
                k += 1
            while pend:
                stats(*pend.pop(0))

        ORDER_HALF_MAJOR = [(oc, h) for h in range(NHALF) for oc in range(8)]
        ORDER_PAIRS = [(2 * p_ + i_, h) for p_ in range(4) for h in range(NHALF) for i_ in range(2)]

        def xb_reads(half=None):
            if half is None:
                return [bf(f"xb{oc}_{h}") for oc in range(8) for h in range(NHALF)]
            return [bf(f"xb{oc}_{half}") for oc in range(8)]

        def mlp_ple(l, it, base, last=False):
            t0 = it * TT
            S.dma("pool", p_sem, ptb[:, :, :], pT_d[l, :, t0:t0 + TT].rearrange("(k p) t -> p k t", p=128),
                  writes=[bf("ptb")], nbytes=256 * TT * 4)
            S.phase = f"l{l}.up"
            arena_gen()
            for sp_ in range(4):
                pair = []
                for i_ in range(2):
                    slot, sbuf_ = w_next(base + 8 + 2 * sp_ + i_, hold=True)
                    pair.append((2 * sp_ + i_, slot[:, :].rearrange("p (k n) -> p k n", k=8), sbuf_))
                for half in range(NHALF):
                    for s, sv, sbuf_ in pair:
                        for j in range(4):
                            hc = s * 4 + j
                            pi = next_ps(ALLB)
                            for kc in range(8):
                                mm(ps(pi), sv[:, kc, j * 128:(j + 1) * 128], xb[:, kc, half * 512:(half + 1) * 512],
                                   kc == 0, kc == 7, [sbuf_] + xb_reads(half), [psb[pi]], kc == 7)
                            hb = bf(f"hT{hc}_{half}")
                            ho = hT[:, hc, half * 512:(half + 1) * 512]
                            lt = lntmp[(hc * 2 + half) % 2]
                            ltb = bf(f"lntmp{(hc * 2 + half) % 2}")
                            act(lt[:, :], ps(pi), AF.Relu, [psb[pi]], [ltb])
                            tt("dve", ho, lt[:, :], lt[:, :], ALU.mult, [ltb], [hb])
                w_release_held()

            S.phase = f"l{l}.down_ln"
            dsl = {}

            def down_group(oc, half, pi):
                if oc not in dsl:
                    slot, sbuf_ = w_next(base + 16 + oc, hold=True)
                    dsl[oc] = (slot[:, :].rearrange("p (k n) -> p k n", k=32), sbuf_)
                sv, sbuf_ = dsl[oc]
                for hc in range(32):
                    mm(ps(pi), sv[:, hc, :], hT[:, hc, half * 512:(half + 1) * 512], hc == 0, hc == 31,
                       [sbuf_, bf(f"hT{hc}_{half}")], [psb[pi]], hc % 8 == 7)
                if oc % 2 == 1 and half == NHALF - 1:
                    w_release_held()

            ln_phase(l, "mlp", (down_group, None), ORDER_PAIRS)
            S.phase = f"l{l}.ple"
            if last:
                arena_gen()
            slot_p, sbuf_p = w_next(base + 26, hold=True)
            spv = slot_p[:, 0:2048].rearrange("p (k n) -> p k n", k=2)
            rotg = [0, 1, 2, 3]
            rotp = [4, 5, 6, 7]
            gsl = [w_next(base + 24 + i_, hold=True) for i_ in range(2)]
            k = 0
            for half in range(NHALF):
                hs = slice(half * 512, (half + 1) * 512)
                for oc in range(8):
                    gslot, gbuf = gsl[oc // 4]
                    gv = gslot[:, :].rearrange("p (k n) -> p k n", k=8)
                    j = oc % 4
                    pg = rotg[k % 4]
                    pp = rotp[k % 4]
                    for kc in range(8):
                        mm(ps(pg), gv[:, kc, j * 128:(j + 1) * 128], xb[:, kc, hs], kc == 0, kc == 7,
                           [gbuf] + xb_reads(half), [psb[pg]], kc == 7)
                    for kc in range(2):
                        mm(ps(pp), spv[:, kc, oc * 128:(oc + 1) * 128], ptb[:, kc, hs], kc == 0, kc == 1,
                           [sbuf_p, bf("ptb")], [psb[pp]], kc == 1)
                    lt = lntmp[k % 2]
                    ltb = bf(f"lntmp{k % 2}")
                    act(lt[:, :], ps(pg), AF.Sigmoid, [psb[pg], cb], [ltb], bias=cc(f"pleb{l}", oc, 1))
                    tt("dve", lt[:, :], lt[:, :], ps(pp), ALU.mult, [ltb, psb[pp]], [ltb])
                    xs = xT32[:, oc, hs]
                    xbuf = bf(f"x{oc}_{half}")
                    if last:
                        tt("pool", xo[:, oc, hs], xs, lt[:, :], ALU.add, [xbuf, ltb], [bf(f"xo{oc}")])
                    else:
                        tt("pool", xs, xs, lt[:, :], ALU.add, [xbuf, ltb], [xbuf])
                    k += 1
            w_release_held()
            if last:
                return
            for oc in range(8):
                for half in range(NHALF):
                    hs = slice(half * 512, (half + 1) * 512)
                    act(xb[:, oc, hs], xT32[:, oc, hs], AF.Copy, [bf(f"x{oc}_{half}")], [bf(f"xb{oc}_{half}")])

        def layer0(it):
            t0 = it * TT
            S.phase = "l0.load"
            arena_gen()
            for half in range(NHALF):
                for oc in range(8):
                    S.dma("sp", x_sems[oc * 2 + half], xT32[:, oc, half * 512:(half + 1) * 512],
                          xT_d[oc * 128:(oc + 1) * 128, t0 + half * 512:t0 + (half + 1) * 512], nbytes=128 * 512 * 4, xlat=15000.0,
                          writes=[bf(f"x{oc}_{half}")])
            for half in range(NHALF):
                for oc in range(8):
                    hs = slice(half * 512, (half + 1) * 512)
                    act(xb[:, oc, hs], xT32[:, oc, hs], AF.Copy, [bf(f"x{oc}_{half}")], [bf(f"xb{oc}_{half}")])
            S.emit("dve", lambda e: e.memset(m_va[:, :, :, 256:257], 1.0), [AT], [bf("mva_ones")], cost=100.0)
            S.phase = "l0.gates"
            for hf in range(NHALF):
                b4 = slice(hf * 4, hf * 4 + 4)
                pg = next_ps(ALLB)
                for bb in range(4):
                    blk = hf * 4 + bb
                    for kc in range(8):
                        mm(ps(pg)[:, bb * 8:(bb + 1) * 8], xb[:, kc, blk * 128:(blk + 1) * 128], wgb[:, kc, :],
                           kc == 0, kc == 7, [bf("wgb")] + xb_reads(hf), [psb[pg]], kc == 7 and bb == 3)
                gzb, gthb = bf(f"gz{hf}"), bf(f"gth{hf}")
                tt("dve", gz[:, b4, :], ps(pg)[:, 0:32].rearrange("p (b g) -> p b g", g=8),
                   cc("gb", 0, 32).rearrange("p (b g) -> p b g", g=8), ALU.add, [psb[pg], cb], [gzb])
                act(gth[:, b4, :], gz[:, b4, :], AF.Tanh, [gzb], [gthb], scale=1.0 / 15.0)
                act(gef[:, b4, :], gth[:, b4, 4:8], AF.Exp, [gthb], [bf(f"gef{hf}")], scale=-15.0)
                act(gsp[:, b4, :], gef[:, b4, :], AF.Ln, [bf(f"gef{hf}")], [bf(f"gsp{hf}")], bias=1.0)
                pa = next_ps(ALLB)
                gspf = gsp[:, b4, :].rearrange("p b g -> p (b g)")
                mm(ps(pa)[:, 0:16], cc("ustrict"), gspf, True, True, [cb, bf(f"gsp{hf}")], [psb[pa]], False)
                mm(ps(pa)[:, 16:32], cc("ones"), gspf, True, True, [cb, bf(f"gsp{hf}")], [psb[pa]], True)
                stt("dve", gei[:, b4, :], ps(pa)[:, 0:16].rearrange("p (b g) -> p b g", g=4), -1.0 / 15.0, gth[:, b4, 0:4],
                    ALU.mult, ALU.add, [gthb, psb[pa]], [bf(f"gei{hf}")])
                act(ges[:, b4, :], gei[:, b4, :], AF.Exp, [bf(f"gei{hf}")], [bf(f"ges{hf}")], scale=15.0)
                act(gfr[:, :, b4, :], ps(pa)[:, 0:32].rearrange("p (a b g) -> p a b g", a=2, g=4), AF.Exp,
                    [psb[pa]], [bf(f"gfr{hf}")], scale=-1.0)
            S.phase = "l0.proj"
            slot, sbuf_ = w_next(1)
            sv = slot[:, :].rearrange("p (k n) -> p k n", k=8)
            for h in range(4):
                for half in range(NHALF):
                    hs = slice(half * 512, (half + 1) * 512)
                    pi = next_ps(ALLB)
                    for kc in range(8):
                        mm(ps(pi), sv[:, kc, h * 128:(h + 1) * 128], xb[:, kc, hs], kc == 0, kc == 7,
                           [sbuf_] + xb_reads(half), [psb[pi]], kc == 7)
                    act(m_kT[:, h, hs], ps(pi), AF.Copy, [psb[pi]], [bf(f"mkT{h}_{half}")])
            for blk in range(NBLK):
                bs = slice(blk * 128, (blk + 1) * 128)
                pi = next_ps(ALLB)
                for kc in range(8):
                    mm(ps(pi), xb[:, kc, bs], sv[:, kc, :], kc == 0, kc == 7,
                       [sbuf_] + xb_reads(blk // 4), [psb[pi]], kc == 7)
                tt("dve", m_kw[:, blk, :].rearrange("p (h n) -> p h n", h=4),
                   ps(pi).rearrange("p (h n) -> p h n", h=4),
                   ges[:, blk, :].unsqueeze(2).broadcast_to([128, 4, 128]), ALU.mult,
                   [psb[pi], bf(f"ges{blk // 4}")], [bf(f"mkw{blk}")])
            slot, sbuf_ = w_next(0)
            sv = slot[:, :].rearrange("p (k n) -> p k n", k=8)
            for h in range(4):
                for half in range(NHALF):
                    hs = slice(half * 512, (half + 1) * 512)
                    pi = next_ps(ALLB)
                    for kc in range(8):
                        mm(ps(pi), sv[:, kc, h * 128:(h + 1) * 128], xb[:, kc, hs], kc == 0, kc == 7,
                           [sbuf_] + xb_reads(half), [psb[pi]], kc == 7)
                    act(m_qT[:, h, hs], ps(pi), AF.Copy, [psb[pi]], [bf(f"mqT{h}_{half}")], scale=128.0 ** -0.5)
            for s in range(2):
                slot, sbuf_ = w_next(2 + s)
                sv = slot[:, :].rearrange("p (k n) -> p k n", k=8)
                for blk in range(NBLK):
                    bs = slice(blk * 128, (blk + 1) * 128)
                    pi = next_ps(ALLB)
                    for kc in range(8):
                        mm(ps(pi), xb[:, kc, bs], sv[:, kc, :], kc == 0, kc == 7,
                           [sbuf_] + xb_reads(blk // 4), [psb[pi]], kc == 7)
                    S.emit("dve" if blk % 2 else "act",
                           (lambda e, o=m_va[:, blk, 2 * s:2 * s + 2, 0:256], i=ps(pi).rearrange("p (h n) -> p h n", h=2):
                            e.tensor_copy(o, i)) if blk % 2 else
                           (lambda e, o=m_va[:, blk, 2 * s:2 * s + 2, 0:256], i=ps(pi).rearrange("p (h n) -> p h n", h=2):
                            e.activation(o, i, AF.Copy)),
                           [psb[pi], AT], [bf(f"mva{blk}_{s}")], cost=680.0)
            for s in range(2):
                slot, sbuf_ = w_next(4 + s)
                sv = slot[:, :].rearrange("p (k n) -> p k n", k=8)
                for blk in range(NBLK):
                    bs = slice(blk * 128, (blk + 1) * 128)
                    pi = next_ps(ALLB)
                    for kc in range(8):
                        mm(ps(pi), xb[:, kc, bs], sv[:, kc, :], kc == 0, kc == 7,
                           [sbuf_] + xb_reads(blk // 4), [psb[pi]], kc == 7)
                    act(m_sgo[:, blk, s * 512:(s + 1) * 512], ps(pi), AF.Sigmoid, [psb[pi]], [bf(f"msgo{blk}_{s}")])
            S.phase = "l0.recur"
            rot1 = [5, 6]
            for blk in range(NBLK):
                bs = slice(blk * 128, (blk + 1) * 128)
                half = blk // 4
                q = blk % 2
                pn = [ps_all[:, 2 * q + h // 2, (h % 2) * 256:(h % 2) * 256 + 256] for h in range(4)]
                pnb = [psb[2 * q + h // 2] for h in range(4)]
                pd = ps_all[:, 4, q * 4:q * 4 + 4]
                for h in range(4):
                    k = blk * 4 + h
                    pS = next_ps(rot1)
                    mm(ps(pS)[:, 0:128], m_kT[:, h, bs], m_qT[:, h, bs], True, True,
                       [bf(f"mkT{h}_{half}"), bf(f"mqT{h}_{half}")], [psb[pS]], False)
                    mm(ps(pS)[:, 128:385], m_kw[:, blk, h * 128:(h + 1) * 128], m_va[:, blk, h, 0:257], True, True,
                       [bf(f"mkw{blk}"), bf(f"mva{blk}_{h // 2}"), bf("mva_ones")], [psb[pS]], True)
                    ptm = PTm[k % 3]
                    ptb_ = bf(f"PTm{k % 3}")
                    stt("dve", ptm[:, :], ps(pS)[:, 0:128], ges[:, blk, h:h + 1], maskmb[:, :], ALU.mult, ALU.mult,
                        [psb[pS], bf(f"ges{half}"), bf("maskmb")], [ptb_])
                    cbt = Cb[k % 3]
                    cbb = bf(f"Cb{k % 3}")
                    act(cbt[:, :], Cs[:, h, :], AF.Copy, [bf(f"Cs{h}"), bf("Cs"), bf(f"gfr{half}")], [cbb],
                        scale=gfr[:, 1, blk, h:h + 1])
                    vb_ = [bf(f"mva{blk}_{h // 2}"), bf("mva_ones")]
                    mm(pn[h], ptm[:, :], m_va[:, blk, h, 0:256], True, False, [ptb_] + vb_, [pnb[h]], False)
                    mm(pn[h], m_qT[:, h, bs], cbt[:, 0:256], False, True, [bf(f"mqT{h}_{half}"), cbb], [pnb[h]], False)
                    mm(pd[:, h:h + 1], ptm[:, :], m_va[:, blk, h, 256:257], True, False, [ptb_] + vb_, [psb[4]], False)
                    mm(pd[:, h:h + 1], m_qT[:, h, bs], cbt[:, 256:257], False, True, [bf(f"mqT{h}_{half}"), cbb], [psb[4]], True)
                    stt("dve", Cs[:, h, :], Cs[:, h, :], gfr[:, 1, blk, h:h + 1], ps(pS)[:, 128:385], ALU.mult, ALU.add,
                        [bf(f"Cs{h}"), bf("Cs"), bf(f"gfr{half}"), psb[pS]], [bf(f"Cs{h}")])
                sm = nsm[q]
                smb = bf(f"nsm{q}")
                act(sm[:, 0, :], pd, AF.Abs, [psb[4]], [smb])
                tt("dve", sm[:, 0, :], sm[:, 0, :], gfr[:, 0, blk, :], ALU.max, [smb, bf(f"gfr{half}")], [smb])
                S.emit("dve", lambda e, o=sm[:, 1, :], i=sm[:, 0, :]: e.reciprocal(o, i), [smb], [smb])
                for h in range(4):
                    S.emit("dve", lambda e, o=nst[q][:, h, :], i=pn[h]: e.bn_stats(o, i), [pnb[h]], [bf(f"nst{q}")], cost=340.0)
                for h in range(4):
                    S.emit("dve", lambda e, o=nmv[q][:, h, :], i=nst[q][:, h, :]: e.bn_aggr(o, i), [bf(f"nst{q}")], [bf(f"nmv{q}")])
                tt("dve", sm[:, 2, :], sm[:, 1, :], sm[:, 1, :], ALU.mult, [smb], [smb])
                tt("dve", sm[:, 2, :].unsqueeze(2), sm[:, 2, :].unsqueeze(2), nmv[q][:, :, 1:2], ALU.mult,
                   [smb, bf(f"nmv{q}")], [smb])
                act(sm[:, 2, :], sm[:, 2, :], AF.Sqrt, [smb, bf("epsc")], [smb], bias=epsc[:, 0:1])
                S.emit("dve", lambda e, o=sm[:, 2, :]: e.reciprocal(o, o), [smb], [smb])
                tt("dve", sm[:, 3, :], sm[:, 2, :], sm[:, 1, :], ALU.mult, [smb], [smb])
                stt("dve", sm[:, 4, :].unsqueeze(2), nmv[q][:, :, 0:1], -1.0, sm[:, 3, :].unsqueeze(2), ALU.mult, ALU.mult,
                    [smb, bf(f"nmv{q}")], [smb])
                hnb = bf(f"hn{q}")
                for h in range(4):
                    act(hn[q][:, h, :], pn[h], AF.Identity, [pnb[h], smb], [hnb],
                        bias=sm[:, 4, h:h + 1], scale=sm[:, 3, h:h + 1])
                yb = bf(f"ytok{q}")
                tt("pool", ytok[q][:, :], hn[q][:, :, :].rearrange("p h n -> p (h n)"), m_sgo[:, blk, :], ALU.mult,
                   [hnb, bf(f"msgo{blk}_0"), bf(f"msgo{blk}_1")], [yb])
                pT_ = 7
                ptv = ps(pT_).bitcast(BF16).rearrange("p (c t) -> p c t", c=8)
                for c in range(8):
                    tr(ptv[:, c, :], ytok[q][:, c * 128:(c + 1) * 128], [yb], [psb[pT_]], c == 7)
                act(yT[:, :, bs], ptv, AF.Copy, [psb[pT_]], xb_reads(blk // 4))
            S.phase = "l0.outproj_ln"
            wst = {}

            def out_group(oc, half, pi):
                si = oc // 4
                if si not in wst:
                    slot, sbuf_ = w_next(6 + si, hold=True)
                    sv = slot[:, :].rearrange("p (k n) -> p k n", k=8)
                    tt("pool", sv, sv, cc("hg").unsqueeze(2).broadcast_to([128, 8, 512]), ALU.mult, [sbuf_, cb], [sbuf_])
                    wst[si] = (sv, sbuf_)
                sv, sbuf_ = wst[si]
                j = oc % 4
                for kc in range(8):
                    mm(ps(pi), sv[:, kc, j * 128:(j + 1) * 128], yT[:, kc, half * 512:(half + 1) * 512], kc == 0, kc == 7,
                       [sbuf_] + xb_reads(half), [psb[pi]], kc == 7)

            ln_phase(0, "mix", (out_group, None), ORDER_HALF_MAJOR)
            w_release_held()
            mlp_ple(0, it, 0, last=(nlayers == 1))

        def rope_evac(pi, dst, dstbuf, biasap, sc, half, k):
            hs = slice(half * 512, (half + 1) * 512)
            q32 = a_q32[k % 2]
            qb_ = a_qb[k % 2]
            ra = a_ra[k % 2]
            rb = a_rb[k % 2]
            b32, bqb, bra, brb = bf(f"aq32{k % 2}"), bf(f"aqb{k % 2}"), bf(f"ara{k % 2}"), bf(f"arb{k % 2}")
            act(q32, ps(pi), AF.Identity, [psb[pi], cb, bf("bq8")], [b32], bias=biasap, scale=sc)
            act(qb_, ps(pi), AF.Identity, [psb[pi], cb, bf("bq8")], [bqb], bias=biasap, scale=sc)
            psw = next_ps([4, 5, 6, 7])
            mm(ps(psw), pmb[:, :], qb_, True, True, [bf("pmb"), bqb], [psb[psw]], True)
            tt("pool", ra, q32, a_cos[:, hs], ALU.mult, [b32, bf("acos")], [bra])
            tt("dve", rb, ps(psw), a_sin[:, hs], ALU.mult, [psb[psw], bf("asin")], [brb])
            tt("dve", dst, ra, rb, ALU.add, [bra, brb], [dstbuf])

        def layer1(it):
            t0 = it * TT
            gb0 = it * NBLK
            S.phase = "l1.rope"
            arena_gen()
            for half in range(NHALF):
                for oc in range(8):
                    hs = slice(half * 512, (half + 1) * 512)
                    act(xT32[:, oc, hs], xT32[:, oc, hs], AF.Identity, [bf(f"x{oc}_{half}"), cb], [bf(f"x{oc}_{half}")],
                        bias=cc("bo", oc, 1), scale=ALPHA)
            S.dma("sp", pos_sem, posi[:, :, :], pos_d[0:1, t0:t0 + TT].partition_broadcast(128), writes=[bf("posi")], nbytes=128 * TT * 4)
            cp("dve", a_posf, posi[:, 0, :], [bf("posi")], [bf("aposf")])
            C1 = 6.28125
            C2 = 2.0 * PI - C1
            ab, ub = bf("aang"), bf("au")
            ts("dve", a_ang, a_posf, cc("invf"), None, ALU.mult, None, [bf("aposf"), cb], [ab])
            ts("dve", a_u, a_ang, 1.0 / (2.0 * PI), None, ALU.mult, None, [ab], [ub])
            cp("dve", a_ki, a_u, [ub], [bf("aki")])
            cp("dve", a_u, a_ki, [bf("aki")], [ub])
            stt("dve", a_ang, a_u, -C1, a_ang, ALU.mult, ALU.add, [ub, ab], [ab])
            stt("dve", a_ang, a_u, -C2, a_ang, ALU.mult, ALU.add, [ub, ab], [ab])
            ts("dve", a_u, a_ang, PI, 2.0 * PI, ALU.is_gt, ALU.mult, [ab], [ub])
            tt("dve", a_ang, a_ang, a_u, ALU.subtract, [ab, ub], [ab])
            act(a_sin, a_ang, AF.Sin, [ab], [bf("asin")])
            ts("dve", a_ang, a_ang, 0.5 * PI, None, ALU.add, None, [ab], [ab])
            ts("dve", a_u, a_ang, PI, 2.0 * PI, ALU.is_gt, ALU.mult, [ab], [ub])
            tt("dve", a_ang, a_ang, a_u, ALU.subtract, [ab, ub], [ab])
            act(a_cos, a_ang, AF.Sin, [ab], [bf("acos")])
            ts("dve", a_sin, a_sin, cc("sgn"), None, ALU.mult, None, [bf("asin"), cb], [bf("asin")])
            S.phase = "l1.proj"
            slot, sbuf_ = w_next(27)
            sv = slot[:, :].rearrange("p (k n) -> p k n", k=8)
            kk = 0
            for c in range(2):
                for half in range(NHALF):
                    hs = slice(half * 512, (half + 1) * 512)
                    pi = next_ps([0, 1, 2, 3])
                    for kc in range(8):
                        mm(ps(pi), sv[:, kc, c * 128:(c + 1) * 128], xb[:, kc, hs], kc == 0, kc == 7,
                           [sbuf_] + xb_reads(half), [psb[pi]], kc == 7)
                    rope_evac(pi, kTa[:, c, 128 + half * 512:128 + (half + 1) * 512], bf(f"kTa{c}_{half}"),
                              cc("bk", c, 1), 1.0, half, kk)
                    kk += 1
            for blk in range(NBLK):
                bs = slice(blk * 128, (blk + 1) * 128)
                pi = next_ps([0, 1, 2, 3])
                for kc in range(8):
                    mm(ps(pi)[:, 0:256], xb[:, kc, bs], sv[:, kc, 256:512], kc == 0, kc == 7,
                       [sbuf_] + xb_reads(blk // 4), [psb[pi]], kc == 7)
                tt("dve", va[:, blk + 1, :, 0:64], ps(pi)[:, 0:256].rearrange("p (h d) -> p h d", h=4),
                   cc("vb").rearrange("p (h d) -> p h d", h=4), ALU.add, [psb[pi], cb, bf("va_all")], [bf(f"va{blk + 1}")])
            for s in range(2):
                slot, sbuf_ = w_next(28 + s)
                sv = slot[:, :].rearrange("p (k n) -> p k n", k=8)
                for jj in range(4):
                    j = s * 4 + jj
                    for half in range(NHALF):
                        hs = slice(half * 512, (half + 1) * 512)
                        pi = next_ps([0, 1, 2, 3])
                        for kc in range(8):
                            mm(ps(pi), sv[:, kc, jj * 128:(jj + 1) * 128], xb[:, kc, hs], kc == 0, kc == 7,
                               [sbuf_] + xb_reads(half), [psb[pi]], kc == 7)
                        rope_evac(pi, a_qT[:, j, hs], bf(f"aqT{j}_{half}"), bq8[:, j:j + 1], 0.125, half, kk)
                        kk += 1
            dump("cos", a_cos, [bf("acos")])
            dump("sin", a_sin, [bf("asin")])
            dump("qT", a_qT[:, :, 0:256], [bf(f"aqT{j}_0") for j in range(8)])
            dump("kT", kTa[:, :, 0:384], [bf(f"kTa{c}_0") for c in range(2)] + [bf("kTa0")])
            dump("va", va[:, 0:3, :, :], [bf("va1"), bf("va2"), bf("va_all")])
            dump("xin", xT32[:, :, 0:128], [bf(f"x{oc}_0") for oc in range(8)])
            S.phase = "l1.attn"
            rot1 = [6, 7]
            for blk in range(NBLK):
                bs = slice(blk * 128, (blk + 1) * 128)
                half = blk // 4
                first = (gb0 + blk == 0)
                q2 = blk % 2
                atok = ytok[q2]
                atb = bf(f"ytok{q2}")
                for kv in range(4):
                    k = blk * 4 + kv
                    off = (kv % 2) * 64
                    kch = kv // 2
                    sbank = (k % 3) * 2
                    pSv = ps_all[:, sbank:sbank + 2, :].rearrange("p a (g s) -> p (a g) s", g=2)
                    kreads = [bf(f"kTa{kch}_{h_}") for h_ in range(NHALF)] + [bf("kTa0"), bf("kTaprev")]
                    mbias = maskbb[1 if first else 0]
                    for pb in range(2):
                        mm(ps(sbank + pb), identb[:, :], mbias[:, :, :].rearrange("p a s -> p (a s)"), True, False,
                           [bf("identb"), bf("maskbb")], [psb[sbank + pb]], False)
                    for g in range(4):
                        j = kch * 4 + g
                        mm(pSv[:, g, :], a_qT[off:off + 64, j, bs], kTa[off:off + 64, kch, blk * 128:blk * 128 + 256],
                           False, g % 2 == 1, [bf(f"aqT{j}_{half}")] + kreads, [psb[sbank + g // 2]], g == 3)
                    sm = asm[k % 3]
                    smb = bf(f"asm{k % 3}")
                    sb2 = [psb[sbank], psb[sbank + 1]]
                    S.emit("dve", lambda e, o=sm[:, 0, 0:1], i=ps_all[:, sbank:sbank + 2, :]: e.tensor_reduce(o, i, AX.XY, ALU.max),
                           sb2, [smb], cost=1150.0)
                    ts("dve", sm[:, 1, 0:1], sm[:, 0, 0:1], sinkmax[:, kv:kv + 1], -1.0, ALU.max, ALU.mult, [smb, bf("sinkmax")], [smb])
                    pe_ = Pexp[k % 3]
                    peb = bf(PEXP_B[k % 3])
                    act(pe_, pSv, AF.Exp, sb2 + [smb], [peb], bias=sm[:, 1, 0:1])
                    pT_ = next_ps(rot1)
                    ptv = ps(pT_).bitcast(BF16).rearrange("p (c t) -> p c t", c=8)
                    for g in range(4):
                        for jj in range(2):
                            tr(ptv[:, g * 2 + jj, :], pe_[:, g, jj * 128:(jj + 1) * 128], [peb], [psb[pT_]],
                               g == 3 and jj == 1)
                    pta = PTa[k % 3]
                    ptab = bf(PTA_B[k % 3])
                    if k % 4 == 3:
                        cp("dve", pta, ptv, [psb[pT_]], [ptab])
                    else:
                        act(pta, ptv, AF.Copy, [psb[pT_]], [ptab])
                    pO = next_ps(rot1)
                    pOv = ps(pO).rearrange("p (g n) -> p g n", g=4)
                    for g in range(4):
                        if not first:
                            mm(pOv[:, g, 0:65], pta[:, g * 2, :], va[:, blk, kv, :], True, False,
                               [ptab, bf(f"va{blk}"), bf("va_all")], [psb[pO]], False)
                        mm(pOv[:, g, 0:65], pta[:, g * 2 + 1, :], va[:, blk + 1, kv, :], first, True,
                           [ptab, bf(f"va{blk + 1}"), bf("va_all")], [psb[pO]], g == 3)
                    act(sm[:, 3, :], cc("sink", kv * 4, 4), AF.Exp, [smb, cb], [smb], bias=sm[:, 1, 0:1])
                    tt("dve", sm[:, 4, :].unsqueeze(2), pOv[:, :, 64:65], sm[:, 3, :].unsqueeze(2), ALU.add,
                       [psb[pO], smb], [smb])
                    S.emit("dve", lambda e, o=sm[:, 5, :], i=sm[:, 4, :]: e.reciprocal(o, i), [smb], [smb])
                    tt("dve", atok[:, kv * 256:(kv + 1) * 256].rearrange("p (g d) -> p g d", g=4), pOv[:, :, 0:64],
                       sm[:, 5, :].unsqueeze(2).broadcast_to([128, 4, 64]), ALU.mult, [psb[pO], smb], [atb])
                dump(f"atok{blk}", atok[:, :], [atb])
                pT_ = next_ps(rot1)
                ptv = ps(pT_).bitcast(BF16).rearrange("p (c t) -> p c t", c=8)
                for c in range(8):
                    tr(ptv[:, c, :], atok[:, c * 128:(c + 1) * 128], [atb], [psb[pT_]], c == 7)
                act(yT[:, :, bs], ptv, AF.Copy, [psb[pT_]], xb_reads(blk // 4))
            cp("dve", kTa[:, :, 0:128], kTa[:, :, TT:TT + 128],
               [bf(f"kTa{c}_{NHALF - 1}") for c in range(2)] + [bf("kTa0")], [bf("kTaprev"), bf("kTa0")])
            cp("dve", va[:, 0, :, :], va[:, NBLK, :, :], [bf(f"va{NBLK}"), bf("va_all")], [bf("va0")])
            S.phase = "l1.oproj_ln"
            wst = {}

            def o_group(oc, half, pi):
                si = oc // 4
                if si not in wst:
                    slot, sbuf_ = w_next(30 + si, hold=True)
                    wst[si] = (slot[:, :].rearrange("p (k n) -> p k n", k=8), sbuf_)
                sv, sbuf_ = wst[si]
                j = oc % 4
                for kc in range(8):
                    mm(ps(pi), sv[:, kc, j * 128:(j + 1) * 128], yT[:, kc, half * 512:(half + 1) * 512], kc == 0, kc == 7,
                       [sbuf_] + xb_reads(half), [psb[pi]], kc == 7)

            ln_phase(1, "mix", (o_group, "pre"), ORDER_HALF_MAJOR)
            w_release_held()
            mlp_ple(1, it, 24, last=True)

        for it in range(ntiles):
            layer0(it)
            if nlayers > 1:
                layer1(it)
            t0 = it * TT
            for oc in range(8):
                S.dma("sp", out_sems[oc], out_d[oc * 128:(oc + 1) * 128, t0:t0 + TT], xo[:, oc, :], nbytes=128 * TT * 4,
                      reads=[bf(f"xo{oc}"), AT], writes=[bf(f"out{it}_{oc}")])
        S.final_waits = list(out_sems) + ([dbg_sem] if dbg else [])
        S.schedule()
        nc.sched = S
        nc.dbg_map = dbg_map
        with nc.Block() as block:
            @block.tensor
            def _(e):
                S.run("pe", e)

            @block.scalar
            def _(e):
                S.run("act", e)

            @block.vector
            def _(e):
                S.run("dve", e)

            @block.gpsimd
            def _(e):
                S.run("pool", e)

            @block.sync
            def _(e):
                S.run("sp", e)
    return nc


def _slab_std(W):
    K, N = W.shape
    a = W.reshape(K // 128, 128, N // 512, 512).transpose(2, 1, 0, 3)
    a = a.reshape(N // 512, 128, (K // 128) * 512)
    if a.shape[2] < SLAB:
        a = np.concatenate([a, np.zeros((a.shape[0], 128, SLAB - a.shape[2]), np.float32)], axis=2)
    return a


def _slab_down(W):
    return W.reshape(32, 128, 8, 128).transpose(2, 1, 0, 3).reshape(8, 128, SLAB)


def _slab_proj(W):
    a = W.reshape(2, 128, 1024).transpose(1, 0, 2).reshape(1, 128, 2048)
    return np.concatenate([a, np.zeros((1, 128, SLAB - 2048), np.float32)], axis=2)


def _pk(v):
    return np.ascontiguousarray(v.reshape(-1, 128).T)


_QPERM = None


def _qperm():
    cols = []
    for j in range(8):
        kp, g = j // 4, j % 4
        for kv in (kp * 2, kp * 2 + 1):
            h = kv * 4 + g
            cols.extend(range(h * 64, h * 64 + 64))
    return np.array(cols)


def prep_shared(inp):
    f = np.float32
    qp = _qperm()
    slabs = []
    w_in = inp["a_w_in"][0]
    slabs.append(_slab_std(w_in[:, 0:3072]))
    slabs.append(_slab_std(inp["a_w_out"][0]))
    slabs.append(_slab_std(inp["mlp_w_up"][0]))
    slabs.append(_slab_down(inp["mlp_w_down"][0]))
    slabs.append(_slab_std(inp["ple_w_gate"][0]))
    slabs.append(_slab_proj(inp["ple_w_proj"][0]))
    slabs.append(_slab_std(inp["kv_w"]))
    slabs.append(_slab_std(inp["b_w_q"][0][:, qp]))
    slabs.append(_slab_std(inp["b_w_o"][0]))
    slabs.append(_slab_std(inp["mlp_w_up"][1]))
    slabs.append(_slab_down(inp["mlp_w_down"][1]))
    slabs.append(_slab_std(inp["ple_w_gate"][1]))
    slabs.append(_slab_proj(inp["ple_w_proj"][1]))
    ws = np.ascontiguousarray(np.concatenate(slabs, axis=0).astype(f))
    assert ws.shape == (NS, 128, SLAB), ws.shape

    cst = np.zeros((128, NCST), f)

    def put(name, arr):
        o, n = _c[name]
        cst[:, o:o + n] = arr

    for l in range(2):
        put(f"mixg{l}", _pk(inp["mix_ln_g"][l]))
        put(f"mixb{l}", _pk(inp["mix_ln_b"][l]))
        put(f"mlpg{l}", _pk(inp["mlp_ln_g"][l]))
        put(f"mlpb{l}", _pk(inp["mlp_ln_b"][l]))
        put(f"pleb{l}", _pk(inp["ple_b_gate"][l]))
    put("bo", _pk(inp["b_b_o"][0]))
    put("hg", _pk(inp["a_head_norm_g"][0]))
    put("bq", _pk(inp["b_b_q"][0][qp]))
    put("bk", _pk(inp["kv_b"][0:256]))
    p = np.arange(128)
    d = p % 64
    half = 8
    invf_tab = np.power(np.float32(500000.0), -np.arange(half, dtype=f) * f(2.0 / 16)).astype(f)
    invf = np.where(d < 16, invf_tab[d % 8], 0.0).astype(f)
    sgn = np.where(d < 8, -1.0, np.where(d < 16, 1.0, 0.0)).astype(f)
    put("invf", invf[:, None])
    put("sgn", sgn[:, None])
    gbias = np.concatenate([inp["a_b_igate"][0], inp["a_b_fgate"][0]]).astype(f)
    put("gb", np.broadcast_to(np.tile(gbias, 8)[None, :], (128, 64)))
    put("vb", np.broadcast_to(inp["kv_b"][256:512][None, :], (128, 256)))
    put("sink", np.broadcast_to(inp["b_sinks"][0][None, :], (128, 16)))
    s = np.arange(128)[:, None]
    t = np.arange(128)[None, :]
    put("ustrict", (s > t).astype(f))
    put("ones", np.ones((128, 128), f))
    put("ident", np.eye(128, dtype=f))
    put("maskm", (s <= t).astype(f))
    NEG = f(-30000.0)
    tq = np.arange(128)[:, None]
    sk = np.arange(128)[None, :]
    prev_ok = sk > tq
    cur_ok = sk <= tq
    put("maskb", np.concatenate([np.where(prev_ok, f(0), NEG), np.where(cur_ok, f(0), NEG)], axis=1).astype(f))
    put("maskbf", np.concatenate([np.full((128, 128), NEG, f), np.where(cur_ok, f(0), NEG)], axis=1).astype(f))
    pm = np.zeros((128, 128), f)
    for m in range(128):
        dm = m % 64
        if dm < 8:
            pm[m + 8, m] = 1.0
        elif dm < 16:
            pm[m - 8, m] = 1.0
    put("pm", pm)
    wg = w_in[:, 3072:3080].reshape(8, 128, 8).transpose(1, 0, 2).reshape(128, 64)
    put("wg", wg)
    return ws, cst


def kernel(**inputs):
    inp = {k: np.asarray(v) for k, v in inputs.items()}
    x = inp["x"]
    Bn, SEQL, _ = x.shape
    ntiles = SEQL // TT
    ws, cst = prep_shared(inp)
    nc = build(ntiles=ntiles, nlayers=2)
    in_maps = []
    for b in range(Bn):
        in_maps.append({
            "xT": np.ascontiguousarray(x[b].T),
            "pT": np.ascontiguousarray(inp["p"][:, b].transpose(0, 2, 1)),
            "pos": np.ascontiguousarray(inp["positions"][b][None, :].astype(np.int32)),
            "wslab": ws,
            "cst": cst,
        })
    res = run_bass_kernel_spmd(nc, in_maps, core_ids=list(range(Bn)))
    out = np.stack([np.ascontiguousarray(r["outT"].T) for r in res.results], axis=0)
    return out.astype(np.float32)
```

```python
import math
from contextlib import ExitStack

import numpy as np
import concourse.bass as bass
import concourse.mybir as mybir
from concourse.bass_utils import run_bass_kernel_spmd

F32 = mybir.dt.float32
BF16 = mybir.dt.bfloat16
I32 = mybir.dt.int32
ALU = mybir.AluOpType
AF = mybir.ActivationFunctionType
AX = mybir.AxisListType

D = 1024
TT = 1024
NBLK = TT // 128
NHALF = TT // 512
NW = 4
SLAB = 4096
ALPHA = float((2 * 2) ** 0.25)
LN_EPS = 1e-5
NS = 51
PI = math.pi

_c = {}
_off = 0


def _add(name, n):
    global _off
    _c[name] = (_off, n)
    _off += n


for _l in range(2):
    for _n in ("mixg", "mixb", "mlpg", "mlpb", "pleb"):
        _add(f"{_n}{_l}", 8)
_add("bo", 8)
_add("hg", 8)
_add("bq", 8)
_add("bk", 2)
_add("invf", 1)
_add("sgn", 1)
_add("gb", 64)
_add("vb", 256)
_add("sink", 16)
_add("ustrict", 128)
_add("ones", 128)
_add("ident", 128)
_add("maskm", 128)
_add("maskb", 256)
_add("maskbf", 256)
_add("pm", 128)
_add("wg", 64)
NCST = _off


class Buf:
    __slots__ = ("name", "lw", "rd", "const", "excl")

    def __init__(self, name, const=False, excl=False):
        self.name = name
        self.lw = None
        self.rd = []
        self.const = const
        self.excl = excl


class Op:
    __slots__ = ("id", "eng", "fns", "cost", "deps", "kind", "dsem", "nbytes", "tag", "ms", "xlat")

    def __init__(self, id, eng, fns, cost, deps, kind, dsem, nbytes, tag):
        self.id = id
        self.eng = eng
        self.fns = fns
        self.cost = cost
        self.deps = deps
        self.kind = kind
        self.dsem = dsem
        self.nbytes = nbytes
        self.tag = tag
        self.ms = None
        self.xlat = 0.0


class Sched:
    WINDOW = 128
    LAT_X = 140.0
    LAT_S = 50.0
    DMA_BW = 170.0
    DMA_LAT = 2000.0

    def __init__(self, nc, es):
        self.nc = nc
        self.es = es
        self.sems = {}
        self.engs = ("pe", "act", "dve", "pool", "sp")
        for n in self.engs:
            self.sems[n] = es.enter_context(nc.semaphore("s_" + n))
        self.ops = []
        self.pending = None
        self.phase = "init"
        self.final_waits = []

    def dsem(self, name):
        self.sems[name] = self.es.enter_context(self.nc.semaphore("d_" + name))
        return name

    def _collect(self, reads, writes):
        deps = set()
        for b in reads:
            if b.lw is not None:
                deps.add(b.lw)
            if b.excl:
                deps.update(b.rd)
        for b in writes:
            if b.lw is not None:
                deps.add(b.lw)
            deps.update(b.rd)
        return deps

    def _commit(self, oid, reads, writes):
        for b in writes:
            b.lw = oid
            b.rd = []
        for b in reads:
            if b.excl:
                b.lw = oid
                b.rd = []
            elif not b.const:
                b.rd.append(oid)

    def emit(self, eng, fn, reads=(), writes=(), inc=True, cost=100.0):
        if eng == "pe":
            if self.pending is None:
                self.pending = Op(len(self.ops), "pe", [], 0.0, set(), "c", None, 0, self.phase)
                self.ops.append(self.pending)
            op = self.pending
            op.fns.append(fn)
            op.cost += cost
            d = self._collect(reads, writes)
            d.discard(op.id)
            op.deps |= d
            self._commit(op.id, reads, writes)
            if inc:
                self.pending = None
            return
        assert self.pending is None or True
        oid = len(self.ops)
        op = Op(oid, eng, [fn], cost, self._collect(reads, writes), "c", None, 0, self.phase)
        self.ops.append(op)
        self._commit(oid, reads, writes)

    def dma(self, q, ds, out, in_, reads=(), writes=(), nbytes=0, xlat=0.0):
        oid = len(self.ops)
        op = Op(oid, q, [lambda e: e.dma_start(out=out, in_=in_)], 60.0 if q == "sp" else 900.0,
                self._collect(reads, writes), "d", ds, nbytes, self.phase)
        op.xlat = xlat
        self.ops.append(op)
        self._commit(oid, reads, writes)

    def schedule(self):
        assert self.pending is None, "open PE accumulation group"
        ops = self.ops
        n = len(ops)
        elist = {e: [] for e in self.engs}
        for op in ops:
            elist[op.eng].append(op.id)
        pos = {e: 0 for e in self.engs}
        done = [False] * n
        fin = [0.0] * n
        free = {e: 0.0 for e in self.engs}
        dma_free = 0.0
        order = {e: [] for e in self.engs}
        left = n
        W = self.WINDOW
        while left:
            best = None
            for e in self.engs:
                lst = elist[e]
                p = pos[e]
                while p < len(lst) and done[lst[p]]:
                    p += 1
                pos[e] = p
                cnt = 0
                i = p
                fe = free[e]
                cand = None
                while i < len(lst) and cnt < W:
                    oid = lst[i]
                    i += 1
                    if done[oid]:
                        continue
                    cnt += 1
                    op = ops[oid]
                    st = fe
                    ok = True
                    for d in op.deps:
                        if not done[d]:
                            ok = False
                            break
                        de = ops[d].eng
                        if ops[d].kind == "d":
                            t = fin[d] + self.LAT_X
                        elif de == e:
                            t = fin[d] + (0.0 if e == "pe" else self.LAT_S)
                        else:
                            t = fin[d] + self.LAT_X
                        if t > st:
                            st = t
                    if not ok:
                        continue
                    if cand is None or st < cand[0]:
                        cand = (st, oid)
                        if st <= fe:
                            break
                if cand is not None and (best is None or cand < best):
                    best = cand
            assert best is not None, "scheduler deadlock"
            st, oid = best
            op = ops[oid]
            e = op.eng
            done[oid] = True
            left -= 1
            order[e].append(oid)
            if op.kind == "d":
                free[e] = st + op.cost
                ds_ = max(free[e], dma_free)
                dma_free = ds_ + op.nbytes / self.DMA_BW
                fin[oid] = dma_free + self.DMA_LAT + op.xlat
            else:
                free[e] = st + op.cost
                fin[oid] = free[e]
        self.order = order
        self.fin = fin
        self.est_ns = max(fin) if fin else 0.0
        cnt = {k: 0 for k in self.sems}
        for e in self.engs:
            for oid in order[e]:
                op = ops[oid]
                if op.kind == "d":
                    cnt[op.dsem] += 16
                    op.ms = (op.dsem, cnt[op.dsem])
                else:
                    cnt[e] += 1
                    op.ms = (e, cnt[e])
        self.final_counts = cnt

    def run(self, eng, e):
        ops = self.ops
        seen = {}
        for oid in self.order[eng]:
            op = ops[oid]
            need = {}
            for d in op.deps:
                sn, idx = ops[d].ms
                if sn == "pe" and eng == "pe":
                    continue
                if need.get(sn, 0) < idx:
                    need[sn] = idx
            for sn, idx in need.items():
                if seen.get(sn, 0) >= idx:
                    continue
                seen[sn] = idx
                e.wait_ge(self.sems[sn], idx)
            last = None
            for fn in op.fns:
                last = fn(e)
            last.then_inc(self.sems[op.ms[0]], 16 if op.kind == "d" else 1)
        if eng == "sp":
            for sn in self.final_waits:
                if self.final_counts[sn] > 0:
                    e.wait_ge(self.sems[sn], self.final_counts[sn])


def build(ntiles=4, nlayers=2, dbg=()):
    nc = bass.Bass("TRN2", target_bir_lowering=False)
    SEQL = ntiles * TT
    xT_d = nc.dram_tensor("xT", [D, SEQL], F32, kind="ExternalInput").ap()
    pT_d = nc.dram_tensor("pT", [2, 256, SEQL], F32, kind="ExternalInput").ap()
    pos_d = nc.dram_tensor("pos", [1, SEQL], I32, kind="ExternalInput").ap()
    ws_d = nc.dram_tensor("wslab", [NS, 128, SLAB], F32, kind="ExternalInput").ap()
    cst_d = nc.dram_tensor("cst", [128, NCST], F32, kind="ExternalInput").ap()
    out_d = nc.dram_tensor("outT", [D, SEQL], F32, kind="ExternalOutput").ap()

    es = ExitStack()
    with es:
        S = Sched(nc, es)

        def sb(name, shape, dt):
            return es.enter_context(nc.sbuf_tensor(name, shape, dt))

        xT32 = sb("xT32", [128, 8, TT], F32)
        xb = sb("xb", [128, 8, TT], BF16)
        yT = xb
        arena = sb("arena", [128, 32768], BF16)
        ring = [sb(f"ring{i}", [128, SLAB], BF16) for i in range(NW)]
        cst = sb("cst_sb", [128, NCST], F32)
        zst = [sb(f"zst{i}", [128, 2, 512], BF16) for i in range(4)]
        lnmean = sb("lnmean", [128, TT], F32)
        lnrstd = sb("lnrstd", [128, TT], F32)
        lntmp = [sb(f"lntmp{i}", [128, 512], F32) for i in range(2)]
        ptb = sb("ptb_sb", [128, 2, TT], BF16)
        identb = sb("identb", [128, 128], BF16)
        onesb = sb("onesb", [128, 128], BF16)
        maskmb = sb("maskmb", [128, 128], BF16)
        maskbb = [sb(f"maskbb{i}", [128, 2, 256], BF16) for i in range(2)]
        sinkmax = sb("sinkmax", [128, 4], F32)
        pmb = sb("pmb", [128, 128], BF16)
        wgb = sb("wgb", [128, 8, 8], BF16)
        bq8 = sb("bq8", [128, 8], F32)
        epsc = sb("epsc", [128, 1], F32)
        Cs = sb("Cs", [128, 4, 257], F32)
        Cb = [sb(f"Cb{i}", [128, 257], BF16) for i in range(2)]
        PTm = [sb(f"PTm{i}", [128, 128], BF16) for i in range(2)]
        gz = sb("gz", [128, 8, 8], F32)
        gth = sb("gth", [128, 8, 8], F32)
        gef = sb("gef", [128, 8, 4], F32)
        gsp = sb("gsp", [128, 8, 4], F32)
        gei = sb("gei", [128, 8, 4], F32)
        ges = sb("ges", [128, 8, 4], F32)
        gfr = sb("gfr", [128, 2, 8, 4], F32)
        nsm = [sb(f"nsm{i}", [128, 8, 4], F32) for i in range(2)]
        nst = [sb(f"nst{i}", [128, 4, 6], F32) for i in range(2)]
        nmv = [sb(f"nmv{i}", [128, 4, 2], F32) for i in range(2)]
        hn = [sb(f"hn{i}", [128, 4, 256], BF16) for i in range(2)]
        ytok = [sb(f"ytok{i}", [128, 1024], BF16) for i in range(2)]
        kTa = sb("kTa", [128, 2, 128 + TT], BF16)
        va = sb("va", [128, NBLK + 1, 4, 65], BF16)
        posi = sb("posi", [128, 1, TT], I32)
        asm = [sb(f"asm{i}", [128, 6, 4], F32) for i in range(3)]
        ps_all = es.enter_context(nc.psum_tensor("ps", [128, 8, 512], F32))
        nc.sbuf_left = nc.sbuf_bytes_remaining

        def av(off, n):
            return arena[:, off:off + n]

        hT = arena[:, :].rearrange("p (c t) -> p c t", c=32)
        xo = arena[:, 0:16384].bitcast(F32).rearrange("p (c t) -> p c t", c=8)
        m_qT = av(0, 4096).rearrange("p (h t) -> p h t", h=4)
        m_kT = av(4096, 4096).rearrange("p (h t) -> p h t", h=4)
        m_kw = av(8192, 4096).rearrange("p (b n) -> p b n", b=8)
        m_va = av(12288, 8 * 4 * 258).rearrange("p (b h n) -> p b h n", b=8, h=4)
        m_sgo = av(20608, 8192).rearrange("p (b n) -> p b n", b=8)
        a_qT = av(0, 8192).rearrange("p (c t) -> p c t", c=8)
        a_cos = av(8192, 2048).bitcast(F32)
        a_sin = av(10240, 2048).bitcast(F32)
        a_posf = av(12288, 2048).bitcast(F32)
        a_ang = av(14336, 2048).bitcast(F32)
        a_u = av(27648, 2048).bitcast(F32)
        a_ki = av(29696, 2048).bitcast(I32)
        a_q32 = [av(16384 + i * 1024, 1024).bitcast(F32) for i in range(2)]
        a_qb = [av(18432 + i * 512, 512) for i in range(2)]
        a_ra = [av(19456 + i * 1024, 1024).bitcast(F32) for i in range(2)]
        a_rb = [av(21504 + i * 1024, 1024).bitcast(F32) for i in range(2)]
        Pexp = [av(23552 + i * 1024, 1024).rearrange("p (g s) -> p g s", g=4) for i in range(2)]
        PTa = [av(25600 + i * 1024, 1024).rearrange("p (c t) -> p c t", c=8) for i in range(2)]
        Pexp.append(av(12288, 1024).rearrange("p (g s) -> p g s", g=4))
        PTa.append(av(27648, 1024).rearrange("p (c t) -> p c t", c=8))
        PEXP_B = ["Pexp0", "Pexp1", "aposf"]
        PTA_B = ["PTa0", "PTa1", "au"]

        def cc(name, lo=0, n=None):
            o, nn = _c[name]
            if n is None:
                n = nn - lo
            return cst[:, o + lo:o + lo + n]

        B = {}

        CONST_BUFS = {"cst", "identb", "onesb", "maskmb", "maskbb", "sinkmax", "pmb", "wgb", "bq8", "epsc"}

        def bf(name):
            if name not in B:
                B[name] = Buf(name, const=name in CONST_BUFS)
            return B[name]

        AT = Buf("arena_tok")
        dummy = sb("dummy_sb", [128, 2], F32)

        def arena_gen():
            S.emit("dve", lambda e: e.memset(dummy[:, 0:1], 0.0), [], [AT], cost=60.0)

        def fsz(ap):
            n = 1
            for d in ap.shape[1:]:
                n *= d
            return n

        def ar(aps, reads):
            for a in aps:
                if getattr(a, "name", None) == "arena":
                    return list(reads) + [AT]
            return reads

        psb = [Buf(f"ps{i}", excl=True) for i in range(8)]
        ringb = [Buf(f"ring{i}") for i in range(NW)]
        ringsem = [S.dsem(f"ring{i}") for i in range(NW)]
        cst_sem = S.dsem("cst")
        x_sems = [S.dsem(f"xin{i}") for i in range(16)]
        p_sem = S.dsem("pin")
        pos_sem = S.dsem("pos")
        out_sems = [S.dsem(f"out{i}") for i in range(8)]

        def ps(i):
            return ps_all[:, i, :]

        dbg_map = {}
        if dbg:
            dbg_d = nc.dram_tensor("dbg", [128, 16384], F32, kind="ExternalOutput").ap()
            dbg_sem = S.dsem("dbg")

        def dump(tag, ap2d, bufs):
            if tag not in dbg or tag in dbg_map:
                return
            shp = list(ap2d.shape[1:])
            n = int(np.prod(shp))
            o = sum(v[1] for v in dbg_map.values())
            dbg_map[tag] = (o, n)
            dst = dbg_d[:, o:o + n]
            if len(shp) == 2:
                dst = dst.rearrange("p (a b) -> p a b", a=shp[0])
            elif len(shp) == 3:
                dst = dst.rearrange("p (a b c) -> p a b c", a=shp[0], b=shp[1])
            S.dma("pool", dbg_sem, dst, ap2d, reads=ar((ap2d,), bufs), nbytes=128 * n * 4)

        def mm(out, lhsT, rhs, start, stop, reads, writes, inc):
            n = fsz(rhs)
            c = (max(n, 48) * 0.5 + 14.0) * (4.0 if rhs.dtype == F32 else 1.0)
            S.emit("pe", lambda e: e.matmul(out, lhsT, rhs, start=start, stop=stop), ar((lhsT, rhs), reads), writes, inc, cost=c)

        def tr(out, in_, reads, writes, inc):
            S.emit("pe", lambda e: e.transpose(out, in_, identb[:, :]), ar((in_,), reads + [bf("identb")]), writes, inc, cost=80.0)

        def act(out, in_, func, reads, writes, bias=None, scale=None):
            kw = {}
            if bias is not None:
                kw["bias"] = bias
            if scale is not None:
                kw["scale"] = scale
            S.emit("act", lambda e: e.activation(out, in_, func, **kw), ar((out, in_), reads), writes, cost=210.0 + 0.9 * fsz(out))

        def tt(eng, out, in0, in1, op, reads, writes):
            S.emit(eng, lambda e: e.tensor_tensor(out, in0, in1, op), ar((out, in0, in1), reads), writes,
                   cost=(70.0 + 1.05 * fsz(out)) * (2.0 if eng == "pool" else 1.0))

        def ts(eng, out, in0, s1, s2, op0, op1, reads, writes):
            c = 70.0 + 1.05 * fsz(out)
            if op1 is None:
                S.emit(eng, lambda e: e.tensor_scalar(out, in0, s1, None, op0), ar((out, in0), reads), writes, cost=c)
            else:
                S.emit(eng, lambda e: e.tensor_scalar(out, in0, s1, s2, op0, op1), ar((out, in0), reads), writes, cost=c)

        def stt(eng, out, in0, sc, in1, op0, op1, reads, writes):
            S.emit(eng, lambda e: e.scalar_tensor_tensor(out, in0, sc, in1, op0, op1), ar((out, in0, in1), reads), writes,
                   cost=70.0 + 1.05 * fsz(out))

        def cp(eng, out, in_, reads, writes):
            S.emit(eng, lambda e: e.tensor_copy(out, in_), ar((out, in_), reads), writes,
                   cost=(70.0 + 1.05 * fsz(out)) * (1.6 if eng == "pool" else 1.0))

        order0 = [1, 0, 2, 3, 4, 5, 6, 7] + list(range(8, 24)) + [26, 24, 25]
        order1 = [27, 28, 29, 30, 31] + list(range(32, 48)) + [50, 48, 49]
        per_tile = order0 + (order1 if nlayers > 1 else [])
        wseq = per_tile * ntiles
        wstate = {"issued": 0, "used": 0}
        wlive = set()
        wheld = set()

        def w_pump():
            while wstate["issued"] < len(wseq):
                i = wstate["issued"]
                prev = i - NW
                if prev >= 0 and (prev >= wstate["used"] or prev in wlive):
                    break
                slot = i % NW
                S.dma("pool", ringsem[slot], ring[slot][:, :], ws_d[wseq[i]], reads=(), writes=[ringb[slot]], nbytes=128 * SLAB * 4)
                wstate["issued"] += 1

        def w_next(expect, hold=False):
            i = wstate["used"]
            assert wseq[i] == expect, (wseq[i], expect)
            for j in list(wlive):
                if j not in wheld:
                    wlive.discard(j)
            wstate["used"] += 1
            wlive.add(i)
            if hold:
                wheld.add(i)
            w_pump()
            assert wstate["issued"] > i
            slot = i % NW
            return ring[slot], ringb[slot]

        def w_release_held():
            for j in list(wheld):
                wheld.discard(j)
                wlive.discard(j)

        S.dma("sp", cst_sem, cst[:, :], cst_d[:, :], writes=[bf("cst")], nbytes=128 * NCST * 4)
        w_pump()
        cb = bf("cst")
        cp("dve", identb[:, :], cc("ident"), [cb], [bf("identb")])
        cp("dve", maskmb[:, :], cc("maskm"), [cb], [bf("maskmb")])
        for i_, nm_ in enumerate(("maskb", "maskbf")):
            cp("dve", maskbb[i_][:, :, :], cc(nm_).unsqueeze(1).broadcast_to([128, 2, 256]), [cb], [bf("maskbb")])
        S.emit("dve", lambda e: e.tensor_reduce(sinkmax[:, :], cc("sink").rearrange("p (k g) -> p k g", k=4), AX.X, ALU.max),
               [cb], [bf("sinkmax")])
        cp("dve", pmb[:, :], cc("pm"), [cb], [bf("pmb")])
        cp("dve", wgb[:, :, :], cc("wg").rearrange("p (k g) -> p k g", k=8), [cb], [bf("wgb")])
        S.emit("dve", lambda e: e.memset(onesb[:, :], 1.0 / 1024.0), [], [bf("onesb")])
        ts("dve", bq8[:, :], cc("bq"), 0.125, None, ALU.mult, None, [cb], [bf("bq8")])
        S.emit("dve", lambda e: e.memset(Cs[:, :, :], 0.0), [], [bf("Cs")])
        S.emit("dve", lambda e: e.memset(epsc[:, :], LN_EPS), [], [bf("epsc")])
        S.emit("dve", lambda e: e.memset(kTa[:, :, 0:128], 0.0), [], [bf("kTa0")])
        S.emit("dve", lambda e: e.memset(va[:, :, :, :], 0.0), [], [bf("va_all")])
        S.emit("dve", lambda e: e.memset(va[:, :, :, 64:65], 1.0), [bf("va_all")], [bf("va_all")])

        psrot = {"i": 0}

        def next_ps(lst):
            i = lst[psrot["i"] % len(lst)]
            psrot["i"] += 1
            return i

        ALLB = list(range(8))

        def ln_phase(l, which, groups, order):
            gcol = f"{which}g{l}"
            bcol = f"{which}b{l}"
            emit_group, bias_name = groups
            rot = [0, 1, 2, 3]
            pm_i = [4, 5]
            pq_i = [6, 7]
            pend = []
            cnt_h = [0, 0]

            def normalize(half):
                hs = slice(half * 512, (half + 1) * 512)
                rb = bf(f"lnrstd{half}")
                pmb_ = psb[pm_i[half]]
                act(lnrstd[:, hs], ps(pm_i[half]), AF.Square, [pmb_], [rb])
                tt("dve", lnrstd[:, hs], ps(pq_i[half]), lnrstd[:, hs], ALU.subtract, [psb[pq_i[half]], rb], [rb])
                act(lnrstd[:, hs], lnrstd[:, hs], AF.Sqrt, [rb, bf("epsc")], [rb], bias=epsc[:, 0:1])
                S.emit("dve", lambda e, o=lnrstd[:, hs]: e.reciprocal(o, o), [rb], [rb], cost=610.0)
                for oc in range(8):
                    xs = xT32[:, oc, hs]
                    xbuf = bf(f"x{oc}_{half}")
                    xbb = bf(f"xb{oc}_{half}")
                    tt("dve", xs, xs, ps(pm_i[half]), ALU.subtract, [xbuf, pmb_], [xbuf])
                    tt("dve", xs, xs, lnrstd[:, hs], ALU.mult, [xbuf, rb], [xbuf])
                    act(xb[:, oc, hs], xs, AF.Identity, [xbuf, cb], [xbb], bias=cc(bcol, oc, 1), scale=cc(gcol, oc, 1))
                for oc in range(8):
                    xs = xT32[:, oc, hs]
                    xbuf = bf(f"x{oc}_{half}")
                    act(xs, xs, AF.Identity, [xbuf, cb], [xbuf], bias=cc(bcol, oc, 1), scale=cc(gcol, oc, 1))

            def stats(half, slot):
                first = cnt_h[half] == 0
                last = cnt_h[half] == 7
                cnt_h[half] += 1
                mm(ps(pm_i[half]), onesb[:, :], zst[slot][:, 0, :], first, last,
                   [bf("onesb"), bf(f"zsta{slot}")], [psb[pm_i[half]]], False)
                mm(ps(pq_i[half]), onesb[:, :], zst[slot][:, 1, :], first, last,
                   [bf("onesb"), bf(f"zst{slot}")], [psb[pq_i[half]]], True)
                if last:
                    normalize(half)

            k = 0
            for oc, half in order:
                pi = next_ps(rot)
                emit_group(oc, half, pi)
                xs = xT32[:, oc, half * 512:(half + 1) * 512]
                xbuf = bf(f"x{oc}_{half}")
                if bias_name == "pre":
                    tt("dve", xs, xs, ps(pi), ALU.add, [xbuf, psb[pi]], [xbuf])
                elif bias_name is not None:
                    tmp = lntmp[k % 2]
                    tb = bf(f"lntmp{k % 2}")
                    act(tmp[:, :], ps(pi), AF.Identity, [psb[pi], cb], [tb], bias=cc(bias_name, oc, 1))
                    stt("dve", xs, xs, ALPHA, tmp[:, :], ALU.mult, ALU.add, [xbuf, tb], [xbuf])
                else:
                    stt("dve", xs, xs, ALPHA, ps(pi), ALU.mult, ALU.add, [xbuf, psb[pi]], [xbuf])
                slot = k % 4
                zb_ = bf(f"zst{slot}")
                cp("pool", zst[slot][:, 0, :], xs, [xbuf], [bf(f"zsta{slot}")])
                act(zst[slot][:, 1, :], xs, AF.Square, [xbuf], [zb_])
                pend.append((half, slot))
                if len(pend) > 1:
                    stats(*pend.pop(0))
                k += 1
            while pend:
                stats(*pend.pop(0))

        ORDER_HALF_MAJOR = [(oc, h) for h in range(NHALF) for oc in range(8)]
        ORDER_PAIRS = [(2 * p_ + i_, h) for p_ in range(4) for h in range(NHALF) for i_ in range(2)]

        def xb_reads(half=None):
            if half is None:
                return [bf(f"xb{oc}_{h}") for oc in range(8) for h in range(NHALF)]
            return [bf(f"xb{oc}_{half}") for oc in range(8)]

        def mlp_ple(l, it, base, last=False):
            t0 = it * TT
            S.dma("pool", p_sem, ptb[:, :, :], pT_d[l, :, t0:t0 + TT].rearrange("(k p) t -> p k t", p=128),
                  writes=[bf("ptb")], nbytes=256 * TT * 4)
            S.phase = f"l{l}.up"
            arena_gen()
            for sp_ in range(4):
                pair = []
                for i_ in range(2):
                    slot, sbuf_ = w_next(base + 8 + 2 * sp_ + i_, hold=True)
                    pair.append((2 * sp_ + i_, slot[:, :].rearrange("p (k n) -> p k n", k=8), sbuf_))
                for half in range(NHALF):
                    for s, sv, sbuf_ in pair:
                        for j in range(4):
                            hc = s * 4 + j
                            pi = next_ps(ALLB)
                            for kc in range(8):
                                mm(ps(pi), sv[:, kc, j * 128:(j + 1) * 128], xb[:, kc, half * 512:(half + 1) * 512],
                                   kc == 0, kc == 7, [sbuf_] + xb_reads(half), [psb[pi]], kc == 7)
                            hb = bf(f"hT{hc}_{half}")
                            ho = hT[:, hc, half * 512:(half + 1) * 512]
                            lt = lntmp[(hc * 2 + half) % 2]
                            ltb = bf(f"lntmp{(hc * 2 + half) % 2}")
                            act(lt[:, :], ps(pi), AF.Relu, [psb[pi]], [ltb])
                            tt("dve", ho, lt[:, :], lt[:, :], ALU.mult, [ltb], [hb])
                w_release_held()

            S.phase = f"l{l}.down_ln"
            dsl = {}

            def down_group(oc, half, pi):
                if oc not in dsl:
                    slot, sbuf_ = w_next(base + 16 + oc, hold=True)
                    dsl[oc] = (slot[:, :].rearrange("p (k n) -> p k n", k=32), sbuf_)
                sv, sbuf_ = dsl[oc]
                for hc in range(32):
                    mm(ps(pi), sv[:, hc, :], hT[:, hc, half * 512:(half + 1) * 512], hc == 0, hc == 31,
                       [sbuf_, bf(f"hT{hc}_{half}")], [psb[pi]], hc % 8 == 7)
                if oc % 2 == 1 and half == NHALF - 1:
                    w_release_held()

            ln_phase(l, "mlp", (down_group, None), ORDER_PAIRS)
            S.phase = f"l{l}.ple"
            if last:
                arena_gen()
            slot_p, sbuf_p = w_next(base + 26, hold=True)
            spv = slot_p[:, 0:2048].rearrange("p (k n) -> p k n", k=2)
            rotg = [0, 1, 2, 3]
            rotp = [4, 5, 6, 7]
            gsl = [w_next(base + 24 + i_, hold=True) for i_ in range(2)]
            k = 0
            for half in range(NHALF):
                hs = slice(half * 512, (half + 1) * 512)
                for oc in range(8):
                    gslot, gbuf = gsl[oc // 4]
                    gv = gslot[:, :].rearrange("p (k n) -> p k n", k=8)
                    j = oc % 4
                    pg = rotg[k % 4]
                    pp = rotp[k % 4]
                    for kc in range(8):
                        mm(ps(pg), gv[:, kc, j * 128:(j + 1) * 128], xb[:, kc, hs], kc == 0, kc == 7,
                           [gbuf] + xb_reads(half), [psb[pg]], kc == 7)
                    for kc in range(2):
                        mm(ps(pp), spv[:, kc, oc * 128:(oc + 1) * 128], ptb[:, kc, hs], kc == 0, kc == 1,
                           [sbuf_p, bf("ptb")], [psb[pp]], kc == 1)
                    lt = lntmp[k % 2]
                    ltb = bf(f"lntmp{k % 2}")
                    act(lt[:, :], ps(pg), AF.Sigmoid, [psb[pg], cb], [ltb], bias=cc(f"pleb{l}", oc, 1))
                    tt("dve", lt[:, :], lt[:, :], ps(pp), ALU.mult, [ltb, psb[pp]], [ltb])
                    xs = xT32[:, oc, hs]
                    xbuf = bf(f"x{oc}_{half}")
                    if last:
                        tt("pool", xo[:, oc, hs], xs, lt[:, :], ALU.add, [xbuf, ltb], [bf(f"xo{oc}")])
                    else:
                        tt("pool", xs, xs, lt[:, :], ALU.add, [xbuf, ltb], [xbuf])
                    k += 1
            w_release_held()
            if last:
                return
            for oc in range(8):
                for half in range(NHALF):
                    hs = slice(half * 512, (half + 1) * 512)
                    act(xb[:, oc, hs], xT32[:, oc, hs], AF.Copy, [bf(f"x{oc}_{half}")], [bf(f"xb{oc}_{half}")])

        def layer0(it):
            t0 = it * TT
            S.phase = "l0.load"
            arena_gen()
            for half in range(NHALF):
                for oc in range(8):
                    S.dma("sp", x_sems[oc * 2 + half], xT32[:, oc, half * 512:(half + 1) * 512],
                          xT_d[oc * 128:(oc + 1) * 128, t0 + half * 512:t0 + (half + 1) * 512], nbytes=128 * 512 * 4, xlat=15000.0,
                          writes=[bf(f"x{oc}_{half}")])
            for half in range(NHALF):
                for oc in range(8):
                    hs = slice(half * 512, (half + 1) * 512)
                    if oc % 2:
                        cp("dve", xb[:, oc, hs], xT32[:, oc, hs], [bf(f"x{oc}_{half}")], [bf(f"xb{oc}_{half}")])
                    else:
                        act(xb[:, oc, hs], xT32[:, oc, hs], AF.Copy, [bf(f"x{oc}_{half}")], [bf(f"xb{oc}_{half}")])
            S.emit("dve", lambda e: e.memset(m_va[:, :, :, 256:257], 1.0), [AT], [bf("mva_ones")], cost=100.0)
            S.phase = "l0.gates"
            for hf in range(NHALF):
                b4 = slice(hf * 4, hf * 4 + 4)
                pg = next_ps(ALLB)
                for bb in range(4):
                    blk = hf * 4 + bb
                    for kc in range(8):
                        mm(ps(pg)[:, bb * 8:(bb + 1) * 8], xb[:, kc, blk * 128:(blk + 1) * 128], wgb[:, kc, :],
                           kc == 0, kc == 7, [bf("wgb")] + xb_reads(hf), [psb[pg]], kc == 7 and bb == 3)
                gzb, gthb = bf(f"gz{hf}"), bf(f"gth{hf}")
                tt("dve", gz[:, b4, :], ps(pg)[:, 0:32].rearrange("p (b g) -> p b g", g=8),
                   cc("gb", 0, 32).rearrange("p (b g) -> p b g", g=8), ALU.add, [psb[pg], cb], [gzb])
                act(gth[:, b4, :], gz[:, b4, :], AF.Tanh, [gzb], [gthb], scale=1.0 / 15.0)
                act(gef[:, b4, :], gth[:, b4, 4:8], AF.Exp, [gthb], [bf(f"gef{hf}")], scale=-15.0)
                act(gsp[:, b4, :], gef[:, b4, :], AF.Ln, [bf(f"gef{hf}")], [bf(f"gsp{hf}")], bias=1.0)
                pa = next_ps(ALLB)
                gspf = gsp[:, b4, :].rearrange("p b g -> p (b g)")
                mm(ps(pa)[:, 0:16], cc("ustrict"), gspf, True, True, [cb, bf(f"gsp{hf}")], [psb[pa]], False)
                mm(ps(pa)[:, 16:32], cc("ones"), gspf, True, True, [cb, bf(f"gsp{hf}")], [psb[pa]], True)
                stt("dve", gei[:, b4, :], ps(pa)[:, 0:16].rearrange("p (b g) -> p b g", g=4), -1.0 / 15.0, gth[:, b4, 0:4],
                    ALU.mult, ALU.add, [gthb, psb[pa]], [bf(f"gei{hf}")])
                act(ges[:, b4, :], gei[:, b4, :], AF.Exp, [bf(f"gei{hf}")], [bf(f"ges{hf}")], scale=15.0)
                act(gfr[:, :, b4, :], ps(pa)[:, 0:32].rearrange("p (a b g) -> p a b g", a=2, g=4), AF.Exp,
                    [psb[pa]], [bf(f"gfr{hf}")], scale=-1.0)
            S.phase = "l0.proj"
            slot, sbuf_ = w_next(1)
            sv = slot[:, :].rearrange("p (k n) -> p k n", k=8)
            for h in range(4):
                for half in range(NHALF):
                    hs = slice(half * 512, (half + 1) * 512)
                    pi = next_ps(ALLB)
                    for kc in range(8):
                        mm(ps(pi), sv[:, kc, h * 128:(h + 1) * 128], xb[:, kc, hs], kc == 0, kc == 7,
                           [sbuf_] + xb_reads(half), [psb[pi]], kc == 7)
                    act(m_kT[:, h, hs], ps(pi), AF.Copy, [psb[pi]], [bf(f"mkT{h}_{half}")])
            for blk in range(NBLK):
                bs = slice(blk * 128, (blk + 1) * 128)
                pi = next_ps(ALLB)
                for kc in range(8):
                    mm(ps(pi), xb[:, kc, bs], sv[:, kc, :], kc == 0, kc == 7,
                       [sbuf_] + xb_reads(blk // 4), [psb[pi]], kc == 7)
                tt("dve", m_kw[:, blk, :].rearrange("p (h n) -> p h n", h=4),
                   ps(pi).rearrange("p (h n) -> p h n", h=4),
                   ges[:, blk, :].unsqueeze(2).broadcast_to([128, 4, 128]), ALU.mult,
                   [psb[pi], bf(f"ges{blk // 4}")], [bf(f"mkw{blk}")])
            slot, sbuf_ = w_next(0)
            sv = slot[:, :].rearrange("p (k n) -> p k n", k=8)
            for h in range(4):
                for half in range(NHALF):
                    hs = slice(half * 512, (half + 1) * 512)
                    pi = next_ps(ALLB)
                    for kc in range(8):
                        mm(ps(pi), sv[:, kc, h * 128:(h + 1) * 128], xb[:, kc, hs], kc == 0, kc == 7,
                           [sbuf_] + xb_reads(half), [psb[pi]], kc == 7)
                    act(m_qT[:, h, hs], ps(pi), AF.Copy, [psb[pi]], [bf(f"mqT{h}_{half}")], scale=128.0 ** -0.5)
            for s in range(2):
                slot, sbuf_ = w_next(2 + s)
                sv = slot[:, :].rearrange("p (k n) -> p k n", k=8)
                for blk in range(NBLK):
                    bs = slice(blk * 128, (blk + 1) * 128)
                    pi = next_ps(ALLB)
                    for kc in range(8):
                        mm(ps(pi), xb[:, kc, bs], sv[:, kc, :], kc == 0, kc == 7,
                           [sbuf_] + xb_reads(blk // 4), [psb[pi]], kc == 7)
                    S.emit("dve" if blk % 2 else "act",
                           (lambda e, o=m_va[:, blk, 2 * s:2 * s + 2, 0:256], i=ps(pi).rearrange("p (h n) -> p h n", h=2):
                            e.tensor_copy(o, i)) if blk % 2 else
                           (lambda e, o=m_va[:, blk, 2 * s:2 * s + 2, 0:256], i=ps(pi).rearrange("p (h n) -> p h n", h=2):
                            e.activation(o, i, AF.Copy)),
                           [psb[pi], AT], [bf(f"mva{blk}_{s}")], cost=680.0)
            for s in range(2):
                slot, sbuf_ = w_next(4 + s)
                sv = slot[:, :].rearrange("p (k n) -> p k n", k=8)
                for blk in range(NBLK):
                    bs = slice(blk * 128, (blk + 1) * 128)
                    pi = next_ps(ALLB)
                    for kc in range(8):
                        mm(ps(pi), xb[:, kc, bs], sv[:, kc, :], kc == 0, kc == 7,
                           [sbuf_] + xb_reads(blk // 4), [psb[pi]], kc == 7)
                    act(m_sgo[:, blk, s * 512:(s + 1) * 512], ps(pi), AF.Sigmoid, [psb[pi]], [bf(f"msgo{blk}_{s}")])
            S.phase = "l0.recur"
            rot1 = [5, 6]
            for blk in range(NBLK):
                bs = slice(blk * 128, (blk + 1) * 128)
                half = blk // 4
                q = blk % 2
                pn = [ps_all[:, 2 * q + h // 2, (h % 2) * 256:(h % 2) * 256 + 256] for h in range(4)]
                pnb = [psb[2 * q + h // 2] for h in range(4)]
                pd = ps_all[:, 4, q * 4:q * 4 + 4]
                for h in range(4):
                    k = blk * 4 + h
                    pS = next_ps(rot1)
                    mm(ps(pS)[:, 0:128], m_kT[:, h, bs], m_qT[:, h, bs], True, True,
                       [bf(f"mkT{h}_{half}"), bf(f"mqT{h}_{half}")], [psb[pS]], False)
                    mm(ps(pS)[:, 128:385], m_kw[:, blk, h * 128:(h + 1) * 128], m_va[:, blk, h, 0:257], True, True,
                       [bf(f"mkw{blk}"), bf(f"mva{blk}_{h // 2}"), bf("mva_ones")], [psb[pS]], True)
                    ptm = PTm[k % 2]
                    ptb_ = bf(f"PTm{k % 2}")
                    stt("dve", ptm[:, :], ps(pS)[:, 0:128], ges[:, blk, h:h + 1], maskmb[:, :], ALU.mult, ALU.mult,
                        [psb[pS], bf(f"ges{half}"), bf("maskmb")], [ptb_])
                    cbt = Cb[k % 2]
                    cbb = bf(f"Cb{k % 2}")
                    act(cbt[:, :], Cs[:, h, :], AF.Copy, [bf(f"Cs{h}"), bf("Cs"), bf(f"gfr{half}")], [cbb],
                        scale=gfr[:, 1, blk, h:h + 1])
                    vb_ = [bf(f"mva{blk}_{h // 2}"), bf("mva_ones")]
                    mm(pn[h], ptm[:, :], m_va[:, blk, h, 0:256], True, False, [ptb_] + vb_, [pnb[h]], False)
                    mm(pn[h], m_qT[:, h, bs], cbt[:, 0:256], False, True, [bf(f"mqT{h}_{half}"), cbb], [pnb[h]], False)
                    mm(pd[:, h:h + 1], ptm[:, :], m_va[:, blk, h, 256:257], True, False, [ptb_] + vb_, [psb[4]], False)
                    mm(pd[:, h:h + 1], m_qT[:, h, bs], cbt[:, 256:257], False, True, [bf(f"mqT{h}_{half}"), cbb], [psb[4]], True)
                    stt("dve", Cs[:, h, :], Cs[:, h, :], gfr[:, 1, blk, h:h + 1], ps(pS)[:, 128:385], ALU.mult, ALU.add,
                        [bf(f"Cs{h}"), bf("Cs"), bf(f"gfr{half}"), psb[pS]], [bf(f"Cs{h}")])
                sm = nsm[q]
                smb = bf(f"nsm{q}")
                act(sm[:, 0, :], pd, AF.Abs, [psb[4]], [smb])
                tt("dve", sm[:, 0, :], sm[:, 0, :], gfr[:, 0, blk, :], ALU.max, [smb, bf(f"gfr{half}")], [smb])
                S.emit("dve", lambda e, o=sm[:, 1, :], i=sm[:, 0, :]: e.reciprocal(o, i), [smb], [smb])
                for h in range(4):
                    S.emit("dve", lambda e, o=nst[q][:, h, :], i=pn[h]: e.bn_stats(o, i), [pnb[h]], [bf(f"nst{q}")], cost=340.0)
                for h in range(4):
                    S.emit("dve", lambda e, o=nmv[q][:, h, :], i=nst[q][:, h, :]: e.bn_aggr(o, i), [bf(f"nst{q}")], [bf(f"nmv{q}")])
                tt("dve", sm[:, 2, :], sm[:, 1, :], sm[:, 1, :], ALU.mult, [smb], [smb])
                tt("dve", sm[:, 2, :].unsqueeze(2), sm[:, 2, :].unsqueeze(2), nmv[q][:, :, 1:2], ALU.mult,
                   [smb, bf(f"nmv{q}")], [smb])
                act(sm[:, 2, :], sm[:, 2, :], AF.Sqrt, [smb, bf("epsc")], [smb], bias=epsc[:, 0:1])
                S.emit("dve", lambda e, o=sm[:, 2, :]: e.reciprocal(o, o), [smb], [smb])
                tt("dve", sm[:, 3, :], sm[:, 2, :], sm[:, 1, :], ALU.mult, [smb], [smb])
                stt("dve", sm[:, 4, :].unsqueeze(2), nmv[q][:, :, 0:1], -1.0, sm[:, 3, :].unsqueeze(2), ALU.mult, ALU.mult,
                    [smb, bf(f"nmv{q}")], [smb])
                hnb = bf(f"hn{q}")
                for h in range(4):
                    act(hn[q][:, h, :], pn[h], AF.Identity, [pnb[h], smb], [hnb],
                        bias=sm[:, 4, h:h + 1], scale=sm[:, 3, h:h + 1])
                yb = bf(f"ytok{q}")
                tt("pool", ytok[q][:, :], hn[q][:, :, :].rearrange("p h n -> p (h n)"), m_sgo[:, blk, :], ALU.mult,
                   [hnb, bf(f"msgo{blk}_0"), bf(f"msgo{blk}_1")], [yb])
                pT_ = 7
                ptv = ps(pT_).bitcast(BF16).rearrange("p (c t) -> p c t", c=8)
                for c in range(8):
                    tr(ptv[:, c, :], ytok[q][:, c * 128:(c + 1) * 128], [yb], [psb[pT_]], c == 7)
                act(yT[:, :, bs], ptv, AF.Copy, [psb[pT_]], xb_reads(blk // 4))
            S.phase = "l0.outproj_ln"
            wst = {}

            def out_group(oc, half, pi):
                si = oc // 4
                if si not in wst:
                    slot, sbuf_ = w_next(6 + si, hold=True)
                    sv = slot[:, :].rearrange("p (k n) -> p k n", k=8)
                    tt("pool", sv, sv, cc("hg").unsqueeze(2).broadcast_to([128, 8, 512]), ALU.mult, [sbuf_, cb], [sbuf_])
                    wst[si] = (sv, sbuf_)
                sv, sbuf_ = wst[si]
                j = oc % 4
                for kc in range(8):
                    mm(ps(pi), sv[:, kc, j * 128:(j + 1) * 128], yT[:, kc, half * 512:(half + 1) * 512], kc == 0, kc == 7,
                       [sbuf_] + xb_reads(half), [psb[pi]], kc == 7)

            ln_phase(0, "mix", (out_group, None), ORDER_HALF_MAJOR)
            w_release_held()
            mlp_ple(0, it, 0, last=(nlayers == 1))

        def rope_evac(pi, dst, dstbuf, biasap, sc, half, k):
            hs = slice(half * 512, (half + 1) * 512)
            q32 = a_q32[k % 2]
            qb_ = a_qb[k % 2]
            ra = a_ra[k % 2]
            rb = a_rb[k % 2]
            b32, bqb, bra, brb = bf(f"aq32{k % 2}"), bf(f"aqb{k % 2}"), bf(f"ara{k % 2}"), bf(f"arb{k % 2}")
            act(q32, ps(pi), AF.Identity, [psb[pi], cb, bf("bq8")], [b32], bias=biasap, scale=sc)
            act(qb_, ps(pi), AF.Identity, [psb[pi], cb, bf("bq8")], [bqb], bias=biasap, scale=sc)
            psw = next_ps([4, 5, 6, 7])
            mm(ps(psw), pmb[:, :], qb_, True, True, [bf("pmb"), bqb], [psb[psw]], True)
            tt("pool", ra, q32, a_cos[:, hs], ALU.mult, [b32, bf("acos")], [bra])
            tt("dve", rb, ps(psw), a_sin[:, hs], ALU.mult, [psb[psw], bf("asin")], [brb])
            tt("dve", dst, ra, rb, ALU.add, [bra, brb], [dstbuf])

        def layer1(it):
            t0 = it * TT
            gb0 = it * NBLK
            S.phase = "l1.rope"
            arena_gen()
            for half in range(NHALF):
                for oc in range(8):
                    hs = slice(half * 512, (half + 1) * 512)
                    act(xT32[:, oc, hs], xT32[:, oc, hs], AF.Identity, [bf(f"x{oc}_{half}"), cb], [bf(f"x{oc}_{half}")],
                        bias=cc("bo", oc, 1), scale=ALPHA)
            S.dma("sp", pos_sem, posi[:, :, :], pos_d[0:1, t0:t0 + TT].partition_broadcast(128), writes=[bf("posi")], nbytes=128 * TT * 4)
            cp("dve", a_posf, posi[:, 0, :], [bf("posi")], [bf("aposf")])
            C1 = 6.28125
            C2 = 2.0 * PI - C1
            ab, ub = bf("aang"), bf("au")
            ts("dve", a_ang, a_posf, cc("invf"), None, ALU.mult, None, [bf("aposf"), cb], [ab])
            ts("dve", a_u, a_ang, 1.0 / (2.0 * PI), None, ALU.mult, None, [ab], [ub])
            cp("dve", a_ki, a_u, [ub], [bf("aki")])
            cp("dve", a_u, a_ki, [bf("aki")], [ub])
            stt("dve", a_ang, a_u, -C1, a_ang, ALU.mult, ALU.add, [ub, ab], [ab])
            stt("dve", a_ang, a_u, -C2, a_ang, ALU.mult, ALU.add, [ub, ab], [ab])
            ts("dve", a_u, a_ang, PI, 2.0 * PI, ALU.is_gt, ALU.mult, [ab], [ub])
            tt("dve", a_ang, a_ang, a_u, ALU.subtract, [ab, ub], [ab])
            act(a_sin, a_ang, AF.Sin, [ab], [bf("asin")])
            ts("dve", a_ang, a_ang, 0.5 * PI, None, ALU.add, None, [ab], [ab])
            ts("dve", a_u, a_ang, PI, 2.0 * PI, ALU.is_gt, ALU.mult, [ab], [ub])
            tt("dve", a_ang, a_ang, a_u, ALU.subtract, [ab, ub], [ab])
            act(a_cos, a_ang, AF.Sin, [ab], [bf("acos")])
            ts("dve", a_sin, a_sin, cc("sgn"), None, ALU.mult, None, [bf("asin"), cb], [bf("asin")])
            S.phase = "l1.proj"
            slot, sbuf_ = w_next(27)
            sv = slot[:, :].rearrange("p (k n) -> p k n", k=8)
            kk = 0
            for c in range(2):
                for half in range(NHALF):
                    hs = slice(half * 512, (half + 1) * 512)
                    pi = next_ps([0, 1, 2, 3])
                    for kc in range(8):
                        mm(ps(pi), sv[:, kc, c * 128:(c + 1) * 128], xb[:, kc, hs], kc == 0, kc == 7,
                           [sbuf_] + xb_reads(half), [psb[pi]], kc == 7)
                    rope_evac(pi, kTa[:, c, 128 + half * 512:128 + (half + 1) * 512], bf(f"kTa{c}_{half}"),
                              cc("bk", c, 1), 1.0, half, kk)
                    kk += 1
            for blk in range(NBLK):
                bs = slice(blk * 128, (blk + 1) * 128)
                pi = next_ps([0, 1, 2, 3])
                for kc in range(8):
                    mm(ps(pi)[:, 0:256], xb[:, kc, bs], sv[:, kc, 256:512], kc == 0, kc == 7,
                       [sbuf_] + xb_reads(blk // 4), [psb[pi]], kc == 7)
                tt("dve", va[:, blk + 1, :, 0:64], ps(pi)[:, 0:256].rearrange("p (h d) -> p h d", h=4),
                   cc("vb").rearrange("p (h d) -> p h d", h=4), ALU.add, [psb[pi], cb, bf("va_all")], [bf(f"va{blk + 1}")])
            for s in range(2):
                slot, sbuf_ = w_next(28 + s)
                sv = slot[:, :].rearrange("p (k n) -> p k n", k=8)
                for jj in range(4):
                    j = s * 4 + jj
                    for half in range(NHALF):
                        hs = slice(half * 512, (half + 1) * 512)
                        pi = next_ps([0, 1, 2, 3])
                        for kc in range(8):
                            mm(ps(pi), sv[:, kc, jj * 128:(jj + 1) * 128], xb[:, kc, hs], kc == 0, kc == 7,
                               [sbuf_] + xb_reads(half), [psb[pi]], kc == 7)
                        rope_evac(pi, a_qT[:, j, hs], bf(f"aqT{j}_{half}"), bq8[:, j:j + 1], 0.125, half, kk)
                        kk += 1
            dump("cos", a_cos, [bf("acos")])
            dump("sin", a_sin, [bf("asin")])
            dump("qT", a_qT[:, :, 0:256], [bf(f"aqT{j}_0") for j in range(8)])
            dump("kT", kTa[:, :, 0:384], [bf(f"kTa{c}_0") for c in range(2)] + [bf("kTa0")])
            dump("va", va[:, 0:3, :, :], [bf("va1"), bf("va2"), bf("va_all")])
            dump("xin", xT32[:, :, 0:128], [bf(f"x{oc}_0") for oc in range(8)])
            S.phase = "l1.attn"
            rot1 = [6, 7]
            for blk in range(NBLK):
                bs = slice(blk * 128, (blk + 1) * 128)
                half = blk // 4
                first = (gb0 + blk == 0)
                q2 = blk % 2
                atok = ytok[q2]
                atb = bf(f"ytok{q2}")
                for kv in range(4):
                    k = blk * 4 + kv
                    off = (kv % 2) * 64
                    kch = kv // 2
                    sbank = (k % 3) * 2
                    pSv = ps_all[:, sbank:sbank + 2, :].rearrange("p a (g s) -> p (a g) s", g=2)
                    kreads = [bf(f"kTa{kch}_{h_}") for h_ in range(NHALF)] + [bf("kTa0"), bf("kTaprev")]
                    mbias = maskbb[1 if first else 0]
                    for pb in range(2):
                        mm(ps(sbank + pb), identb[:, :], mbias[:, :, :].rearrange("p a s -> p (a s)"), True, False,
                           [bf("identb"), bf("maskbb")], [psb[sbank + pb]], False)
                    for g in range(4):
                        j = kch * 4 + g
                        mm(pSv[:, g, :], a_qT[off:off + 64, j, bs], kTa[off:off + 64, kch, blk * 128:blk * 128 + 256],
                           False, g % 2 == 1, [bf(f"aqT{j}_{half}")] + kreads, [psb[sbank + g // 2]], g == 3)
                    sm = asm[k % 3]
                    smb = bf(f"asm{k % 3}")
                    sb2 = [psb[sbank], psb[sbank + 1]]
                    S.emit("dve", lambda e, o=sm[:, 0, 0:1], i=ps_all[:, sbank:sbank + 2, :]: e.tensor_reduce(o, i, AX.XY, ALU.max),
                           sb2, [smb], cost=1150.0)
                    ts("dve", sm[:, 1, 0:1], sm[:, 0, 0:1], sinkmax[:, kv:kv + 1], -1.0, ALU.max, ALU.mult, [smb, bf("sinkmax")], [smb])
                    pe_ = Pexp[k % 3]
                    peb = bf(PEXP_B[k % 3])
                    act(pe_, pSv, AF.Exp, sb2 + [smb], [peb], bias=sm[:, 1, 0:1])
                    pT_ = next_ps(rot1)
                    ptv = ps(pT_).bitcast(BF16).rearrange("p (c t) -> p c t", c=8)
                    for g in range(4):
                        for jj in range(2):
                            tr(ptv[:, g * 2 + jj, :], pe_[:, g, jj * 128:(jj + 1) * 128], [peb], [psb[pT_]],
                               g == 3 and jj == 1)
                    pta = PTa[k % 3]
                    ptab = bf(PTA_B[k % 3])
                    if k % 4 == 3:
                        cp("dve", pta, ptv, [psb[pT_]], [ptab])
                    else:
                        act(pta, ptv, AF.Copy, [psb[pT_]], [ptab])
                    pO = next_ps(rot1)
                    pOv = ps(pO).rearrange("p (g n) -> p g n", g=4)
                    for g in range(4):
                        if not first:
                            mm(pOv[:, g, 0:65], pta[:, g * 2, :], va[:, blk, kv, :], True, False,
                               [ptab, bf(f"va{blk}"), bf("va_all")], [psb[pO]], False)
                        mm(pOv[:, g, 0:65], pta[:, g * 2 + 1, :], va[:, blk + 1, kv, :], first, True,
                           [ptab, bf(f"va{blk + 1}"), bf("va_all")], [psb[pO]], g == 3)
                    act(sm[:, 3, :], cc("sink", kv * 4, 4), AF.Exp, [smb, cb], [smb], bias=sm[:, 1, 0:1])
                    tt("dve", sm[:, 4, :].unsqueeze(2), pOv[:, :, 64:65], sm[:, 3, :].unsqueeze(2), ALU.add,
                       [psb[pO], smb], [smb])
                    S.emit("dve", lambda e, o=sm[:, 5, :], i=sm[:, 4, :]: e.reciprocal(o, i), [smb], [smb])
                    tt("dve", atok[:, kv * 256:(kv + 1) * 256].rearrange("p (g d) -> p g d", g=4), pOv[:, :, 0:64],
                       sm[:, 5, :].unsqueeze(2).broadcast_to([128, 4, 64]), ALU.mult, [psb[pO], smb], [atb])
                dump(f"atok{blk}", atok[:, :], [atb])
                pT_ = next_ps(rot1)
                ptv = ps(pT_).bitcast(BF16).rearrange("p (c t) -> p c t", c=8)
                for c in range(8):
                    tr(ptv[:, c, :], atok[:, c * 128:(c + 1) * 128], [atb], [psb[pT_]], c == 7)
                act(yT[:, :, bs], ptv, AF.Copy, [psb[pT_]], xb_reads(blk // 4))
            cp("dve", kTa[:, :, 0:128], kTa[:, :, TT:TT + 128],
               [bf(f"kTa{c}_{NHALF - 1}") for c in range(2)] + [bf("kTa0")], [bf("kTaprev"), bf("kTa0")])
            cp("dve", va[:, 0, :, :], va[:, NBLK, :, :], [bf(f"va{NBLK}"), bf("va_all")], [bf("va0")])
            S.phase = "l1.oproj_ln"
            wst = {}

            def o_group(oc, half, pi):
                si = oc // 4
                if si not in wst:
                    slot, sbuf_ = w_next(30 + si, hold=True)
                    wst[si] = (slot[:, :].rearrange("p (k n) -> p k n", k=8), sbuf_)
                sv, sbuf_ = wst[si]
                j = oc % 4
                for kc in range(8):
                    mm(ps(pi), sv[:, kc, j * 128:(j + 1) * 128], yT[:, kc, half * 512:(half + 1) * 512], kc == 0, kc == 7,
                       [sbuf_] + xb_reads(half), [psb[pi]], kc == 7)

            ln_phase(1, "mix", (o_group, "pre"), ORDER_HALF_MAJOR)
            w_release_held()
            mlp_ple(1, it, 24, last=True)

        for it in range(ntiles):
            layer0(it)
            if nlayers > 1:
                layer1(it)
            t0 = it * TT
            for oc in range(8):
                S.dma("sp", out_sems[oc], out_d[oc * 128:(oc + 1) * 128, t0:t0 + TT], xo[:, oc, :], nbytes=128 * TT * 4,
                      reads=[bf(f"xo{oc}"), AT], writes=[bf(f"out{it}_{oc}")])
        S.final_waits = list(out_sems) + ([dbg_sem] if dbg else [])
        S.schedule()
        nc.sched = S
        nc.dbg_map = dbg_map
        with nc.Block() as block:
            @block.tensor
            def _(e):
                S.run("pe", e)

            @block.scalar
            def _(e):
                S.run("act", e)

            @block.vector
            def _(e):
                S.run("dve", e)

            @block.gpsimd
            def _(e):
                S.run("pool", e)

            @block.sync
            def _(e):
                S.run("sp", e)
    return nc


def _slab_std(W):
    K, N = W.shape
    a = W.reshape(K // 128, 128, N // 512, 512).transpose(2, 1, 0, 3)
    a = a.reshape(N // 512, 128, (K // 128) * 512)
    if a.shape[2] < SLAB:
        a = np.concatenate([a, np.zeros((a.shape[0], 128, SLAB - a.shape[2]), np.float32)], axis=2)
    return a


def _slab_down(W):
    return W.reshape(32, 128, 8, 128).transpose(2, 1, 0, 3).reshape(8, 128, SLAB)


def _slab_proj(W):
    a = W.reshape(2, 128, 1024).transpose(1, 0, 2).reshape(1, 128, 2048)
    return np.concatenate([a, np.zeros((1, 128, SLAB - 2048), np.float32)], axis=2)


def _pk(v):
    return np.ascontiguousarray(v.reshape(-1, 128).T)


_QPERM = None


def _qperm():
    cols = []
    for j in range(8):
        kp, g = j // 4, j % 4
        for kv in (kp * 2, kp * 2 + 1):
            h = kv * 4 + g
            cols.extend(range(h * 64, h * 64 + 64))
    return np.array(cols)


def prep_shared(inp):
    f = np.float32
    qp = _qperm()
    slabs = []
    w_in = inp["a_w_in"][0]
    slabs.append(_slab_std(w_in[:, 0:3072]))
    slabs.append(_slab_std(inp["a_w_out"][0]))
    slabs.append(_slab_std(inp["mlp_w_up"][0]))
    slabs.append(_slab_down(inp["mlp_w_down"][0]))
    slabs.append(_slab_std(inp["ple_w_gate"][0]))
    slabs.append(_slab_proj(inp["ple_w_proj"][0]))
    slabs.append(_slab_std(inp["kv_w"]))
    slabs.append(_slab_std(inp["b_w_q"][0][:, qp]))
    slabs.append(_slab_std(inp["b_w_o"][0]))
    slabs.append(_slab_std(inp["mlp_w_up"][1]))
    slabs.append(_slab_down(inp["mlp_w_down"][1]))
    slabs.append(_slab_std(inp["ple_w_gate"][1]))
    slabs.append(_slab_proj(inp["ple_w_proj"][1]))
    ws = np.ascontiguousarray(np.concatenate(slabs, axis=0).astype(f))
    assert ws.shape == (NS, 128, SLAB), ws.shape

    cst = np.zeros((128, NCST), f)

    def put(name, arr):
        o, n = _c[name]
        cst[:, o:o + n] = arr

    for l in range(2):
        put(f"mixg{l}", _pk(inp["mix_ln_g"][l]))
        put(f"mixb{l}", _pk(inp["mix_ln_b"][l]))
        put(f"mlpg{l}", _pk(inp["mlp_ln_g"][l]))
        put(f"mlpb{l}", _pk(inp["mlp_ln_b"][l]))
        put(f"pleb{l}", _pk(inp["ple_b_gate"][l]))
    put("bo", _pk(inp["b_b_o"][0]))
    put("hg", _pk(inp["a_head_norm_g"][0]))
    put("bq", _pk(inp["b_b_q"][0][qp]))
    put("bk", _pk(inp["kv_b"][0:256]))
    p = np.arange(128)
    d = p % 64
    half = 8
    invf_tab = np.power(np.float32(500000.0), -np.arange(half, dtype=f) * f(2.0 / 16)).astype(f)
    invf = np.where(d < 16, invf_tab[d % 8], 0.0).astype(f)
    sgn = np.where(d < 8, -1.0, np.where(d < 16, 1.0, 0.0)).astype(f)
    put("invf", invf[:, None])
    put("sgn", sgn[:, None])
    gbias = np.concatenate([inp["a_b_igate"][0], inp["a_b_fgate"][0]]).astype(f)
    put("gb", np.broadcast_to(np.tile(gbias, 8)[None, :], (128, 64)))
    put("vb", np.broadcast_to(inp["kv_b"][256:512][None, :], (128, 256)))
    put("sink", np.broadcast_to(inp["b_sinks"][0][None, :], (128, 16)))
    s = np.arange(128)[:, None]
    t = np.arange(128)[None, :]
    put("ustrict", (s > t).astype(f))
    put("ones", np.ones((128, 128), f))
    put("ident", np.eye(128, dtype=f))
    put("maskm", (s <= t).astype(f))
    NEG = f(-30000.0)
    tq = np.arange(128)[:, None]
    sk = np.arange(128)[None, :]
    prev_ok = sk > tq
    cur_ok = sk <= tq
    put("maskb", np.concatenate([np.where(prev_ok, f(0), NEG), np.where(cur_ok, f(0), NEG)], axis=1).astype(f))
    put("maskbf", np.concatenate([np.full((128, 128), NEG, f), np.where(cur_ok, f(0), NEG)], axis=1).astype(f))
    pm = np.zeros((128, 128), f)
    for m in range(128):
        dm = m % 64
        if dm < 8:
            pm[m + 8, m] = 1.0
        elif dm < 16:
            pm[m - 8, m] = 1.0
    put("pm", pm)
    wg = w_in[:, 3072:3080].reshape(8, 128, 8).transpose(1, 0, 2).reshape(128, 64)
    put("wg", wg)
    return ws, cst


def kernel(**inputs):
    inp = {k: np.asarray(v) for k, v in inputs.items()}
    x = inp["x"]
    Bn, SEQL, _ = x.shape
    ntiles = SEQL // TT
    ws, cst = prep_shared(inp)
    nc = build(ntiles=ntiles, nlayers=2)
    in_maps = []
    for b in range(Bn):
        in_maps.append({
            "xT": np.ascontiguousarray(x[b].T),
            "pT": np.ascontiguousarray(inp["p"][:, b].transpose(0, 2, 1)),
            "pos": np.ascontiguousarray(inp["positions"][b][None, :].astype(np.int32)),
            "wslab": ws,
            "cst": cst,
        })
    res = run_bass_kernel_spmd(nc, in_maps, core_ids=list(range(Bn)))
    out = np.stack([np.ascontiguousarray(r["outT"].T) for r in res.results], axis=0)
    return out.astype(np.float32)
```
